# Optimizing a Trainium2 kernel written in Bass

```python
import jax, jax.numpy as jnp
from jax import lax
import numpy as np

D_MODEL = 1024
BATCH = 8
SEQ = 4096
DEPTH = 2

HEAD_DIM = 64
N_MIX_HEADS = 12
N_MEM_HEADS = 4
MEM_LEN = 256
DILATED_GROUPS = ((128, 1), (512, 4), (2048, 16))
HEADS_PER_GROUP = N_MIX_HEADS // len(DILATED_GROUPS)
ROT_DIM = HEAD_DIM // 4
ROT_HALF = ROT_DIM // 2
ROPE_THETA = 500000.0
D_FF = 2816
BLOCK = 128
N_MIXERS = 2
N_A_LAYERS = (DEPTH + 1) // 2
N_B_LAYERS = DEPTH // 2
DEEPNORM_ALPHA = (2 * DEPTH) ** 0.25
DEEPNORM_BETA = (8 * DEPTH) ** -0.25
LN_EPS = 1e-5
MIX_W = N_MIX_HEADS * HEAD_DIM
MEM_W = N_MEM_HEADS * HEAD_DIM
A_IN_W = 3 * MIX_W + MEM_W
B_IN_W = 3 * MIX_W + N_MIX_HEADS + MEM_W
A_OUT_IN = HEADS_PER_GROUP * HEAD_DIM + MEM_W
B_OUT_IN = MIX_W + MEM_W
ATTN_SCALE = HEAD_DIM ** -0.5

kernel_name = "hybrid_dilated_fox_macaron_deepnorm"


def layer_norm(x, g, b):
    xf = x.astype(jnp.float32)
    mu = jnp.mean(xf, axis=-1, keepdims=True)
    var = jnp.mean(jnp.square(xf - mu), axis=-1, keepdims=True)
    y = (xf - mu) * lax.rsqrt(var + LN_EPS) * g.astype(jnp.float32) + b.astype(jnp.float32)
    return y.astype(x.dtype)


def swiglu(x, w_gate_up, w_down):
    gate, up = jnp.split(x @ w_gate_up, 2, axis=-1)
    return (jax.nn.silu(gate) * up) @ w_down


def rope_partial(t, cos, sin):
    c = cos[None, :, None, :].astype(t.dtype)
    s = sin[None, :, None, :].astype(t.dtype)
    t1 = t[..., :ROT_HALF]
    t2 = t[..., ROT_HALF:ROT_DIM]
    return jnp.concatenate([t1 * c - t2 * s, t2 * c + t1 * s, t[..., ROT_DIM:]], axis=-1)


def banded_causal_attention(q, k, v, n_back):
    N, L, H, dh = q.shape
    nb = -(-L // BLOCK)
    pad = nb * BLOCK - L
    padw = ((0, 0), (0, pad), (0, 0), (0, 0))
    q = jnp.pad(q, padw)
    k = jnp.pad(k, padw)
    v = jnp.pad(v, padw)
    qb = q.reshape(N, nb, BLOCK, H, dh)

    def with_prev(t):
        tb = t.reshape(N, nb, BLOCK, H, dh)
        prev = jnp.concatenate([jnp.zeros_like(tb[:, :1]), tb[:, :-1]], axis=1)
        return jnp.concatenate([prev, tb], axis=2)

    kk = with_prev(k)
    vv = with_prev(v)
    s = jnp.einsum('nbqhd,nbkhd->nbhqk', qb, kk).astype(jnp.float32) * ATTN_SCALE
    qi = jnp.arange(BLOCK)[:, None]
    ki = jnp.arange(2 * BLOCK)[None, :]
    dist = qi + BLOCK - ki
    kpos = jnp.arange(nb)[:, None, None] * BLOCK - BLOCK + ki[None]
    mask = (dist >= 0) & (dist <= n_back) & (kpos >= 0)
    s = jnp.where(mask[None, :, None], s, -jnp.inf)
    lse = jax.nn.logsumexp(s, axis=-1)
    p = jnp.exp(s - lse[..., None])
    o = jnp.einsum('nbhqk,nbkhd->nbqhd', p.astype(v.dtype), vv)
    o = o.reshape(N, nb * BLOCK, H, dh)[:, :L]
    lse = lse.transpose(0, 1, 3, 2).reshape(N, nb * BLOCK, H)[:, :L]
    return o, lse


def dilated_group_attention(q, k, v, window, dilation):
    B, S, H, dh = q.shape
    r = dilation
    L = S // r

    def split(t):
        return t.reshape(B, L, r, H, dh).transpose(0, 2, 1, 3, 4).reshape(B * r, L, H, dh)

    o, lse = banded_causal_attention(split(q), split(k), split(v), window // r)
    o = o.reshape(B, r, L, H, dh).transpose(0, 2, 1, 3, 4).reshape(B, S, H, dh)
    lse = lse.reshape(B, r, L, H).transpose(0, 2, 1, 3).reshape(B, S, H)
    return o, lse


def memory_attention(q_mem, mem, w_kv):
    B, S, _ = q_mem.shape
    M = mem.shape[1]
    qm = q_mem.reshape(B, S, N_MEM_HEADS, HEAD_DIM)
    km, vm = jnp.split(mem @ w_kv, 2, axis=-1)
    km = km.reshape(B, M, N_MEM_HEADS, HEAD_DIM)
    vm = vm.reshape(B, M, N_MEM_HEADS, HEAD_DIM)
    s = jnp.einsum('bshd,bmhd->bhsm', qm, km).astype(jnp.float32) * ATTN_SCALE
    p = jax.nn.softmax(s, axis=-1)
    return jnp.einsum('bhsm,bmhd->bshd', p.astype(vm.dtype), vm)


def dilated_mixer(x, mem, w_in, w_mem_kv, w_out, cos, sin):
    B, S, _ = x.shape
    h = x @ w_in
    q = h[..., :MIX_W].reshape(B, S, N_MIX_HEADS, HEAD_DIM)
    k = h[..., MIX_W:2 * MIX_W].reshape(B, S, N_MIX_HEADS, HEAD_DIM)
    v = h[..., 2 * MIX_W:3 * MIX_W].reshape(B, S, N_MIX_HEADS, HEAD_DIM)
    q_mem = h[..., 3 * MIX_W:]
    q = rope_partial(q, cos, sin)
    k = rope_partial(k, cos, sin)
    outs, lses = [], []
    for g, (window, dilation) in enumerate(DILATED_GROUPS):
        sl = slice(g * HEADS_PER_GROUP, (g + 1) * HEADS_PER_GROUP)
        o, l = dilated_group_attention(q[:, :, sl], k[:, :, sl], v[:, :, sl], window, dilation)
        outs.append(o)
        lses.append(l)
    alpha = jax.nn.softmax(jnp.stack(lses, axis=0), axis=0)
    o_a = jnp.einsum('gbsh,gbshd->bshd', alpha.astype(x.dtype), jnp.stack(outs, axis=0))
    o_m = memory_attention(q_mem, mem, w_mem_kv)
    cat = jnp.concatenate([o_a.reshape(B, S, -1), o_m.reshape(B, S, -1)], axis=-1)
    return cat @ w_out


def fox_attention(q, k, v, logf):
    B, S, H, dh = q.shape
    nq = S // BLOCK
    c = jnp.cumsum(logf, axis=1).transpose(0, 2, 1)
    qb = q.reshape(B, nq, BLOCK, H, dh).transpose(1, 0, 2, 3, 4)
    cqb = c.reshape(B, H, nq, BLOCK).transpose(2, 0, 1, 3)
    kpos = jnp.arange(S)

    def block(args):
        i, q_i, cq_i = args
        s = jnp.einsum('bqhd,bkhd->bhqk', q_i, k).astype(jnp.float32) * ATTN_SCALE
        s = s + cq_i[..., None] - c[:, :, None, :]
        qpos = i * BLOCK + jnp.arange(BLOCK)
        s = jnp.where(kpos[None, :] <= qpos[:, None], s, -jnp.inf)
        p = jax.nn.softmax(s, axis=-1)
        return jnp.einsum('bhqk,bkhd->bqhd', p.astype(v.dtype), v)

    o = lax.map(block, (jnp.arange(nq), qb, cqb))
    return o.transpose(1, 0, 2, 3, 4).reshape(B, S, H, dh)


def forgetting_mixer(x, mem, w_in, forget_bias, w_mem_kv, w_out):
    B, S, _ = x.shape
    h = x @ w_in
    q = h[..., :MIX_W].reshape(B, S, N_MIX_HEADS, HEAD_DIM)
    k = h[..., MIX_W:2 * MIX_W].reshape(B, S, N_MIX_HEADS, HEAD_DIM)
    v = h[..., 2 * MIX_W:3 * MIX_W].reshape(B, S, N_MIX_HEADS, HEAD_DIM)
    f_logit = h[..., 3 * MIX_W:3 * MIX_W + N_MIX_HEADS].astype(jnp.float32)
    q_mem = h[..., 3 * MIX_W + N_MIX_HEADS:]
    logf = jax.nn.log_sigmoid(f_logit + forget_bias.astype(jnp.float32))
    o_b = fox_attention(q, k, v, logf)
    o_m = memory_attention(q_mem, mem, w_mem_kv)
    cat = jnp.concatenate([o_b.reshape(B, S, -1), o_m.reshape(B, S, -1)], axis=-1)
    return cat @ w_out


def setup_inputs(seed: int = 0) -> dict:
    key = jax.random.key(seed)
    ks = jax.random.split(key, 16)

    def nrm(k, shape, fan_in):
        return jax.random.normal(k, shape, jnp.float32) * fan_in ** -0.5

    x = jax.random.normal(ks[0], (BATCH, SEQ, D_MODEL), jnp.float32)
    mem = jax.random.normal(ks[1], (BATCH, MEM_LEN, D_MODEL), jnp.float32)
    ffn1_w_gate_up = nrm(ks[2], (DEPTH, D_MODEL, 2 * D_FF), D_MODEL)
    ffn1_w_down = nrm(ks[3], (DEPTH, D_FF, D_MODEL), D_FF) * DEEPNORM_BETA
    ffn2_w_gate_up = nrm(ks[4], (DEPTH, D_MODEL, 2 * D_FF), D_MODEL)
    ffn2_w_down = nrm(ks[5], (DEPTH, D_FF, D_MODEL), D_FF) * DEEPNORM_BETA
    ln_gain = 1.0 + 0.02 * jax.random.normal(ks[6], (DEPTH, 3, D_MODEL), jnp.float32)
    ln_bias = 0.02 * jax.random.normal(ks[7], (DEPTH, 3, D_MODEL), jnp.float32)
    mem_w_kv = nrm(ks[8], (DEPTH, D_MODEL, 2 * MEM_W), D_MODEL)
    a_w_in = nrm(ks[9], (N_A_LAYERS, D_MODEL, A_IN_W), D_MODEL)
    a_w_out = nrm(ks[10], (N_A_LAYERS, A_OUT_IN, D_MODEL), A_OUT_IN) * DEEPNORM_BETA
    b_w_in = nrm(ks[11], (N_B_LAYERS, D_MODEL, B_IN_W), D_MODEL)
    b_forget_bias = jax.random.uniform(ks[12], (N_B_LAYERS, N_MIX_HEADS), jnp.float32, 1.0, 4.0)
    b_w_out = nrm(ks[13], (N_B_LAYERS, B_OUT_IN, D_MODEL), B_OUT_IN) * DEEPNORM_BETA
    return {"x": x, "mem": mem,
            "ffn1_w_gate_up": ffn1_w_gate_up, "ffn1_w_down": ffn1_w_down,
            "ffn2_w_gate_up": ffn2_w_gate_up, "ffn2_w_down": ffn2_w_down,
            "ln_gain": ln_gain, "ln_bias": ln_bias, "mem_w_kv": mem_w_kv,
            "a_w_in": a_w_in, "a_w_out": a_w_out,
            "b_w_in": b_w_in, "b_forget_bias": b_forget_bias, "b_w_out": b_w_out}


def reference(x, mem, ffn1_w_gate_up, ffn1_w_down, ffn2_w_gate_up, ffn2_w_down,
              ln_gain, ln_bias, mem_w_kv, a_w_in, a_w_out, b_w_in, b_forget_bias, b_w_out):
    pos = jnp.arange(x.shape[1], dtype=jnp.float32)
    inv_freq = 1.0 / (ROPE_THETA ** (jnp.arange(ROT_HALF, dtype=jnp.float32) / ROT_HALF))
    ang = pos[:, None] * inv_freq[None, :]
    cos = jnp.cos(ang)
    sin = jnp.sin(ang)
    for i in range(DEPTH):
        j = i // N_MIXERS
        x = layer_norm(DEEPNORM_ALPHA * x + 0.5 * swiglu(x, ffn1_w_gate_up[i], ffn1_w_down[i]),
                       ln_gain[i, 0], ln_bias[i, 0])
        if i % N_MIXERS == 0:
            mix = dilated_mixer(x, mem, a_w_in[j], mem_w_kv[i], a_w_out[j], cos, sin)
        else:
            mix = forgetting_mixer(x, mem, b_w_in[j], b_forget_bias[j], mem_w_kv[i], b_w_out[j])
        x = layer_norm(DEEPNORM_ALPHA * x + mix, ln_gain[i, 1], ln_bias[i, 1])
        x = layer_norm(DEEPNORM_ALPHA * x + 0.5 * swiglu(x, ffn2_w_gate_up[i], ffn2_w_down[i]),
                       ln_gain[i, 2], ln_bias[i, 2])
    return x
```

```python
import sys
import numpy as np
from contextlib import ExitStack

import concourse.bass as bass
import concourse.mybir as mybir
from concourse.bass_utils import run_bass_kernel_spmd

F32 = mybir.dt.float32
BF16 = mybir.dt.bfloat16
AF = mybir.ActivationFunctionType
ALU = mybir.AluOpType

D = 1024
S = 4096
DFF = 2816
NFC = DFF // 128
DEPTH = 2
ALPHA = float((2 * DEPTH) ** 0.25)
LN_EPS = 1e-5
HD = 64
NMIX = 12
NMEMH = 4
MEML = 256
MIXW = 768
MEMW = 256
A_IN_W = 3 * MIXW + MEMW
B_IN_W = 3 * MIXW + NMIX + MEMW
SCALE = HD ** -0.5
TT = 1024
NT = S // TT

ENGS = ("pe", "act", "dve", "pool", "sp")
DEBUG_LINES = False
DEBUG_UPTO = 99
DEBUG_PROJ = 0
LINEMAP = {}


class Buf:
    def __init__(self, ap, name=""):
        self.ap = ap
        self.name = name
        self.w = None
        self.r = {}

    def __getitem__(self, k):
        return self.ap[k]


class DmaSem:
    def __init__(self, key, handle):
        self.key = key
        self.h = handle
        self.count = 0


class _Rec:
    def __init__(self):
        self.call = None

    def __getattr__(self, name):
        def f(*a, **kw):
            assert self.call is None
            self.call = (name, a, kw)
            return self
        return f


class Prog:
    def __init__(self, nc, es, n_dma_sems=18, n_stage_sets=8, n_sw_sems=26):
        self.nc = nc
        self.sem = {}
        self.q = {e: [] for e in ENGS}
        self.cnt = {e: 0 for e in ENGS}
        self.waited = {e: {} for e in ENGS}
        self.pend = {e: ([], []) for e in ENGS}
        self.free_eng_sets = []
        for i in range(n_stage_sets):
            st = {}
            for e in ENGS:
                st[e] = es.enter_context(nc.semaphore(f"s_{e}_{i}"))
            self.free_eng_sets.append(st)
        self.dma_sems = []
        for i in range(n_dma_sems):
            k = f"dma{i}"
            h = es.enter_context(nc.semaphore(f"s_{k}"))
            self.sem[k] = h
            self.dma_sems.append(DmaSem(k, h))
        self.sw_sems = []
        for i in range(n_sw_sems):
            k = f"swdma{i}"
            h = es.enter_context(nc.semaphore(f"s_{k}"))
            self.sem[k] = h
            self.sw_sems.append(DmaSem(k, h))
        self.sw_next = 0
        self.stage_sw = []
        self.dma_next = 0
        self.stage_id = -1

    def begin_stage(self):
        self.stage_id += 1
        st = self.free_eng_sets[self.stage_id]
        for e in ENGS:
            self.sem[e] = st[e]
        self.q = {e: [] for e in ENGS}
        self.cnt = {e: 0 for e in ENGS}
        self.waited = {e: {} for e in ENGS}
        self.pend = {e: ([], []) for e in ENGS}
        self.dma_next = 0
        self.stage_sw = []

    def new_dma_sem(self, sw=False):
        if sw:
            s = self.sw_sems[self.sw_next]
            self.sw_next += 1
            self.stage_sw.append(s)
            return s
        s = self.dma_sems[self.dma_next]
        self.dma_next += 1
        return s

    def _eng(self, e):
        nc = self.nc
        return {"pe": nc.tensor, "act": nc.scalar, "dve": nc.vector, "pool": nc.gpsimd, "sp": nc.sync}[e]

    def _wait(self, e, k, v):
        if self.waited[e].get(k, 0) >= v:
            return
        self.waited[e][k] = v
        h = self.sem[k]
        self.q[e].append(lambda eng, h=h, v=v: eng.wait_ge(h, v))

    def _deps(self, e, reads, writes):
        for b in reads:
            if b.w is not None:
                self._wait(e, *b.w)
        for b in writes:
            if b.w is not None:
                self._wait(e, *b.w)
            for t in b.r.values():
                self._wait(e, *t)

    def op(self, e, fn, reads=(), writes=(), inc=True, touch=()):
        self._deps(e, reads, writes)
        pr, pw = self.pend[e]
        pr.extend(reads)
        pw.extend(writes)
        pw.extend(touch)
        rec = _Rec()
        fn(rec)
        name, a, kw = rec.call
        ln = sys._getframe(1).f_lineno if DEBUG_LINES else 0

        def emit(eng, name=name, a=a, kw=kw, ln=ln):
            i = getattr(eng, name)(*a, **kw)
            if DEBUG_LINES:
                LINEMAP[i.ins.name] = ln
            return i
        if not inc:
            self.q[e].append(emit)
            return None
        self.cnt[e] += 1
        tok = (e, self.cnt[e])
        h = self.sem[e]
        self.q[e].append(lambda eng, emit=emit, h=h: emit(eng).then_inc(h, 1))
        for b in pr:
            b.r[e] = tok
        for b in pw:
            b.w = tok
            b.r = {}
        self.pend[e] = ([], [])
        return tok

    def dma(self, e, out_ap, in_ap, sem, reads=(), writes=()):
        assert not self.pend[e][0] and not self.pend[e][1]
        assert (e == "pool") == sem.key.startswith("swdma"), (e, sem.key)
        self._deps(e, reads, writes)
        sem.count += 16
        tok = (sem.key, sem.count)
        h = sem.h
        self.q[e].append(lambda eng, o=out_ap, i=in_ap, h=h: eng.dma_start(out=o, in_=i).then_inc(h, 16))
        for b in reads:
            b.r[sem.key] = tok
        for b in writes:
            b.w = tok
            b.r = {}
        return tok

    def wait_tok(self, e, tok):
        self._wait(e, *tok)

    def end_stage(self, final_toks=()):
        for e in ENGS:
            assert not self.pend[e][0] and not self.pend[e][1], e
        for e in ENGS:
            for k in ENGS:
                if k != e and self.cnt[k] > 0:
                    self._wait(e, k, self.cnt[k])
        for t in final_toks:
            self._wait("sp", *t)
        for ds in self.dma_sems[:self.dma_next] + self.stage_sw:
            if ds.count > 0:
                self._wait("sp", ds.key, ds.count)
        with self.nc.Block() as block:
            for e, reg in (("pe", block.tensor), ("act", block.scalar), ("dve", block.vector),
                           ("pool", block.gpsimd), ("sp", block.sync)):
                lst = self.q[e]

                def body(eng, lst=lst):
                    for f in lst:
                        f(eng)
                reg(body)


class Ctx:
    def __init__(self, P, es):
        self.P = P
        self.nc = P.nc
        self.es = es

    def sb(self, name, shape, dt):
        t = self.es.enter_context(self.nc.sbuf_tensor(f"{name}_{self.P.stage_id}", list(shape), dt))
        return t

    def ps(self, name, shape, dt=F32):
        t = self.es.enter_context(self.nc.psum_tensor(f"{name}_{self.P.stage_id}", list(shape), dt))
        return t


def make_ident(P, C, ident_f32, sem):
    idf = Buf(C.sb("idf", [128, 128], F32))
    idb = Buf(C.sb("idb", [128, 128], BF16))
    P.dma("sp", idf.ap[:, :], ident_f32, P.new_dma_sem(), writes=[idf])
    P.op("dve", lambda v: v.tensor_copy(idb.ap[:, :], idf.ap[:, :]), reads=[idf], writes=[idb])
    return idf, idb


class XPrep:
    def __init__(self, P, C, idb, banks):
        self.P = P
        self.xst = [Buf(C.sb(f"xst{i}", [128, D], F32)) for i in range(2)]
        self.xbf = [Buf(C.sb(f"xbf{i}", [128, D], BF16)) for i in range(2)]
        self.sem = [P.new_dma_sem() for _ in range(2)]
        self.idb = idb
        self.banks = banks
        self.n = 0

    def __call__(self, row_ap, dst):
        self.part2(self.part1(row_ap), dst)

    def part1(self, row_ap):
        P = self.P
        k = self.n % 2
        bank = self.banks[self.n % len(self.banks)]
        self.n += 1
        xst, xbf = self.xst[k], self.xbf[k]
        P.dma("sp", xst.ap[:, :], row_ap, self.sem[k], writes=[xst])
        P.op("act", lambda a: a.copy(xbf.ap[:, :], xst.ap[:, :]), reads=[xst], writes=[xbf])
        return (k, bank)

    def part2(self, h, dst):
        P = self.P
        k, bank = h
        xbf, idb = self.xbf[k], self.idb
        pv = bank.ap.bitcast(BF16)
        for c in range(8):
            P.op("pe", lambda pe, c=c: pe.transpose(
                pv[:, c * 128:(c + 1) * 128], xbf.ap[:, c * 128:(c + 1) * 128], idb.ap[:, :]),
                reads=[xbf, idb], writes=[bank] if c == 0 else [], inc=(c == 7))
        P.op("dve", lambda v: v.tensor_copy(dst.ap, pv[:, :].rearrange("p (c n) -> p c n", c=8)),
             reads=[bank], writes=[dst])


class LNEpi:
    def __init__(self, P, C, gain, bias, sem_const, depth=2):
        self.P = P
        self.depth = depth
        self.xres = [Buf(C.sb(f"xres{i}", [128, D], F32)) for i in range(depth)]
        self.zb = [Buf(C.sb(f"z{i}", [128, D], F32)) for i in range(depth)]
        self.ob = [Buf(C.sb(f"ob{i}", [128, D], F32)) for i in range(depth)]
        self.st = [Buf(C.sb(f"st{i}", [128, 16], F32)) for i in range(depth)]
        self.gbc = Buf(C.sb("gbc", [128, D], F32))
        self.bbc = Buf(C.sb("bbc", [128, D], F32))
        self.s_x = [P.new_dma_sem() for _ in range(depth)]
        self.s_o = [P.new_dma_sem() for _ in range(depth)]
        P.dma("sp", self.gbc.ap[:, :], gain.partition_broadcast(128), P.new_dma_sem(), writes=[self.gbc])
        P.dma("sp", self.bbc.ap[:, :], bias.partition_broadcast(128), P.new_dma_sem(), writes=[self.bbc])

    def load_x(self, k, row_ap):
        self.P.dma("sp", self.xres[k].ap[:, :], row_ap, self.s_x[k], writes=[self.xres[k]])

    def __call__(self, k, banks, dst_ap, yscale):
        self.e1(k, banks, yscale)
        self.e2(k, dst_ap)

    def e1(self, k, banks, yscale):
        self.e1a(k)
        self.e1b(k, banks, yscale)

    def e1a(self, k):
        xres = self.xres[k]
        self.P.op("act", lambda a: a.mul(xres.ap[:, :], xres.ap[:, :], ALPHA), reads=[xres], writes=[xres])

    def e1b(self, k, banks, yscale):
        P = self.P
        xres, z, o, sb_ = self.xres[k], self.zb[k], self.ob[k], self.st[k]
        for n in range(2):
            P.op("dve", lambda v, n=n: v.scalar_tensor_tensor(
                z.ap[:, n * 512:(n + 1) * 512], banks[n].ap[:, :], float(yscale),
                xres.ap[:, n * 512:(n + 1) * 512], ALU.mult, ALU.add),
                reads=[banks[n], xres], writes=[z] if n == 0 else [], inc=(n == 1))
        for n in range(2):
            P.op("dve", lambda v, n=n: v.bn_stats(sb_.ap[:, n * 6:(n + 1) * 6], z.ap[:, n * 512:(n + 1) * 512]),
                 reads=[z], writes=[sb_] if n == 0 else [], inc=(n == 1))
        P.op("dve", lambda v: v.bn_aggr(sb_.ap[:, 12:14], sb_.ap[:, 0:12]), reads=[sb_], writes=[sb_])
        P.op("act", lambda a: a.activation(sb_.ap[:, 14:15], sb_.ap[:, 13:14], AF.Sqrt, bias=LN_EPS, scale=1.0),
             reads=[sb_], writes=[sb_])

    def e2(self, k, dst_ap):
        P = self.P
        xres, z, o, sb_ = self.xres[k], self.zb[k], self.ob[k], self.st[k]
        gbc, bbc = self.gbc, self.bbc
        P.op("dve", lambda v: v.reciprocal(sb_.ap[:, 14:15], sb_.ap[:, 14:15]), reads=[sb_], writes=[sb_])
        P.op("dve", lambda v: v.scalar_tensor_tensor(
            sb_.ap[:, 15:16], sb_.ap[:, 12:13], -1.0, sb_.ap[:, 14:15], ALU.mult, ALU.mult),
            reads=[sb_], writes=[sb_])
        P.op("act", lambda a: a.activation(z.ap[:, :], z.ap[:, :], AF.Identity, bias=sb_.ap[:, 15:16],
                                           scale=sb_.ap[:, 14:15]), reads=[z, sb_], writes=[z])
        P.op("pool", lambda g: g.tensor_tensor(o.ap[:, :], z.ap[:, :], gbc.ap[:, :], ALU.mult),
             reads=[z, gbc], writes=[o])
        P.op("pool", lambda g: g.tensor_tensor(o.ap[:, :], o.ap[:, :], bbc.ap[:, :], ALU.add),
             reads=[o, bbc], writes=[o])
        P.dma("sp", dst_ap, o.ap[:, :], self.s_o[k], reads=[o])


def stage_ffn(P, x_src, x_dst, w_gu, w_down, gain, bias, ident_f32, ntiles=NT):
    P.begin_stage()
    NWG = 3
    with ExitStack() as es:
        C = Ctx(P, es)
        wd = C.sb("wd", [128, NFC, D], BF16)
        wg = [Buf(C.sb(f"wg{i}", [128, 8, 512], BF16)) for i in range(NWG)]
        xT_t = [C.sb(f"xT{i}", [128, 8, TT], BF16) for i in range(2)]
        gT_t = C.sb("gT", [128, NFC, TT], BF16)
        sg = [Buf(C.sb(f"sg{i}", [128, 512], F32)) for i in range(2)]
        pg = [Buf(C.ps(f"pg{i}", [128, 512], F32)) for i in range(4)]
        pd = [Buf(C.ps(f"pd{i}", [128, 512], F32)) for i in range(4)]
        wd_b = [Buf(wd[:, j, :]) for j in range(NFC)]
        xT = [[Buf(xT_t[i][:, :, s * 128:(s + 1) * 128]) for s in range(8)] for i in range(2)]
        gT = [[Buf(gT_t[:, j, h * 512:(h + 1) * 512]) for h in range(2)] for j in range(NFC)]

        s_const = P.new_dma_sem()
        s_wd = P.new_dma_sem(sw=True)
        s_wg = [P.new_dma_sem(sw=True) for _ in range(NWG)]
        idf, idb = make_ident(P, C, ident_f32, s_const)
        xprep = XPrep(P, C, idb, pd)
        epi = LNEpi(P, C, gain, bias, s_const)

        wdv = w_down.rearrange("(c p) n -> p c n", p=128)
        for j in range(NFC):
            P.dma("pool", wd[:, j, :], wdv[:, j, :], s_wd, writes=[wd_b[j]])
        for j in range(NFC):
            wd_b[j].w = (s_wd.key, s_wd.count)
        wguv = w_gu.rearrange("(c p) n -> p c n", p=128)

        def load_wg(step):
            j2 = step % (NFC // 2)
            slot = step % NWG
            b = wg[slot]
            P.dma("pool", b.ap[:, :, 0:256], wguv[:, :, 256 * j2:256 * j2 + 256], s_wg[slot], writes=[b])
            P.dma("pool", b.ap[:, :, 256:512], wguv[:, :, DFF + 256 * j2:DFF + 256 * j2 + 256], s_wg[slot])
            b.w = (s_wg[slot].key, s_wg[slot].count)

        nsteps = ntiles * (NFC // 2)
        xsv = x_src.rearrange("(t s p) d -> t s p d", s=8, p=128)
        xdv = x_dst.rearrange("(t s p) d -> t s p d", s=8, p=128)

        for stp in range(min(NWG, nsteps)):
            load_wg(stp)
        for s in range(8):
            xprep(xsv[0, s], xT[0][s])

        gstep = 0
        grp = 0
        xh = {}
        pending_e2 = None
        for t in range(ntiles):
            xTt = xT[t % 2]
            for j2 in range(NFC // 2):
                if j2 == 1 and pending_e2 is not None:
                    epi.e2(*pending_e2)
                    pending_e2 = None
                slot = gstep % NWG
                wb = wg[slot]
                for jj in range(2):
                    j = 2 * j2 + jj
                    for h in range(2):
                        bg = pg[2 * (grp % 2)]
                        bu = pg[2 * (grp % 2) + 1]
                        sgb = sg[grp % 2]
                        grp += 1
                        rd = [wb] + [xTt[4 * h + q] for q in range(4)]
                        for (bank, off) in ((bg, 0), (bu, 256)):
                            for c in range(8):
                                P.op("pe", lambda pe, bank=bank, c=c, off=off, jj=jj, h=h, wb=wb, t=t: pe.matmul(
                                    bank.ap[:, :], wb.ap[:, c, off + jj * 128:off + jj * 128 + 128],
                                    xT_t[t % 2][:, c, h * 512:(h + 1) * 512], start=(c == 0), stop=(c == 7)),
                                    reads=rd if c == 0 else [], writes=[bank] if c == 0 else [], inc=(c == 7))
                        P.op("act", lambda a, sgb=sgb, bg=bg: a.activation(sgb.ap[:, :], bg.ap[:, :], AF.Silu),
                             reads=[bg], writes=[sgb])
                        dst = gT[j][h]
                        P.op("dve", lambda v, dst=dst, sgb=sgb, bu=bu: v.tensor_tensor(
                            dst.ap, sgb.ap[:, :], bu.ap[:, :], ALU.mult), reads=[sgb, bu], writes=[dst])
                gstep += 1
                if gstep + NWG - 1 < nsteps:
                    load_wg(gstep + NWG - 1)
                if t + 1 < ntiles:
                    if 2 <= j2 <= 9:
                        xprep.part2(xh.pop(j2 - 2), xT[(t + 1) % 2][j2 - 2])
                    if 1 <= j2 <= 8:
                        xh[j2 - 1] = xprep.part1(xsv[t + 1, j2 - 1])
            epi.load_x(0, xsv[t, 0])
            for s in range(8):
                k = s % 2
                if s + 1 < 8:
                    epi.load_x((s + 1) % 2, xsv[t, s + 1])
                banks = (pd[2 * k], pd[2 * k + 1])
                for n in range(2):
                    bank = banks[n]
                    for j in range(NFC):
                        P.op("pe", lambda pe, bank=bank, j=j, s=s, n=n: pe.matmul(
                            bank.ap[:, :], gT_t[:, j, s * 128:(s + 1) * 128], wd[:, j, n * 512:(n + 1) * 512],
                            start=(j == 0), stop=(j == NFC - 1)),
                            reads=[gT[j][s // 4], wd_b[j]], writes=[bank] if j == 0 else [], inc=(j == NFC - 1))
                epi.e1(k, banks, 0.5)
                if s >= 1:
                    epi.e2((s - 1) % 2, xdv[t, s - 1])
            pending_e2 = (1, xdv[t, 7])
        epi.e2(*pending_e2)
        P.end_stage()


def stage_mix(P, layer, x_src, w_in, mem, w_kv, og, consts, fbias=None, cq=None, ck=None):
    ident_f32, rope_tabs, masks_f32 = consts
    P.begin_stage()
    dil = (1, 4, 16) if layer == 0 else (1, 1, 1)
    qmem_off = 3 * MIXW if layer == 0 else 3 * MIXW + NMIX
    with ExitStack() as es:
        C = Ctx(P, es)
        pb = [Buf(C.ps(f"pb{i}", [128, 512], F32)) for i in range(8)]
        s_const = P.new_dma_sem()
        idf, idb = make_ident(P, C, ident_f32, s_const)
        xprep = XPrep(P, C, idb, [pb[4], pb[5]])
        xT_t = [C.sb(f"xT{i}", [128, 8, TT], BF16) for i in range(2)]
        xT = [[Buf(xT_t[i][:, :, s * 128:(s + 1) * 128]) for s in range(8)] for i in range(2)]
        wgt = [Buf(C.sb("wgt0", [128, 8, 768], BF16))] * 2
        s_wgt = [P.new_dma_sem(sw=True)] * 2
        qkbf = [Buf(C.sb(f"qkbf{i}", [128, 512], BF16)) for i in range(2)]
        qk_all = C.sb("qkall", [70, 8, S], BF16)
        qkt = [Buf(qk_all[0:64, :, t * TT:(t + 1) * TT]) for t in range(NT)]
        v_sb = C.sb("vsb", [128, 32, 4, 65], BF16)
        vt = [Buf(v_sb[:, 8 * t:8 * t + 8, :, :]) for t in range(NT)]
        vones = Buf(v_sb[:, :, :, 64:65])
        pT = [Buf(C.sb(f"pT{i}", [128, 512], BF16)) for i in range(3)]
        ost = [Buf(C.sb(f"ost{i}", [128, 1040], F32)) for i in range(2)]
        s_ost = [P.new_dma_sem() for _ in range(2)]
        mkf = Buf(C.sb("mkf", [128, 384], F32))
        mk = Buf(C.sb("mk", [128, 3, 128], BF16))
        memT_t = C.sb("memT", [128, 8, MEML], BF16)
        memT = [Buf(memT_t[:, :, i * 128:(i + 1) * 128]) for i in range(2)]
        wkv = Buf(C.sb("wkv", [128, 8, 512], BF16))
        kmT = Buf(C.sb("kmT", [64, 4, MEML], BF16))
        vm_sb = Buf(C.sb("vmsb", [128, 2, 4, 65], BF16))
        s_wkv = P.new_dma_sem(sw=True)
        if layer == 0:
            tabg = Buf(C.sb("tabg", [128, 32, 128], F32))
            s_tab = P.new_dma_sem()
            rtmp = [[Buf(C.sb(f"rt{i}_{q}", [128, 64], F32)) for q in range(4)] for i in range(2)]
            a32b = [Buf(C.sb(f"a32_{i}", [128, 512], F32)) for i in range(2)]
        else:
            wf = Buf(C.sb("wf", [128, 8, NMIX], BF16))
            s_wf = P.new_dma_sem(sw=True)
            fst = [Buf(C.sb(f"fst{i}", [128, NMIX], F32)) for i in range(2)]
            fT = Buf(C.sb("fT", [NMIX, S], F32))
            augb = Buf(qk_all[64:70, :, :])
            s_aug = P.new_dma_sem()
            cq_b = Buf(cq)
            ck_b = Buf(ck)

        P.dma("sp", mkf.ap[:, :], masks_f32, s_const, writes=[mkf])
        P.op("dve", lambda v: v.tensor_copy(mk.ap[:, :, :], mkf.ap[:, :].rearrange("p (a b) -> p a b", a=3)),
             reads=[mkf], writes=[mk])
        M_DIAG, M_PREV, M_ALL = 0, 1, 2
        P.op("pool", lambda g: g.memset(v_sb[:, :, :, :].rearrange("p u j e -> p (u j) e")[:, :, 64:65], 1.0),
             writes=[vones])
        P.op("pool", lambda g: g.memset(vm_sb.ap[:, :, :, :].rearrange("p u j e -> p (u j) e")[:, :, 64:65], 1.0),
             writes=[vm_sb])

        w_inv = w_in.rearrange("(c p) n -> p c n", p=128)

        def load_wgt(g):
            b = wgt[g % 2]
            if g < 3:
                for i in range(3):
                    P.dma("pool", b.ap[:, :, i * 256:(i + 1) * 256],
                          w_inv[:, :, i * MIXW + g * 256:i * MIXW + (g + 1) * 256], s_wgt[g % 2],
                          writes=[b] if i == 0 else [])
            else:
                P.dma("pool", b.ap[:, :, 0:256], w_inv[:, :, qmem_off:qmem_off + 256], s_wgt[g % 2], writes=[b])
            b.w = (s_wgt[g % 2].key, s_wgt[g % 2].count)

        load_wgt(0)
        if layer == 1:
            P.dma("pool", wf.ap[:, :, :], w_inv[:, :, 3 * MIXW:3 * MIXW + NMIX], s_wf, writes=[wf])
        P.dma("pool", wkv.ap[:, :, :], w_kv.rearrange("(c p) n -> p c n", p=128), s_wkv, writes=[wkv])

        def proj_mm(g, u, xTbuf_t, xTb, s, nslots, with_v, with_f, rope_ap):
            wb = wgt[g % 2]
            A = pb[u % 2]
            Bk = pb[2 + u % 2]
            ncol = nslots * 64
            for c in range(8):
                P.op("pe", lambda pe, c=c: pe.matmul(
                    A.ap[:, 0:ncol], xTbuf_t[:, c, s * 128:(s + 1) * 128], wb.ap[:, c, 0:ncol],
                    start=(c == 0), stop=(c == 7)),
                    reads=[xTb, wb] if c == 0 else [], writes=[A] if c == 0 else [], inc=(c == 7))
            if with_v:
                for c in range(8):
                    P.op("pe", lambda pe, c=c: pe.matmul(
                        Bk.ap[:, 0:256], xTbuf_t[:, c, s * 128:(s + 1) * 128], wb.ap[:, c, 512:768],
                        start=(c == 0), stop=(c == 7), skip_group_check=True),
                        reads=[xTb, wb] if c == 0 else [], writes=[Bk] if c == 0 else [],
                        inc=(c == 7 and not with_f))
            if with_f:
                for c in range(8):
                    P.op("pe", lambda pe, c=c: pe.matmul(
                        Bk.ap[:, 256:256 + NMIX], xTbuf_t[:, c, s * 128:(s + 1) * 128], wf.ap[:, c, :],
                        start=False, stop=(c == 7), skip_group_check=True), reads=[wf] if c == 0 else [], inc=(c == 7))

        def proj_fin(g, u, xTbuf_t, xTb, s, nslots, with_v, with_f, rope_ap):
            wb = wgt[g % 2]
            A = pb[u % 2]
            Bk = pb[2 + u % 2]
            ncol = nslots * 64
            qb = qkbf[u % 2]
            if rope_ap is not None:
                tb = tabg
                a32 = a32b[u % 2]
                P.op("act", lambda a: a.copy(a32.ap[:, :], A.ap[:, :]), reads=[A], writes=[a32])
                Av = a32.ap[:, :].rearrange("p (j d) -> p j d", j=8)
                qv = qb.ap[:, :].rearrange("p (j d) -> p j d", j=8)
                cosv = tb.ap[:, u, 0:64].rearrange("p (j d) -> p j d", j=8)
                sinv = tb.ap[:, u, 64:128].rearrange("p (j d) -> p j d", j=8)
                rt = rtmp[u % 2]
                rv = [r_.ap[:, :].rearrange("p (j d) -> p j d", j=8) for r_ in rt]
                P.op("pool", lambda g_: g_.tensor_copy(qv[:, :, 16:64], Av[:, :, 16:64]), reads=[a32], writes=[qb])
                P.op("dve", lambda v: v.tensor_tensor(rv[0], Av[:, :, 0:8], cosv, ALU.mult),
                     reads=[a32, tb], writes=[rt[0]], inc=False)
                P.op("dve", lambda v: v.tensor_tensor(rv[1], Av[:, :, 8:16], sinv, ALU.mult),
                     reads=[a32, tb], writes=[rt[1]], inc=False)
                P.op("dve", lambda v: v.tensor_tensor(rv[2], Av[:, :, 8:16], cosv, ALU.mult),
                     reads=[a32, tb], writes=[rt[2]], inc=False)
                P.op("dve", lambda v: v.tensor_tensor(rv[3], Av[:, :, 0:8], sinv, ALU.mult),
                     reads=[a32, tb], writes=[rt[3]])
                P.op("pool", lambda g_: g_.tensor_tensor(qv[:, :, 0:8], rv[0], rv[1], ALU.subtract),
                     reads=[rt[0], rt[1]], writes=[qb], inc=False)
                P.op("pool", lambda g_: g_.tensor_tensor(qv[:, :, 8:16], rv[2], rv[3], ALU.add),
                     reads=[rt[2], rt[3]], writes=[qb])
            else:
                P.op("act", lambda a: a.copy(qb.ap[:, 0:ncol], A.ap[:, 0:ncol]), reads=[A], writes=[qb])
            if with_v:
                P.op("act", lambda a: a.copy(v_sb[:, u, :, 0:64], Bk.ap[:, 0:256].rearrange("p (j d) -> p j d", j=4)),
                     reads=[Bk], writes=[vt[u // 8]])
            if with_f:
                fs = fst[u % 2]
                P.op("act", lambda a: a.copy(fs.ap[:, :], Bk.ap[:, 256:256 + NMIX]), reads=[Bk], writes=[fs])
                P.op("pe", lambda pe: pe.transpose(Bk.ap[0:NMIX, 384:512], fs.ap[:, :], idf.ap[:, :]),
                     reads=[fs, idf], writes=[Bk])
                P.op("dve", lambda v: v.tensor_copy(fT.ap[:, u * 128:(u + 1) * 128], Bk.ap[0:NMIX, 384:512]),
                     reads=[Bk], writes=[fT])
            tq = pb[6 + u % 2]
            tqv = tq.ap.bitcast(BF16)
            for sl in range(nslots):
                P.op("pe", lambda pe, sl=sl: pe.transpose(
                    tqv[0:64, sl * 128:(sl + 1) * 128], qb.ap[:, sl * 64:(sl + 1) * 64], idb.ap[:, :]),
                    reads=[qb, idb], writes=[tq] if sl == 0 else [], inc=(sl == nslots - 1))
            P.op("dve", lambda v: v.tensor_copy(
                qk_all[0:64, 0:nslots, u * 128:(u + 1) * 128],
                tqv[0:64, 0:nslots * 128].rearrange("p (j n) -> p j n", j=nslots)),
                reads=[tq], writes=[qkt[u // 8]])

        def proj_group(g):
            r = dil[g] if g < 3 else 1
            L = S // r
            nslots = 8 if g < 3 else 4
            xr = x_src.rearrange("(n r) d -> r n d", r=r)
            if layer == 0 and g < 3:
                P.dma("sp", tabg.ap[:, :, :], rope_tabs[g].rearrange("(u p) f -> p u f", p=128), s_tab, writes=[tabg])

            def rows(u):
                m0 = u * 128
                return xr[m0 // L, (m0 % L):(m0 % L) + 128, :]

            if g == 0:
                for s in range(8):
                    xprep(rows(s), xT[0][s])
            rope_ap = True if (layer == 0 and g < 3) else None
            xh = {}

            def args(u):
                t, s = u // 8, u % 8
                return (g, u, xT_t[t % 2], xT[t % 2][s], s, nslots, g < 3, (layer == 1 and g == 0), rope_ap)

            proj_mm(*args(0))
            for u in range(32):
                if u + 7 in xh:
                    v = u + 7
                    xprep.part2(xh.pop(v), xT[(v // 8) % 2][v % 8])
                if u + 8 < 32:
                    xh[u + 8] = xprep.part1(rows(u + 8))
                if u + 1 < 32:
                    proj_mm(*args(u + 1))
                proj_fin(*args(u))
            if g + 1 < 4:
                load_wgt(g + 1)
                r2 = dil[g + 1] if g + 1 < 3 else 1
                xr2 = x_src.rearrange("(n r) d -> r n d", r=r2)
                L2 = S // r2
                for s in range(8):
                    m0 = s * 128
                    xprep(xr2[m0 // L2, (m0 % L2):(m0 % L2) + 128, :], xT[0][s])

        def forget_prep():
            CH = 512
            nb_ = Buf(C.sb("negb", [NMIX, 1], F32))
            ones = Buf(C.sb("ones12", [NMIX, CH], F32))
            onesb = Buf(C.sb("ones12b", [NMIX, CH], BF16))
            e1 = Buf(C.sb("e1", [NMIX, CH], F32))
            cc = [Buf(C.sb(f"cc{i}", [NMIX, CH], F32)) for i in range(2)]
            c8 = Buf(C.sb("c8", [NMIX, CH], F32))
            pc = [Buf(C.sb(f"pc{i}", [NMIX, CH], BF16)) for i in range(3)]
            npc = [Buf(C.sb(f"npc{i}", [NMIX, CH], BF16)) for i in range(3)]
            s_c = P.new_dma_sem()
            s_cs = P.new_dma_sem()
            P.dma("sp", nb_.ap[:, :], fbias.rearrange("(h o) -> h o", o=1), s_c, writes=[nb_])
            P.op("act", lambda a: a.mul(nb_.ap[:, :], nb_.ap[:, :], -1.0), reads=[nb_], writes=[nb_])
            P.op("pool", lambda g_: g_.memset(ones.ap[:, :], 1.0), writes=[ones])
            P.op("pool", lambda g_: g_.memset(onesb.ap[:, :], 1.0), writes=[onesb])
            for ci in range(S // CH):
                sl = slice(ci * CH, (ci + 1) * CH)
                P.op("act", lambda a, sl=sl: a.activation(e1.ap[:, :], fT.ap[:, sl], AF.Exp, bias=nb_.ap[:, 0:1],
                                                          scale=-1.0), reads=[fT, nb_], writes=[e1])
                P.op("act", lambda a: a.activation(e1.ap[:, :], e1.ap[:, :], AF.Ln, bias=1.0, scale=1.0),
                     reads=[e1], writes=[e1])
                cur, prev = cc[ci % 2], cc[(ci + 1) % 2]
                init = 0.0 if ci == 0 else prev.ap[:, CH - 1:CH]
                P.op("dve", lambda v, cur=cur, init=init: v.tensor_tensor_scan(
                    cur.ap[:, :], ones.ap[:, :], e1.ap[:, :], init, ALU.mult, ALU.subtract),
                    reads=[ones, e1, prev], writes=[cur])
                P.op("act", lambda a, cur=cur: a.mul(c8.ap[:, :], cur.ap[:, :], 1.0 / SCALE), reads=[cur], writes=[c8])
                for i in range(3):
                    P.op("dve", lambda v, i=i: v.tensor_copy(pc[i].ap[:, :], c8.ap[:, :]), reads=[c8], writes=[pc[i]])
                    if i < 2:
                        P.op("dve", lambda v, i=i: v.tensor_tensor(c8.ap[:, :], c8.ap[:, :], pc[i].ap[:, :],
                                                                    ALU.subtract), reads=[c8, pc[i]], writes=[c8])
                for i in range(3):
                    P.op("act", lambda a, i=i: a.mul(npc[i].ap[:, :], pc[i].ap[:, :], -1.0),
                         reads=[pc[i]], writes=[npc[i]])
                for i in range(3):
                    P.dma("sp", cq[:, i, sl], pc[i].ap[:, :], s_cs, reads=[pc[i]], writes=[cq_b] if i == 0 else [])
                    P.dma("sp", ck[:, 3 + i, sl], npc[i].ap[:, :], s_cs, reads=[npc[i]], writes=[ck_b] if i == 0 else [])
                    P.dma("sp", cq[:, 3 + i, sl], onesb.ap[:, :], s_cs, reads=[onesb])
                    P.dma("sp", ck[:, i, sl], onesb.ap[:, :], s_cs, reads=[onesb])
                cq_b.w = (s_cs.key, s_cs.count)
                ck_b.w = (s_cs.key, s_cs.count)
                for b_ in pc + npc + [onesb]:
                    b_.r[s_cs.key] = (s_cs.key, s_cs.count)

        def load_aug(g):
            for j in range(4):
                P.dma("sp", qk_all[64:70, j, :], cq[4 * g + j], s_aug, reads=[cq_b], writes=[augb] if j == 0 else [])
                P.dma("sp", qk_all[64:70, 4 + j, :], ck[4 * g + j], s_aug, reads=[ck_b])
            augb.w = (s_aug.key, s_aug.count)

        cnt = {"s": 0, "o": 0, "st": 0}

        def flush_o(ob, ncol, dst_ap, view=None):
            k = cnt["st"] % 2
            cnt["st"] += 1
            stg = ost[k]
            P.op("dve", lambda v: v.tensor_copy(stg.ap[:, 0:ncol], ob.ap[:, 0:ncol]), reads=[ob], writes=[stg])
            return stg, k

        def attn_banded(g):
            r = dil[g]
            nb = 32 // r
            ogv = og[g].rearrange("(n r) f -> r n f", r=r)
            items = [(B, hp) for B in range(32) for hp in range(2)]
            st_ = {}

            def emit_score(idx):
                B, hp = items[idx]
                b = B % nb
                sb_ = pb[idx % 3]
                pt = pT[idx % 3]
                tiles_q = [qkt[B // 8]] + ([qkt[(B - 1) // 8]] if b > 0 else [])
                for hh in range(2):
                    j = hp * 2 + hh
                    reg = sb_.ap[:, hh * 256:hh * 256 + 128]
                    first = (hh == 0)
                    if b > 0:
                        P.op("pe", lambda pe: pe.matmul(reg, idb.ap[:, :], mk.ap[:, M_PREV, :],
                                                         start=first, stop=False, skip_group_check=True),
                             reads=[idb, mk], writes=[sb_] if first else [], inc=False)
                        P.op("pe", lambda pe: pe.matmul(
                            reg, qk_all[0:64, 4 + j, (B - 1) * 128:B * 128], qk_all[0:64, j, B * 128:(B + 1) * 128],
                            start=False, stop=True, skip_group_check=True), reads=tiles_q, inc=False)
                    else:
                        P.op("pe", lambda pe: pe.matmul(reg, idb.ap[:, :], mk.ap[:, M_ALL, :],
                                                         start=first, stop=True, skip_group_check=True),
                             reads=[idb, mk], writes=[sb_] if first else [], inc=False)
                    reg2 = sb_.ap[:, hh * 256 + 128:hh * 256 + 256]
                    P.op("pe", lambda pe: pe.matmul(reg2, idb.ap[:, :], mk.ap[:, M_DIAG, :],
                                                     start=False, stop=False, skip_group_check=True), inc=False)
                    P.op("pe", lambda pe: pe.matmul(
                        reg2, qk_all[0:64, 4 + j, B * 128:(B + 1) * 128], qk_all[0:64, j, B * 128:(B + 1) * 128],
                        start=False, stop=True, skip_group_check=True), reads=tiles_q, inc=(hh == 1))
                P.op("act", lambda a: a.activation(pt.ap[:, :], sb_.ap[:, :], AF.Exp, scale=SCALE),
                     reads=[sb_], writes=[pt])

            def emit_pv(idx):
                B, hp = items[idx]
                b = B % nb
                c = B // nb
                pt = pT[idx % 3]
                if hp == 0:
                    st_["ob"] = pb[3 + cnt["o"] % 2]
                    cnt["o"] += 1
                ob = st_["ob"]
                tiles_v = [vt[B // 8]] + ([vt[(B - 1) // 8]] if b > 0 else [])
                for hh in range(2):
                    j = hp * 2 + hh
                    oreg = ob.ap[:, j * 65:(j + 1) * 65]
                    firstw = (hp == 0 and hh == 0)
                    if b > 0:
                        P.op("pe", lambda pe: pe.matmul(
                            oreg, pt.ap[:, hh * 256:hh * 256 + 128], v_sb[:, B - 1, j, :], start=firstw, stop=False,
                            skip_group_check=True),
                            reads=[pt, vones] + tiles_v, writes=[ob] if firstw else [], inc=False)
                    P.op("pe", lambda pe: pe.matmul(
                        oreg, pt.ap[:, hh * 256 + 128:hh * 256 + 256], v_sb[:, B, j, :],
                        start=(b == 0 and firstw), stop=True, skip_group_check=True),
                        reads=[pt, vones] + tiles_v, writes=[ob] if (firstw and b == 0) else [],
                        touch=[ob], inc=(hh == 1))
                if hp == 1:
                    stg, k = flush_o(ob, 260, None)
                    P.dma("sp", ogv[c, b * 128:(b + 1) * 128, :], stg.ap[:, 0:260], s_ost[k], reads=[stg])

            LA = 2
            for idx in range(len(items) + LA):
                if idx < len(items):
                    emit_score(idx)
                if idx - LA >= 0:
                    emit_pv(idx - LA)

        def attn_qtiles(g, causal, K, kT_fn, v_fn, kdeps_fn):
            ogv = og[g].rearrange("(t i p) (j e) -> t p i j e", i=4, p=128, j=4)
            items = []
            for T in range(8):
                nkb = (4 * T + 4) if causal else 2
                for j in range(4):
                    for KB in range(nkb):
                        items.append((T, j, KB, nkb))
            st_ = {}

            def emit_score(idx):
                T, j, KB, nkb = items[idx]
                sb_ = pb[idx % 3]
                pt = pT[idx % 3]
                jd = KB - 4 * T if causal else -1
                kdeps = kdeps_fn(KB)
                qdeps = [qkt[T // 2]] + ([augb] if (layer == 1 and causal) else [])
                if jd < 0:
                    c0 = 0
                    P.op("pe", lambda pe: pe.matmul(
                        sb_.ap[:, 0:512], kT_fn(j, KB), qk_all[0:K, j, T * 512:(T + 1) * 512],
                        start=True, stop=True), reads=kdeps + qdeps, writes=[sb_])
                else:
                    c0 = jd * 128
                    reg = sb_.ap[:, c0:c0 + 128]
                    P.op("pe", lambda pe: pe.matmul(reg, idb.ap[:, :], mk.ap[:, M_DIAG, :],
                                                     start=True, stop=False, skip_group_check=True),
                         reads=[idb, mk], writes=[sb_], inc=False)
                    P.op("pe", lambda pe: pe.matmul(
                        reg, kT_fn(j, KB), qk_all[0:K, j, T * 512 + c0:T * 512 + c0 + 128],
                        start=False, stop=True, skip_group_check=True), reads=kdeps + qdeps, inc=(jd == 3))
                    if jd < 3:
                        P.op("pe", lambda pe: pe.matmul(
                            sb_.ap[:, c0 + 128:512], kT_fn(j, KB),
                            qk_all[0:K, j, T * 512 + c0 + 128:(T + 1) * 512], start=False, stop=True,
                            skip_group_check=True))
                P.op("act", lambda a: a.activation(
                    pt.ap[:, c0:512], sb_.ap[:, c0:512], AF.Exp, scale=SCALE), reads=[sb_], writes=[pt])

            def emit_pv(idx):
                T, j, KB, nkb = items[idx]
                pt = pT[idx % 3]
                jd = KB - 4 * T if causal else -1
                kdeps = kdeps_fn(KB)
                if KB == 0:
                    st_["ob"] = pb[3 + cnt["o"] % 2]
                    cnt["o"] += 1
                    if j == 0:
                        st_["k"] = cnt["st"] % 2
                        cnt["st"] += 1
                ob = st_["ob"]
                stg = ost[st_["k"]]
                stv = stg.ap[:, :].rearrange("p (i j e) -> p i j e", i=4, j=4)
                i0_ = max(jd, 0)
                for i in range(i0_, 4):
                    last_kb = (4 * T + i) if causal else 1
                    P.op("pe", lambda pe: pe.matmul(
                        ob.ap[:, i * 65:(i + 1) * 65], pt.ap[:, i * 128:(i + 1) * 128], v_fn(j, KB),
                        start=(KB == 0 and i == 0), stop=(KB == last_kb), skip_group_check=True),
                        reads=[pt] + kdeps, writes=[ob] if (KB == 0 and i == 0) else [], touch=[ob], inc=(i == 3))
                if KB == nkb - 1:
                    P.op("dve", lambda v: v.tensor_copy(
                        stv[:, :, j, :], ob.ap[:, 0:260].rearrange("p (i e) -> p i e", i=4)),
                        reads=[ob], writes=[stg])
                    if j == 3:
                        P.dma("sp", ogv[T], stv, s_ost[st_["k"]], reads=[stg])

            LA = 2
            for idx in range(len(items) + LA):
                if idx < len(items):
                    emit_score(idx)
                if idx - LA >= 0:
                    emit_pv(idx - LA)

        def mem_kv():
            mv = mem.rearrange("(s p) d -> s p d", p=128)
            for i in range(2):
                xprep(mv[i], memT[i])
            for hp in range(2):
                bk = pb[hp]
                for hh in range(2):
                    j = hp * 2 + hh
                    for c in range(8):
                        P.op("pe", lambda pe, c=c, j=j, hh=hh, bk=bk: pe.matmul(
                            bk.ap[0:64, hh * 256:(hh + 1) * 256], wkv.ap[:, c, j * 64:(j + 1) * 64],
                            memT_t[:, c, :], start=(c == 0 and hh == 0), stop=(c == 7), skip_group_check=True),
                            reads=[wkv] + memT if c == 0 else [], writes=[bk] if (c == 0 and hh == 0) else [],
                            inc=(c == 7 and hh == 1))
                P.op("act", lambda a, bk=bk, hp=hp: a.copy(
                    kmT.ap[:, 2 * hp:2 * hp + 2, :], bk.ap[0:64, :].rearrange("p (j n) -> p j n", j=2)),
                    reads=[bk], writes=[kmT])
            for mb in range(2):
                bk = pb[2 + mb]
                for c in range(8):
                    P.op("pe", lambda pe, c=c, mb=mb, bk=bk: pe.matmul(
                        bk.ap[:, 0:256], memT_t[:, c, mb * 128:(mb + 1) * 128], wkv.ap[:, c, 256:512],
                        start=(c == 0), stop=(c == 7)),
                        reads=[wkv] + memT if c == 0 else [], writes=[bk] if c == 0 else [], inc=(c == 7))
                P.op("act", lambda a, bk=bk, mb=mb: a.copy(
                    vm_sb.ap[:, mb, :, 0:64], bk.ap[:, 0:256].rearrange("p (j d) -> p j d", j=4)),
                    reads=[bk], writes=[vm_sb])

        upto = DEBUG_UPTO
        if upto >= 1:
            mem_kv()
        for g in range(4):
            if upto < 2 or (upto in (2, 3) and g > 0):
                break
            proj_group(g)
            if upto == 2:
                break
            if layer == 1 and g == 0:
                forget_prep()
            if g < 3:
                if layer == 0:
                    attn_banded(g)
                else:
                    load_aug(g)
                    attn_qtiles(g, True, 70,
                                lambda j, KB: qk_all[0:70, 4 + j, KB * 128:(KB + 1) * 128],
                                lambda j, KB: v_sb[:, KB, j, :],
                                lambda KB: [qkt[KB // 8], vt[KB // 8], vones, augb])
            else:
                attn_qtiles(g, False, 64,
                            lambda j, KB: kmT.ap[:, j, KB * 128:(KB + 1) * 128],
                            lambda j, KB: vm_sb.ap[:, KB, j, :],
                            lambda KB: [kmT, vm_sb])
        P.end_stage()


def stage_out(P, layer, x_src, x_dst, og, w_out, gain, bias, ident_f32):
    P.begin_stage()
    nch = 4 if layer == 0 else 8
    NB = 4
    NPB = 3
    with ExitStack() as es:
        C = Ctx(P, es)
        pb = [Buf(C.ps(f"pb{i}", [128, 512], F32)) for i in range(8)]
        s_const = P.new_dma_sem()
        idf, idb = make_ident(P, C, ident_f32, s_const)
        epi = LNEpi(P, C, gain, bias, s_const, depth=NB)
        wo = Buf(C.sb("wo", [128, nch, D], BF16))
        s_wo = P.new_dma_sem(sw=True)
        P.dma("pool", wo.ap[:, :, :], w_out.rearrange("(c p) n -> p c n", p=128), s_wo, writes=[wo])
        ogt = [[Buf(C.sb(f"ogt{k}_{i}", [128, 260], F32)) for i in range(4)] for k in range(NB)]
        s_og = [P.new_dma_sem() for _ in range(NB)]
        rec = [[Buf(C.sb(f"rec{k}_{i}", [128, 4], F32)) for i in range(4)] for k in range(NB)]
        cat = [Buf(C.sb(f"cat{k}", [128, nch * 128], BF16)) for k in range(NB)]
        catT = [Buf(C.sb(f"catT{k}", [128, nch, 128], BF16)) for k in range(NB)]
        xsv = x_src.rearrange("(u p) d -> u p d", p=128)
        xdv = x_dst.rearrange("(u p) d -> u p d", p=128)
        ogv = [o.rearrange("(u p) f -> u p f", p=128) for o in og]
        NU = S // 128

        def loads(u):
            k = u % NB
            epi.load_x(k, xsv[u])
            for i in range(4):
                P.dma("sp", ogt[k][i].ap[:, :], ogv[i][u], s_og[k], writes=[ogt[k][i]])
            for i in range(4):
                ogt[k][i].w = (s_og[k].key, s_og[k].count)
                ogt[k][i].r = {}

        st_banks = {}

        def stage_a(u):
            k = u % NB
            if layer == 0:
                a0, a1, a2 = ogt[k][0], ogt[k][1], ogt[k][2]
                P.op("dve", lambda g_: g_.tensor_tensor(a0.ap[:, :], a0.ap[:, :], a1.ap[:, :], ALU.add),
                     reads=[a0, a1], writes=[a0])
                P.op("dve", lambda g_: g_.tensor_tensor(a0.ap[:, :], a0.ap[:, :], a2.ap[:, :], ALU.add),
                     reads=[a0, a2], writes=[a0])
                parts = [(ogt[k][0], 0), (ogt[k][3], 256)]
            else:
                parts = [(ogt[k][i], 256 * i) for i in range(4)]
            ct = cat[k]
            for pi, (src, col) in enumerate(parts):
                rc = rec[k][pi]
                sv = src.ap[:, :].rearrange("p (j e) -> p j e", j=4)
                P.op("dve", lambda v: v.reciprocal(rc.ap[:, :], sv[:, :, 64]), reads=[src], writes=[rc])
                P.op("dve", lambda v: v.tensor_tensor(
                    ct.ap[:, col:col + 256].rearrange("p (j d) -> p j d", j=4), sv[:, :, 0:64],
                    rc.ap[:, :].unsqueeze(2).broadcast_to([128, 4, 64]), ALU.mult),
                    reads=[src, rc], writes=[ct] if pi == 0 else [], touch=[ct])
            tb = pb[6 + u % 2]
            tbv = tb.ap.bitcast(BF16)
            for ch in range(nch):
                P.op("pe", lambda pe, ch=ch: pe.transpose(
                    tbv[:, ch * 128:(ch + 1) * 128], ct.ap[:, ch * 128:(ch + 1) * 128], idb.ap[:, :]),
                    reads=[ct, idb], writes=[tb] if ch == 0 else [], inc=(ch == nch - 1))
            cT = catT[k]
            P.op("act", lambda a: a.copy(cT.ap[:, :, :], tbv[:, 0:nch * 128].rearrange("p (c n) -> p c n", c=nch)),
                 reads=[tb], writes=[cT])
            kb = u % NPB
            banks = (pb[2 * kb], pb[2 * kb + 1])
            st_banks[u] = banks
            for n in range(2):
                bank = banks[n]
                for ch in range(nch):
                    P.op("pe", lambda pe, bank=bank, ch=ch, n=n: pe.matmul(
                        bank.ap[:, :], cT.ap[:, ch, :], wo.ap[:, ch, n * 512:(n + 1) * 512],
                        start=(ch == 0), stop=(ch == nch - 1)),
                        reads=[cT, wo] if ch == 0 else [], writes=[bank] if ch == 0 else [], inc=(ch == nch - 1))

        for u in range(min(NB, NU)):
            loads(u)
        for step in range(NU + 3):
            ua, u1, u2 = step, step - 1, step - 2
            if 0 <= u2 < NU:
                epi.e2(u2 % NB, xdv[u2])
                if u2 + NB < NU:
                    loads(u2 + NB)
            if 0 <= u1 < NU:
                epi.e1a(u1 % NB)
            if ua < NU:
                stage_a(ua)
            if 0 <= u1 < NU:
                epi.e1b(u1 % NB, st_banks[u1], 1.0)
        P.end_stage()


def host_constants():
    ident = np.eye(128, dtype=np.float32)
    NEG = np.float32(-1.0e5)
    k = np.arange(128)[:, None]
    q = np.arange(128)[None, :]
    m_diag = np.where(k <= q, 0.0, NEG).astype(np.float32)
    m_prev = np.where(k >= q, 0.0, NEG).astype(np.float32)
    m_all = np.full((128, 128), NEG, np.float32)
    masks = np.concatenate([m_diag, m_prev, m_all], axis=1)
    pos = np.arange(S, dtype=np.float32)
    inv_freq = (1.0 / (np.float32(500000.0) ** (np.arange(8, dtype=np.float32) / np.float32(8)))).astype(np.float32)
    ang = (pos[:, None] * inv_freq[None, :]).astype(np.float32)
    cos = np.cos(ang).astype(np.float32)
    sin = np.sin(ang).astype(np.float32)
    tabs = np.zeros((3, S, 128), np.float32)
    for g, r in enumerate((1, 4, 16)):
        L = S // r
        m = np.arange(S)
        tok = (m // L) + r * (m % L)
        tabs[g, :, 0:64] = np.tile(cos[tok], (1, 8))
        tabs[g, :, 64:128] = np.tile(sin[tok], (1, 8))
    return ident, masks, tabs


def build_program(stages=None):
    nc = bass.Bass("TRN2", target_bir_lowering=False)

    def din(name, shape):
        return nc.dram_tensor(name, list(shape), F32, kind="ExternalInput").ap()

    x = din("x", [S, D])
    mem = din("mem", [MEML, D])
    f1gu = din("ffn1_w_gate_up", [DEPTH, D, 2 * DFF])
    f1d = din("ffn1_w_down", [DEPTH, DFF, D])
    f2gu = din("ffn2_w_gate_up", [DEPTH, D, 2 * DFF])
    f2d = din("ffn2_w_down", [DEPTH, DFF, D])
    lng = din("ln_gain", [DEPTH, 3, D])
    lnb = din("ln_bias", [DEPTH, 3, D])
    wkv = din("mem_w_kv", [DEPTH, D, 2 * MEMW])
    awin = din("a_w_in", [1, D, A_IN_W])
    awout = din("a_w_out", [1, 4 * HD + MEMW, D])
    bwin = din("b_w_in", [1, D, B_IN_W])
    bfb = din("b_forget_bias", [1, NMIX])
    bwout = din("b_w_out", [1, MIXW + MEMW, D])
    ident = din("c_ident", [128, 128])
    masks = din("c_masks", [128, 384])
    tabs = din("c_rope", [3, S, 128])
    out = nc.dram_tensor("out", [S, D], F32, kind="ExternalOutput").ap()
    xa = nc.dram_tensor("scr_xa", [S, D], F32, kind="Internal").ap()
    xb = nc.dram_tensor("scr_xb", [S, D], F32, kind="Internal").ap()
    og = [nc.dram_tensor(f"scr_og{i}", [S, 260], F32, kind="Internal").ap() for i in range(4)]
    cq = nc.dram_tensor("scr_cq", [NMIX, 6, S], BF16, kind="Internal").ap()
    ck = nc.dram_tensor("scr_ck", [NMIX, 6, S], BF16, kind="Internal").ap()
    consts = (ident, tabs, masks)
    with ExitStack() as es:
        P = Prog(nc, es)
        allst = [
            lambda: stage_ffn(P, x, xa, f1gu[0], f1d[0], lng[0, 0], lnb[0, 0], ident),
            lambda: stage_mix(P, 0, xa, awin[0], mem, wkv[0], og, consts),
            lambda: stage_out(P, 0, xa, xb, og, awout[0], lng[0, 1], lnb[0, 1], ident),
            lambda: stage_ffn(P, xb, xa, f2gu[0], f2d[0], lng[0, 2], lnb[0, 2], ident),
            lambda: stage_ffn(P, xa, xb, f1gu[1], f1d[1], lng[1, 0], lnb[1, 0], ident),
            lambda: stage_mix(P, 1, xb, bwin[0], mem, wkv[1], og, consts, fbias=bfb[0], cq=cq, ck=ck),
            lambda: stage_out(P, 1, xb, xa, og, bwout[0], lng[1, 1], lnb[1, 1], ident),
            lambda: stage_ffn(P, xa, out, f2gu[1], f2d[1], lng[1, 2], lnb[1, 2], ident),
        ]
        for i, st in enumerate(allst):
            if stages is None or i in stages:
                st()
    return nc


def kernel(x, mem, ffn1_w_gate_up, ffn1_w_down, ffn2_w_gate_up, ffn2_w_down, ln_gain, ln_bias, mem_w_kv,
           a_w_in, a_w_out, b_w_in, b_forget_bias, b_w_out):
    ncores = 8
    ident, masks, tabs = host_constants()
    f32 = lambda a: np.ascontiguousarray(np.asarray(a, dtype=np.float32))
    shared = {
        "ffn1_w_gate_up": f32(ffn1_w_gate_up), "ffn1_w_down": f32(ffn1_w_down),
        "ffn2_w_gate_up": f32(ffn2_w_gate_up), "ffn2_w_down": f32(ffn2_w_down),
        "ln_gain": f32(ln_gain), "ln_bias": f32(ln_bias), "mem_w_kv": f32(mem_w_kv),
        "a_w_in": f32(a_w_in), "a_w_out": f32(a_w_out), "b_w_in": f32(b_w_in),
        "b_forget_bias": f32(b_forget_bias), "b_w_out": f32(b_w_out),
        "c_ident": ident, "c_masks": masks, "c_rope": tabs,
    }
    xs = f32(x)
    ms = f32(mem)
    in_maps = []
    for b in range(ncores):
        d = dict(shared)
        d["x"] = xs[b]
        d["mem"] = ms[b]
        in_maps.append(d)
    nc = build_program()
    res = run_bass_kernel_spmd(nc, in_maps, core_ids=list(range(ncores)))
    return np.stack([np.asarray(r["out"], dtype=np.float32) for r in res.results], axis=0)
```

```python
import sys
import numpy as np
from contextlib import ExitStack

import concourse.bass as bass
import concourse.mybir as mybir
from concourse.bass_utils import run_bass_kernel_spmd

F32 = mybir.dt.float32
BF16 = mybir.dt.bfloat16
AF = mybir.ActivationFunctionType
ALU = mybir.AluOpType

D = 1024
S = 4096
DFF = 2816
NFC = DFF // 128
DEPTH = 2
ALPHA = float((2 * DEPTH) ** 0.25)
LN_EPS = 1e-5
HD = 64
NMIX = 12
NMEMH = 4
MEML = 256
MIXW = 768
MEMW = 256
A_IN_W = 3 * MIXW + MEMW
B_IN_W = 3 * MIXW + NMIX + MEMW
SCALE = HD ** -0.5
TT = 1024
NT = S // TT

ENGS = ("pe", "act", "dve", "pool", "sp")
DEBUG_LINES = False
DEBUG_UPTO = 99
DEBUG_PROJ = 0
LINEMAP = {}


class Buf:
    def __init__(self, ap, name=""):
        self.ap = ap
        self.name = name
        self.w = None
        self.r = {}

    def __getitem__(self, k):
        return self.ap[k]


class DmaSem:
    def __init__(self, key, handle):
        self.key = key
        self.h = handle
        self.count = 0


class _Rec:
    def __init__(self):
        self.call = None

    def __getattr__(self, name):
        def f(*a, **kw):
            assert self.call is None
            self.call = (name, a, kw)
            return self
        return f


class Prog:
    def __init__(self, nc, es, n_dma_sems=18, n_stage_sets=8, n_sw_sems=26):
        self.nc = nc
        self.sem = {}
        self.q = {e: [] for e in ENGS}
        self.cnt = {e: 0 for e in ENGS}
        self.waited = {e: {} for e in ENGS}
        self.pend = {e: ([], []) for e in ENGS}
        self.free_eng_sets = []
        for i in range(n_stage_sets):
            st = {}
            for e in ENGS:
                st[e] = es.enter_context(nc.semaphore(f"s_{e}_{i}"))
            self.free_eng_sets.append(st)
        self.dma_sems = []
        for i in range(n_dma_sems):
            k = f"dma{i}"
            h = es.enter_context(nc.semaphore(f"s_{k}"))
            self.sem[k] = h
            self.dma_sems.append(DmaSem(k, h))
        self.sw_sems = []
        for i in range(n_sw_sems):
            k = f"swdma{i}"
            h = es.enter_context(nc.semaphore(f"s_{k}"))
            self.sem[k] = h
            self.sw_sems.append(DmaSem(k, h))
        self.sw_next = 0
        self.stage_sw = []
        self.dma_next = 0
        self.stage_id = -1

    def begin_stage(self):
        self.stage_id += 1
        st = self.free_eng_sets[self.stage_id]
        for e in ENGS:
            self.sem[e] = st[e]
        self.q = {e: [] for e in ENGS}
        self.cnt = {e: 0 for e in ENGS}
        self.waited = {e: {} for e in ENGS}
        self.pend = {e: ([], []) for e in ENGS}
        self.dma_next = 0
        self.stage_sw = []

    def new_dma_sem(self, sw=False):
        if sw:
            s = self.sw_sems[self.sw_next]
            self.sw_next += 1
            self.stage_sw.append(s)
            return s
        s = self.dma_sems[self.dma_next]
        self.dma_next += 1
        return s

    def _eng(self, e):
        nc = self.nc
        return {"pe": nc.tensor, "act": nc.scalar, "dve": nc.vector, "pool": nc.gpsimd, "sp": nc.sync}[e]

    def _wait(self, e, k, v):
        if self.waited[e].get(k, 0) >= v:
            return
        self.waited[e][k] = v
        h = self.sem[k]
        self.q[e].append(lambda eng, h=h, v=v: eng.wait_ge(h, v))

    def _deps(self, e, reads, writes):
        for b in reads:
            if b.w is not None:
                self._wait(e, *b.w)
        for b in writes:
            if b.w is not None:
                self._wait(e, *b.w)
            for t in b.r.values():
                self._wait(e, *t)

    def op(self, e, fn, reads=(), writes=(), inc=True, touch=()):
        self._deps(e, reads, writes)
        pr, pw = self.pend[e]
        pr.extend(reads)
        pw.extend(writes)
        pw.extend(touch)
        rec = _Rec()
        fn(rec)
        name, a, kw = rec.call
        ln = sys._getframe(1).f_lineno if DEBUG_LINES else 0

        def emit(eng, name=name, a=a, kw=kw, ln=ln):
            i = getattr(eng, name)(*a, **kw)
            if DEBUG_LINES:
                LINEMAP[i.ins.name] = ln
            return i
        if not inc:
            self.q[e].append(emit)
            return None
        self.cnt[e] += 1
        tok = (e, self.cnt[e])
        h = self.sem[e]
        self.q[e].append(lambda eng, emit=emit, h=h: emit(eng).then_inc(h, 1))
        for b in pr:
            b.r[e] = tok
        for b in pw:
            b.w = tok
            b.r = {}
        self.pend[e] = ([], [])
        return tok

    def dma(self, e, out_ap, in_ap, sem, reads=(), writes=()):
        assert not self.pend[e][0] and not self.pend[e][1]
        assert (e == "pool") == sem.key.startswith("swdma"), (e, sem.key)
        self._deps(e, reads, writes)
        sem.count += 16
        tok = (sem.key, sem.count)
        h = sem.h
        self.q[e].append(lambda eng, o=out_ap, i=in_ap, h=h: eng.dma_start(out=o, in_=i).then_inc(h, 16))
        for b in reads:
            b.r[sem.key] = tok
        for b in writes:
            b.w = tok
            b.r = {}
        return tok

    def wait_tok(self, e, tok):
        self._wait(e, *tok)

    def end_stage(self, final_toks=()):
        for e in ENGS:
            assert not self.pend[e][0] and not self.pend[e][1], e
        for e in ENGS:
            for k in ENGS:
                if k != e and self.cnt[k] > 0:
                    self._wait(e, k, self.cnt[k])
        for t in final_toks:
            self._wait("sp", *t)
        for ds in self.dma_sems[:self.dma_next] + self.stage_sw:
            if ds.count > 0:
                self._wait("sp", ds.key, ds.count)
        with self.nc.Block() as block:
            for e, reg in (("pe", block.tensor), ("act", block.scalar), ("dve", block.vector),
                           ("pool", block.gpsimd), ("sp", block.sync)):
                lst = self.q[e]

                def body(eng, lst=lst):
                    for f in lst:
                        f(eng)
                reg(body)


class Ctx:
    def __init__(self, P, es):
        self.P = P
        self.nc = P.nc
        self.es = es

    def sb(self, name, shape, dt):
        t = self.es.enter_context(self.nc.sbuf_tensor(f"{name}_{self.P.stage_id}", list(shape), dt))
        return t

    def ps(self, name, shape, dt=F32):
        t = self.es.enter_context(self.nc.psum_tensor(f"{name}_{self.P.stage_id}", list(shape), dt))
        return t


def make_ident(P, C, ident_f32, sem):
    idf = Buf(C.sb("idf", [128, 128], F32))
    idb = Buf(C.sb("idb", [128, 128], BF16))
    P.dma("sp", idf.ap[:, :], ident_f32, P.new_dma_sem(), writes=[idf])
    P.op("dve", lambda v: v.tensor_copy(idb.ap[:, :], idf.ap[:, :]), reads=[idf], writes=[idb])
    return idf, idb


class XPrep:
    def __init__(self, P, C, idb, banks):
        self.P = P
        self.xst = [Buf(C.sb(f"xst{i}", [128, D], F32)) for i in range(2)]
        self.xbf = [Buf(C.sb(f"xbf{i}", [128, D], BF16)) for i in range(2)]
        self.sem = [P.new_dma_sem() for _ in range(2)]
        self.idb = idb
        self.banks = banks
        self.n = 0

    def __call__(self, row_ap, dst):
        self.part2(self.part1(row_ap), dst)

    def part1(self, row_ap):
        P = self.P
        k = self.n % 2
        bank = self.banks[self.n % len(self.banks)]
        self.n += 1
        xst, xbf = self.xst[k], self.xbf[k]
        P.dma("sp", xst.ap[:, :], row_ap, self.sem[k], writes=[xst])
        P.op("act", lambda a: a.copy(xbf.ap[:, :], xst.ap[:, :]), reads=[xst], writes=[xbf])
        return (k, bank)

    def part2(self, h, dst):
        P = self.P
        k, bank = h
        xbf, idb = self.xbf[k], self.idb
        pv = bank.ap.bitcast(BF16)
        for c in range(8):
            P.op("pe", lambda pe, c=c: pe.transpose(
                pv[:, c * 128:(c + 1) * 128], xbf.ap[:, c * 128:(c + 1) * 128], idb.ap[:, :]),
                reads=[xbf, idb], writes=[bank] if c == 0 else [], inc=(c == 7))
        P.op("dve", lambda v: v.tensor_copy(dst.ap, pv[:, :].rearrange("p (c n) -> p c n", c=8)),
             reads=[bank], writes=[dst])


class LNEpi:
    def __init__(self, P, C, gain, bias, sem_const, depth=2):
        self.P = P
        self.depth = depth
        self.xres = [Buf(C.sb(f"xres{i}", [128, D], F32)) for i in range(depth)]
        self.zb = [Buf(C.sb(f"z{i}", [128, D], F32)) for i in range(depth)]
        self.ob = [Buf(C.sb(f"ob{i}", [128, D], F32)) for i in range(depth)]
        self.st = [Buf(C.sb(f"st{i}", [128, 16], F32)) for i in range(depth)]
        self.gbc = Buf(C.sb("gbc", [128, D], F32))
        self.bbc = Buf(C.sb("bbc", [128, D], F32))
        self.s_x = [P.new_dma_sem() for _ in range(depth)]
        self.s_o = [P.new_dma_sem() for _ in range(depth)]
        P.dma("sp", self.gbc.ap[:, :], gain.partition_broadcast(128), P.new_dma_sem(), writes=[self.gbc])
        P.dma("sp", self.bbc.ap[:, :], bias.partition_broadcast(128), P.new_dma_sem(), writes=[self.bbc])

    def load_x(self, k, row_ap):
        self.P.dma("sp", self.xres[k].ap[:, :], row_ap, self.s_x[k], writes=[self.xres[k]])

    def __call__(self, k, banks, dst_ap, yscale):
        self.e1(k, banks, yscale)
        self.e2(k, dst_ap)

    def e1(self, k, banks, yscale):
        self.e1a(k)
        self.e1b(k, banks, yscale)

    def e1a(self, k):
        xres = self.xres[k]
        self.P.op("act", lambda a: a.mul(xres.ap[:, :], xres.ap[:, :], ALPHA), reads=[xres], writes=[xres])

    def e1b(self, k, banks, yscale):
        P = self.P
        xres, z, o, sb_ = self.xres[k], self.zb[k], self.ob[k], self.st[k]
        for n in range(2):
            P.op("dve", lambda v, n=n: v.scalar_tensor_tensor(
                z.ap[:, n * 512:(n + 1) * 512], banks[n].ap[:, :], float(yscale),
                xres.ap[:, n * 512:(n + 1) * 512], ALU.mult, ALU.add),
                reads=[banks[n], xres], writes=[z] if n == 0 else [], inc=(n == 1))
        for n in range(2):
            P.op("dve", lambda v, n=n: v.bn_stats(sb_.ap[:, n * 6:(n + 1) * 6], z.ap[:, n * 512:(n + 1) * 512]),
                 reads=[z], writes=[sb_] if n == 0 else [], inc=(n == 1))
        P.op("dve", lambda v: v.bn_aggr(sb_.ap[:, 12:14], sb_.ap[:, 0:12]), reads=[sb_], writes=[sb_])
        P.op("act", lambda a: a.activation(sb_.ap[:, 14:15], sb_.ap[:, 13:14], AF.Sqrt, bias=LN_EPS, scale=1.0),
             reads=[sb_], writes=[sb_])

    def e2(self, k, dst_ap):
        P = self.P
        xres, z, o, sb_ = self.xres[k], self.zb[k], self.ob[k], self.st[k]
        gbc, bbc = self.gbc, self.bbc
        P.op("dve", lambda v: v.reciprocal(sb_.ap[:, 14:15], sb_.ap[:, 14:15]), reads=[sb_], writes=[sb_])
        P.op("dve", lambda v: v.scalar_tensor_tensor(
            sb_.ap[:, 15:16], sb_.ap[:, 12:13], -1.0, sb_.ap[:, 14:15], ALU.mult, ALU.mult),
            reads=[sb_], writes=[sb_])
        P.op("act", lambda a: a.activation(z.ap[:, :], z.ap[:, :], AF.Identity, bias=sb_.ap[:, 15:16],
                                           scale=sb_.ap[:, 14:15]), reads=[z, sb_], writes=[z])
        P.op("pool", lambda g: g.tensor_tensor(o.ap[:, :], z.ap[:, :], gbc.ap[:, :], ALU.mult),
             reads=[z, gbc], writes=[o])
        P.op("pool", lambda g: g.tensor_tensor(o.ap[:, :], o.ap[:, :], bbc.ap[:, :], ALU.add),
             reads=[o, bbc], writes=[o])
        P.dma("sp", dst_ap, o.ap[:, :], self.s_o[k], reads=[o])


def stage_ffn(P, x_src, x_dst, w_gu, w_down, gain, bias, ident_f32, ntiles=NT):
    P.begin_stage()
    NWG = 3
    with ExitStack() as es:
        C = Ctx(P, es)
        wd = C.sb("wd", [128, NFC, D], BF16)
        wg = [Buf(C.sb(f"wg{i}", [128, 8, 512], BF16)) for i in range(NWG)]
        xT_t = [C.sb(f"xT{i}", [128, 8, TT], BF16) for i in range(2)]
        gT_t = C.sb("gT", [128, NFC, TT], BF16)
        sg = [Buf(C.sb(f"sg{i}", [128, 512], F32)) for i in range(2)]
        pg = [Buf(C.ps(f"pg{i}", [128, 512], F32)) for i in range(4)]
        pd = [Buf(C.ps(f"pd{i}", [128, 512], F32)) for i in range(4)]
        wd_b = [Buf(wd[:, j, :]) for j in range(NFC)]
        xT = [[Buf(xT_t[i][:, :, s * 128:(s + 1) * 128]) for s in range(8)] for i in range(2)]
        gT = [[Buf(gT_t[:, j, h * 512:(h + 1) * 512]) for h in range(2)] for j in range(NFC)]

        s_const = P.new_dma_sem()
        s_wd = P.new_dma_sem(sw=True)
        s_wg = [P.new_dma_sem(sw=True) for _ in range(NWG)]
        idf, idb = make_ident(P, C, ident_f32, s_const)
        xprep = XPrep(P, C, idb, pd)
        epi = LNEpi(P, C, gain, bias, s_const)

        wguv = w_gu.rearrange("(c p) n -> p c n", p=128)

        def load_wg(step):
            j2 = step % (NFC // 2)
            slot = step % NWG
            b = wg[slot]
            P.dma("pool", b.ap[:, :, 0:256], wguv[:, :, 256 * j2:256 * j2 + 256], s_wg[slot], writes=[b])
            P.dma("pool", b.ap[:, :, 256:512], wguv[:, :, DFF + 256 * j2:DFF + 256 * j2 + 256], s_wg[slot])
            b.w = (s_wg[slot].key, s_wg[slot].count)

        nsteps = ntiles * (NFC // 2)
        xsv = x_src.rearrange("(t s p) d -> t s p d", s=8, p=128)
        xdv = x_dst.rearrange("(t s p) d -> t s p d", s=8, p=128)

        for stp in range(min(NWG, nsteps)):
            load_wg(stp)
        wdv = w_down.rearrange("(c p) n -> p c n", p=128)
        for j in range(NFC):
            P.dma("pool", wd[:, j, :], wdv[:, j, :], s_wd, writes=[wd_b[j]])
        for j in range(NFC):
            wd_b[j].w = (s_wd.key, s_wd.count)
        h0 = xprep.part1(xsv[0, 0])
        for s in range(8):
            h1 = xprep.part1(xsv[0, s + 1]) if s + 1 < 8 else None
            xprep.part2(h0, xT[0][s])
            h0 = h1

        gstep = 0
        grp = 0
        xh = {}
        pending_e2 = None
        for t in range(ntiles):
            xTt = xT[t % 2]
            for j2 in range(NFC // 2):
                if j2 == 1 and pending_e2 is not None:
                    epi.e2(*pending_e2)
                    pending_e2 = None
                slot = gstep % NWG
                wb = wg[slot]
                for jj in range(2):
                    j = 2 * j2 + jj
                    for h in range(2):
                        bg = pg[2 * (grp % 2)]
                        bu = pg[2 * (grp % 2) + 1]
                        sgb = sg[grp % 2]
                        grp += 1
                        rd = [wb] + [xTt[4 * h + q] for q in range(4)]
                        for (bank, off) in ((bg, 0), (bu, 256)):
                            for c in range(8):
                                P.op("pe", lambda pe, bank=bank, c=c, off=off, jj=jj, h=h, wb=wb, t=t: pe.matmul(
                                    bank.ap[:, :], wb.ap[:, c, off + jj * 128:off + jj * 128 + 128],
                                    xT_t[t % 2][:, c, h * 512:(h + 1) * 512], start=(c == 0), stop=(c == 7)),
                                    reads=rd if c == 0 else [], writes=[bank] if c == 0 else [], inc=(c == 7))
                        P.op("act", lambda a, sgb=sgb, bg=bg: a.activation(sgb.ap[:, :], bg.ap[:, :], AF.Silu),
                             reads=[bg], writes=[sgb])
                        dst = gT[j][h]
                        P.op("dve", lambda v, dst=dst, sgb=sgb, bu=bu: v.tensor_tensor(
                            dst.ap, sgb.ap[:, :], bu.ap[:, :], ALU.mult), reads=[sgb, bu], writes=[dst])
                gstep += 1
                if gstep + NWG - 1 < nsteps:
                    load_wg(gstep + NWG - 1)
                if t + 1 < ntiles:
                    if 2 <= j2 <= 9:
                        xprep.part2(xh.pop(j2 - 2), xT[(t + 1) % 2][j2 - 2])
                    if 1 <= j2 <= 8:
                        xh[j2 - 1] = xprep.part1(xsv[t + 1, j2 - 1])
            epi.load_x(0, xsv[t, 0])
            for s in range(8):
                k = s % 2
                if s + 1 < 8:
                    epi.load_x((s + 1) % 2, xsv[t, s + 1])
                banks = (pd[2 * k], pd[2 * k + 1])
                for n in range(2):
                    bank = banks[n]
                    for j in range(NFC):
                        P.op("pe", lambda pe, bank=bank, j=j, s=s, n=n: pe.matmul(
                            bank.ap[:, :], gT_t[:, j, s * 128:(s + 1) * 128], wd[:, j, n * 512:(n + 1) * 512],
                            start=(j == 0), stop=(j == NFC - 1)),
                            reads=[gT[j][s // 4], wd_b[j]], writes=[bank] if j == 0 else [], inc=(j == NFC - 1))
                epi.e1(k, banks, 0.5)
                if s >= 1:
                    epi.e2((s - 1) % 2, xdv[t, s - 1])
            pending_e2 = (1, xdv[t, 7])
        epi.e2(*pending_e2)
        P.end_stage()


def stage_mix(P, layer, x_src, w_in, mem, w_kv, og, consts, fbias=None, cq=None, ck=None):
    ident_f32, rope_tabs, masks_f32 = consts
    P.begin_stage()
    dil = (1, 4, 16) if layer == 0 else (1, 1, 1)
    qmem_off = 3 * MIXW if layer == 0 else 3 * MIXW + NMIX
    with ExitStack() as es:
        C = Ctx(P, es)
        pb = [Buf(C.ps(f"pb{i}", [128, 512], F32)) for i in range(8)]
        s_const = P.new_dma_sem()
        idf, idb = make_ident(P, C, ident_f32, s_const)
        xprep = XPrep(P, C, idb, [pb[4], pb[5]])
        xT_t = [C.sb(f"xT{i}", [128, 8, TT], BF16) for i in range(2)]
        xT = [[Buf(xT_t[i][:, :, s * 128:(s + 1) * 128]) for s in range(8)] for i in range(2)]
        wgt = [Buf(C.sb("wgt0", [128, 8, 768], BF16))] * 2
        s_wgt = [P.new_dma_sem(sw=True)] * 2
        qkbf = [Buf(C.sb(f"qkbf{i}", [128, 512], BF16)) for i in range(2)]
        qk_all = C.sb("qkall", [70, 8, S], BF16)
        qkt = [Buf(qk_all[0:64, :, t * TT:(t + 1) * TT]) for t in range(NT)]
        v_sb = C.sb("vsb", [128, 32, 4, 65], BF16)
        vt = [Buf(v_sb[:, 8 * t:8 * t + 8, :, :]) for t in range(NT)]
        vones = Buf(v_sb[:, :, :, 64:65])
        pT = [Buf(C.sb(f"pT{i}", [128, 512], BF16)) for i in range(3)]
        ost = [Buf(C.sb(f"ost{i}", [128, 1040], F32)) for i in range(2)]
        s_ost = [P.new_dma_sem() for _ in range(2)]
        mkf = Buf(C.sb("mkf", [128, 384], F32))
        mk = Buf(C.sb("mk", [128, 3, 128], BF16))
        memT_t = C.sb("memT", [128, 8, MEML], BF16)
        memT = [Buf(memT_t[:, :, i * 128:(i + 1) * 128]) for i in range(2)]
        wkv = Buf(C.sb("wkv", [128, 8, 512], BF16))
        kmT = Buf(C.sb("kmT", [64, 4, MEML], BF16))
        vm_sb = Buf(C.sb("vmsb", [128, 2, 4, 65], BF16))
        s_wkv = P.new_dma_sem(sw=True)
        if layer == 0:
            tabg = Buf(C.sb("tabg", [128, 32, 128], F32))
            s_tab = P.new_dma_sem()
            rtmp = [[Buf(C.sb(f"rt{i}_{q}", [128, 64], F32)) for q in range(4)] for i in range(2)]
            a32b = [Buf(C.sb(f"a32_{i}", [128, 512], F32)) for i in range(2)]
        else:
            wf = Buf(C.sb("wf", [128, 8, NMIX], BF16))
            s_wf = P.new_dma_sem(sw=True)
            fst = [Buf(C.sb(f"fst{i}", [128, NMIX], F32)) for i in range(2)]
            fT = Buf(C.sb("fT", [NMIX, S], F32))
            augb = Buf(qk_all[64:70, :, :])
            s_aug = P.new_dma_sem()
            cq_b = Buf(cq)
            ck_b = Buf(ck)

        P.dma("sp", mkf.ap[:, :], masks_f32, s_const, writes=[mkf])
        P.op("dve", lambda v: v.tensor_copy(mk.ap[:, :, :], mkf.ap[:, :].rearrange("p (a b) -> p a b", a=3)),
             reads=[mkf], writes=[mk])
        M_DIAG, M_PREV, M_ALL = 0, 1, 2
        P.op("pool", lambda g: g.memset(v_sb[:, :, :, :].rearrange("p u j e -> p (u j) e")[:, :, 64:65], 1.0),
             writes=[vones])
        P.op("pool", lambda g: g.memset(vm_sb.ap[:, :, :, :].rearrange("p u j e -> p (u j) e")[:, :, 64:65], 1.0),
             writes=[vm_sb])

        w_inv = w_in.rearrange("(c p) n -> p c n", p=128)

        def load_wgt(g):
            b = wgt[g % 2]
            if g < 3:
                for i in range(3):
                    P.dma("pool", b.ap[:, :, i * 256:(i + 1) * 256],
                          w_inv[:, :, i * MIXW + g * 256:i * MIXW + (g + 1) * 256], s_wgt[g % 2],
                          writes=[b] if i == 0 else [])
            else:
                P.dma("pool", b.ap[:, :, 0:256], w_inv[:, :, qmem_off:qmem_off + 256], s_wgt[g % 2], writes=[b])
            b.w = (s_wgt[g % 2].key, s_wgt[g % 2].count)

        load_wgt(0)
        if layer == 1:
            P.dma("pool", wf.ap[:, :, :], w_inv[:, :, 3 * MIXW:3 * MIXW + NMIX], s_wf, writes=[wf])
        P.dma("pool", wkv.ap[:, :, :], w_kv.rearrange("(c p) n -> p c n", p=128), s_wkv, writes=[wkv])

        def proj_mm(g, u, xTbuf_t, xTb, s, nslots, with_v, with_f, rope_ap):
            wb = wgt[g % 2]
            A = pb[u % 2]
            Bk = pb[2 + u % 2]
            ncol = nslots * 64
            for c in range(8):
                P.op("pe", lambda pe, c=c: pe.matmul(
                    A.ap[:, 0:ncol], xTbuf_t[:, c, s * 128:(s + 1) * 128], wb.ap[:, c, 0:ncol],
                    start=(c == 0), stop=(c == 7)),
                    reads=[xTb, wb] if c == 0 else [], writes=[A] if c == 0 else [], inc=(c == 7))
            if with_v:
                for c in range(8):
                    P.op("pe", lambda pe, c=c: pe.matmul(
                        Bk.ap[:, 0:256], xTbuf_t[:, c, s * 128:(s + 1) * 128], wb.ap[:, c, 512:768],
                        start=(c == 0), stop=(c == 7), skip_group_check=True),
                        reads=[xTb, wb] if c == 0 else [], writes=[Bk] if c == 0 else [],
                        inc=(c == 7 and not with_f))
            if with_f:
                for c in range(8):
                    P.op("pe", lambda pe, c=c: pe.matmul(
                        Bk.ap[:, 256:256 + NMIX], xTbuf_t[:, c, s * 128:(s + 1) * 128], wf.ap[:, c, :],
                        start=False, stop=(c == 7), skip_group_check=True), reads=[wf] if c == 0 else [], inc=(c == 7))

        def proj_fin(g, u, xTbuf_t, xTb, s, nslots, with_v, with_f, rope_ap):
            wb = wgt[g % 2]
            A = pb[u % 2]
            Bk = pb[2 + u % 2]
            ncol = nslots * 64
            qb = qkbf[u % 2]
            if rope_ap is not None:
                tb = tabg
                a32 = a32b[u % 2]
                P.op("act", lambda a: a.copy(a32.ap[:, :], A.ap[:, :]), reads=[A], writes=[a32])
                Av = a32.ap[:, :].rearrange("p (j d) -> p j d", j=8)
                qv = qb.ap[:, :].rearrange("p (j d) -> p j d", j=8)
                cosv = tb.ap[:, u, 0:64].rearrange("p (j d) -> p j d", j=8)
                sinv = tb.ap[:, u, 64:128].rearrange("p (j d) -> p j d", j=8)
                rt = rtmp[u % 2]
                rv = [r_.ap[:, :].rearrange("p (j d) -> p j d", j=8) for r_ in rt]
                P.op("pool", lambda g_: g_.tensor_copy(qv[:, :, 16:64], Av[:, :, 16:64]), reads=[a32], writes=[qb])
                P.op("dve", lambda v: v.tensor_tensor(rv[0], Av[:, :, 0:8], cosv, ALU.mult),
                     reads=[a32, tb], writes=[rt[0]], inc=False)
                P.op("dve", lambda v: v.tensor_tensor(rv[1], Av[:, :, 8:16], sinv, ALU.mult),
                     reads=[a32, tb], writes=[rt[1]], inc=False)
                P.op("dve", lambda v: v.tensor_tensor(rv[2], Av[:, :, 8:16], cosv, ALU.mult),
                     reads=[a32, tb], writes=[rt[2]], inc=False)
                P.op("dve", lambda v: v.tensor_tensor(rv[3], Av[:, :, 0:8], sinv, ALU.mult),
                     reads=[a32, tb], writes=[rt[3]])
                P.op("pool", lambda g_: g_.tensor_tensor(qv[:, :, 0:8], rv[0], rv[1], ALU.subtract),
                     reads=[rt[0], rt[1]], writes=[qb], inc=False)
                P.op("pool", lambda g_: g_.tensor_tensor(qv[:, :, 8:16], rv[2], rv[3], ALU.add),
                     reads=[rt[2], rt[3]], writes=[qb])
            else:
                P.op("act", lambda a: a.copy(qb.ap[:, 0:ncol], A.ap[:, 0:ncol]), reads=[A], writes=[qb])
            if with_v:
                P.op("act", lambda a: a.copy(v_sb[:, u, :, 0:64], Bk.ap[:, 0:256].rearrange("p (j d) -> p j d", j=4)),
                     reads=[Bk], writes=[vt[u // 8]])
            if with_f:
                fs = fst[u % 2]
                P.op("act", lambda a: a.copy(fs.ap[:, :], Bk.ap[:, 256:256 + NMIX]), reads=[Bk], writes=[fs])
                P.op("pe", lambda pe: pe.transpose(Bk.ap[0:NMIX, 384:512], fs.ap[:, :], idf.ap[:, :]),
                     reads=[fs, idf], writes=[Bk])
                P.op("dve", lambda v: v.tensor_copy(fT.ap[:, u * 128:(u + 1) * 128], Bk.ap[0:NMIX, 384:512]),
                     reads=[Bk], writes=[fT])
            tq = pb[6 + u % 2]
            tqv = tq.ap.bitcast(BF16)
            for sl in range(nslots):
                P.op("pe", lambda pe, sl=sl: pe.transpose(
                    tqv[0:64, sl * 128:(sl + 1) * 128], qb.ap[:, sl * 64:(sl + 1) * 64], idb.ap[:, :]),
                    reads=[qb, idb], writes=[tq] if sl == 0 else [], inc=(sl == nslots - 1))
            P.op("dve", lambda v: v.tensor_copy(
                qk_all[0:64, 0:nslots, u * 128:(u + 1) * 128],
                tqv[0:64, 0:nslots * 128].rearrange("p (j n) -> p j n", j=nslots)),
                reads=[tq], writes=[qkt[u // 8]])

        def proj_group(g):
            r = dil[g] if g < 3 else 1
            L = S // r
            nslots = 8 if g < 3 else 4
            xr = x_src.rearrange("(n r) d -> r n d", r=r)
            if layer == 0 and g < 3:
                P.dma("sp", tabg.ap[:, :, :], rope_tabs[g].rearrange("(u p) f -> p u f", p=128), s_tab, writes=[tabg])

            def rows(u):
                m0 = u * 128
                return xr[m0 // L, (m0 % L):(m0 % L) + 128, :]

            if g == 0:
                for s in range(8):
                    xprep(rows(s), xT[0][s])
            rope_ap = True if (layer == 0 and g < 3) else None
            xh = {}

            def args(u):
                t, s = u // 8, u % 8
                return (g, u, xT_t[t % 2], xT[t % 2][s], s, nslots, g < 3, (layer == 1 and g == 0), rope_ap)

            proj_mm(*args(0))
            for u in range(32):
                if u + 7 in xh:
                    v = u + 7
                    xprep.part2(xh.pop(v), xT[(v // 8) % 2][v % 8])
                if u + 8 < 32:
                    xh[u + 8] = xprep.part1(rows(u + 8))
                if u + 1 < 32:
                    proj_mm(*args(u + 1))
                proj_fin(*args(u))
            if g + 1 < 4:
                load_wgt(g + 1)
                r2 = dil[g + 1] if g + 1 < 3 else 1
                xr2 = x_src.rearrange("(n r) d -> r n d", r=r2)
                L2 = S // r2
                for s in range(8):
                    m0 = s * 128
                    xprep(xr2[m0 // L2, (m0 % L2):(m0 % L2) + 128, :], xT[0][s])

        def forget_prep():
            CH = 512
            nb_ = Buf(C.sb("negb", [NMIX, 1], F32))
            ones = Buf(C.sb("ones12", [NMIX, CH], F32))
            onesb = Buf(C.sb("ones12b", [NMIX, CH], BF16))
            e1 = Buf(C.sb("e1", [NMIX, CH], F32))
            cc = [Buf(C.sb(f"cc{i}", [NMIX, CH], F32)) for i in range(2)]
            c8 = Buf(C.sb("c8", [NMIX, CH], F32))
            pc = [Buf(C.sb(f"pc{i}", [NMIX, CH], BF16)) for i in range(3)]
            npc = [Buf(C.sb(f"npc{i}", [NMIX, CH], BF16)) for i in range(3)]
            s_c = P.new_dma_sem()
            s_cs = P.new_dma_sem()
            P.dma("sp", nb_.ap[:, :], fbias.rearrange("(h o) -> h o", o=1), s_c, writes=[nb_])
            P.op("act", lambda a: a.mul(nb_.ap[:, :], nb_.ap[:, :], -1.0), reads=[nb_], writes=[nb_])
            P.op("pool", lambda g_: g_.memset(ones.ap[:, :], 1.0), writes=[ones])
            P.op("pool", lambda g_: g_.memset(onesb.ap[:, :], 1.0), writes=[onesb])
            for ci in range(S // CH):
                sl = slice(ci * CH, (ci + 1) * CH)
                P.op("act", lambda a, sl=sl: a.activation(e1.ap[:, :], fT.ap[:, sl], AF.Exp, bias=nb_.ap[:, 0:1],
                                                          scale=-1.0), reads=[fT, nb_], writes=[e1])
                P.op("act", lambda a: a.activation(e1.ap[:, :], e1.ap[:, :], AF.Ln, bias=1.0, scale=1.0),
                     reads=[e1], writes=[e1])
                cur, prev = cc[ci % 2], cc[(ci + 1) % 2]
                init = 0.0 if ci == 0 else prev.ap[:, CH - 1:CH]
                P.op("dve", lambda v, cur=cur, init=init: v.tensor_tensor_scan(
                    cur.ap[:, :], ones.ap[:, :], e1.ap[:, :], init, ALU.mult, ALU.subtract),
                    reads=[ones, e1, prev], writes=[cur])
                P.op("act", lambda a, cur=cur: a.mul(c8.ap[:, :], cur.ap[:, :], 1.0 / SCALE), reads=[cur], writes=[c8])
                for i in range(3):
                    P.op("dve", lambda v, i=i: v.tensor_copy(pc[i].ap[:, :], c8.ap[:, :]), reads=[c8], writes=[pc[i]])
                    if i < 2:
                        P.op("dve", lambda v, i=i: v.tensor_tensor(c8.ap[:, :], c8.ap[:, :], pc[i].ap[:, :],
                                                                    ALU.subtract), reads=[c8, pc[i]], writes=[c8])
                for i in range(3):
                    P.op("act", lambda a, i=i: a.mul(npc[i].ap[:, :], pc[i].ap[:, :], -1.0),
                         reads=[pc[i]], writes=[npc[i]])
                for i in range(3):
                    P.dma("sp", cq[:, i, sl], pc[i].ap[:, :], s_cs, reads=[pc[i]], writes=[cq_b] if i == 0 else [])
                    P.dma("sp", ck[:, 3 + i, sl], npc[i].ap[:, :], s_cs, reads=[npc[i]], writes=[ck_b] if i == 0 else [])
                    P.dma("sp", cq[:, 3 + i, sl], onesb.ap[:, :], s_cs, reads=[onesb])
                    P.dma("sp", ck[:, i, sl], onesb.ap[:, :], s_cs, reads=[onesb])
                cq_b.w = (s_cs.key, s_cs.count)
                ck_b.w = (s_cs.key, s_cs.count)
                for b_ in pc + npc + [onesb]:
                    b_.r[s_cs.key] = (s_cs.key, s_cs.count)

        def load_aug(g):
            for j in range(4):
                P.dma("sp", qk_all[64:70, j, :], cq[4 * g + j], s_aug, reads=[cq_b], writes=[augb] if j == 0 else [])
                P.dma("sp", qk_all[64:70, 4 + j, :], ck[4 * g + j], s_aug, reads=[ck_b])
            augb.w = (s_aug.key, s_aug.count)

        cnt = {"s": 0, "o": 0, "st": 0}

        def flush_o(ob, ncol, dst_ap, view=None):
            k = cnt["st"] % 2
            cnt["st"] += 1
            stg = ost[k]
            P.op("dve", lambda v: v.tensor_copy(stg.ap[:, 0:ncol], ob.ap[:, 0:ncol]), reads=[ob], writes=[stg])
            return stg, k

        def attn_banded(g):
            r = dil[g]
            nb = 32 // r
            ogv = og[g].rearrange("(n r) f -> r n f", r=r)
            items = [(B, hp) for B in range(32) for hp in range(2)]
            st_ = {}

            def emit_score(idx):
                B, hp = items[idx]
                b = B % nb
                sb_ = pb[idx % 3]
                pt = pT[idx % 3]
                tiles_q = [qkt[B // 8]] + ([qkt[(B - 1) // 8]] if b > 0 else [])
                for hh in range(2):
                    j = hp * 2 + hh
                    reg = sb_.ap[:, hh * 256:hh * 256 + 128]
                    first = (hh == 0)
                    if b > 0:
                        P.op("pe", lambda pe: pe.matmul(reg, idb.ap[:, :], mk.ap[:, M_PREV, :],
                                                         start=first, stop=False, skip_group_check=True),
                             reads=[idb, mk], writes=[sb_] if first else [], inc=False)
                        P.op("pe", lambda pe: pe.matmul(
                            reg, qk_all[0:64, 4 + j, (B - 1) * 128:B * 128], qk_all[0:64, j, B * 128:(B + 1) * 128],
                            start=False, stop=True, skip_group_check=True), reads=tiles_q, inc=False)
                    else:
                        P.op("pe", lambda pe: pe.matmul(reg, idb.ap[:, :], mk.ap[:, M_ALL, :],
                                                         start=first, stop=True, skip_group_check=True),
                             reads=[idb, mk], writes=[sb_] if first else [], inc=False)
                    reg2 = sb_.ap[:, hh * 256 + 128:hh * 256 + 256]
                    P.op("pe", lambda pe: pe.matmul(reg2, idb.ap[:, :], mk.ap[:, M_DIAG, :],
                                                     start=False, stop=False, skip_group_check=True), inc=False)
                    P.op("pe", lambda pe: pe.matmul(
                        reg2, qk_all[0:64, 4 + j, B * 128:(B + 1) * 128], qk_all[0:64, j, B * 128:(B + 1) * 128],
                        start=False, stop=True, skip_group_check=True), reads=tiles_q, inc=(hh == 1))
                P.op("act", lambda a: a.activation(pt.ap[:, :], sb_.ap[:, :], AF.Exp, scale=SCALE),
                     reads=[sb_], writes=[pt])

            def emit_pv(idx):
                B, hp = items[idx]
                b = B % nb
                c = B // nb
                pt = pT[idx % 3]
                if hp == 0:
                    st_["ob"] = pb[3 + cnt["o"] % 2]
                    cnt["o"] += 1
                ob = st_["ob"]
                tiles_v = [vt[B // 8]] + ([vt[(B - 1) // 8]] if b > 0 else [])
                for hh in range(2):
                    j = hp * 2 + hh
                    oreg = ob.ap[:, j * 65:(j + 1) * 65]
                    firstw = (hp == 0 and hh == 0)
                    if b > 0:
                        P.op("pe", lambda pe: pe.matmul(
                            oreg, pt.ap[:, hh * 256:hh * 256 + 128], v_sb[:, B - 1, j, :], start=firstw, stop=False,
                            skip_group_check=True),
                            reads=[pt, vones] + tiles_v, writes=[ob] if firstw else [], inc=False)
                    P.op("pe", lambda pe: pe.matmul(
                        oreg, pt.ap[:, hh * 256 + 128:hh * 256 + 256], v_sb[:, B, j, :],
                        start=(b == 0 and firstw), stop=True, skip_group_check=True),
                        reads=[pt, vones] + tiles_v, writes=[ob] if (firstw and b == 0) else [],
                        touch=[ob], inc=(hh == 1))
                if hp == 1:
                    stg, k = flush_o(ob, 260, None)
                    P.dma("sp", ogv[c, b * 128:(b + 1) * 128, :], stg.ap[:, 0:260], s_ost[k], reads=[stg])

            LA = 2
            for idx in range(len(items) + LA):
                if idx < len(items):
                    emit_score(idx)
                if idx - LA >= 0:
                    emit_pv(idx - LA)

        def attn_qtiles(g, causal, K, kT_fn, v_fn, kdeps_fn):
            ogv = og[g].rearrange("(t i p) (j e) -> t p i j e", i=4, p=128, j=4)
            items = []
            for T in range(8):
                nkb = (4 * T + 4) if causal else 2
                for j in range(4):
                    for KB in range(nkb):
                        items.append((T, j, KB, nkb))
            st_ = {}

            def emit_score(idx):
                T, j, KB, nkb = items[idx]
                sb_ = pb[idx % 3]
                pt = pT[idx % 3]
                jd = KB - 4 * T if causal else -1
                kdeps = kdeps_fn(KB)
                qdeps = [qkt[T // 2]] + ([augb] if (layer == 1 and causal) else [])
                if jd < 0:
                    c0 = 0
                    P.op("pe", lambda pe: pe.matmul(
                        sb_.ap[:, 0:512], kT_fn(j, KB), qk_all[0:K, j, T * 512:(T + 1) * 512],
                        start=True, stop=True), reads=kdeps + qdeps, writes=[sb_])
                else:
                    c0 = jd * 128
                    reg = sb_.ap[:, c0:c0 + 128]
                    P.op("pe", lambda pe: pe.matmul(reg, idb.ap[:, :], mk.ap[:, M_DIAG, :],
                                                     start=True, stop=False, skip_group_check=True),
                         reads=[idb, mk], writes=[sb_], inc=False)
                    P.op("pe", lambda pe: pe.matmul(
                        reg, kT_fn(j, KB), qk_all[0:K, j, T * 512 + c0:T * 512 + c0 + 128],
                        start=False, stop=True, skip_group_check=True), reads=kdeps + qdeps, inc=(jd == 3))
                    if jd < 3:
                        P.op("pe", lambda pe: pe.matmul(
                            sb_.ap[:, c0 + 128:512], kT_fn(j, KB),
                            qk_all[0:K, j, T * 512 + c0 + 128:(T + 1) * 512], start=False, stop=True,
                            skip_group_check=True))
                P.op("act", lambda a: a.activation(
                    pt.ap[:, c0:512], sb_.ap[:, c0:512], AF.Exp, scale=SCALE), reads=[sb_], writes=[pt])

            def emit_pv(idx):
                T, j, KB, nkb = items[idx]
                pt = pT[idx % 3]
                jd = KB - 4 * T if causal else -1
                kdeps = kdeps_fn(KB)
                if KB == 0:
                    st_["ob"] = pb[3 + cnt["o"] % 2]
                    cnt["o"] += 1
                    if j == 0:
                        st_["k"] = cnt["st"] % 2
                        cnt["st"] += 1
                ob = st_["ob"]
                stg = ost[st_["k"]]
                stv = stg.ap[:, :].rearrange("p (i j e) -> p i j e", i=4, j=4)
                i0_ = max(jd, 0)
                for i in range(i0_, 4):
                    last_kb = (4 * T + i) if causal else 1
                    P.op("pe", lambda pe: pe.matmul(
                        ob.ap[:, i * 65:(i + 1) * 65], pt.ap[:, i * 128:(i + 1) * 128], v_fn(j, KB),
                        start=(KB == 0 and i == 0), stop=(KB == last_kb), skip_group_check=True),
                        reads=[pt] + kdeps, writes=[ob] if (KB == 0 and i == 0) else [], touch=[ob], inc=(i == 3))
                if KB == nkb - 1:
                    P.op("dve", lambda v: v.tensor_copy(
                        stv[:, :, j, :], ob.ap[:, 0:260].rearrange("p (i e) -> p i e", i=4)),
                        reads=[ob], writes=[stg])
                    if j == 3:
                        P.dma("sp", ogv[T], stv, s_ost[st_["k"]], reads=[stg])

            LA = 2
            for idx in range(len(items) + LA):
                if idx < len(items):
                    emit_score(idx)
                if idx - LA >= 0:
                    emit_pv(idx - LA)

        def mem_kv():
            mv = mem.rearrange("(s p) d -> s p d", p=128)
            for i in range(2):
                xprep(mv[i], memT[i])
            for hp in range(2):
                bk = pb[hp]
                for hh in range(2):
                    j = hp * 2 + hh
                    for c in range(8):
                        P.op("pe", lambda pe, c=c, j=j, hh=hh, bk=bk: pe.matmul(
                            bk.ap[0:64, hh * 256:(hh + 1) * 256], wkv.ap[:, c, j * 64:(j + 1) * 64],
                            memT_t[:, c, :], start=(c == 0 and hh == 0), stop=(c == 7), skip_group_check=True),
                            reads=[wkv] + memT if c == 0 else [], writes=[bk] if (c == 0 and hh == 0) else [],
                            inc=(c == 7 and hh == 1))
                P.op("act", lambda a, bk=bk, hp=hp: a.copy(
                    kmT.ap[:, 2 * hp:2 * hp + 2, :], bk.ap[0:64, :].rearrange("p (j n) -> p j n", j=2)),
                    reads=[bk], writes=[kmT])
            for mb in range(2):
                bk = pb[2 + mb]
                for c in range(8):
                    P.op("pe", lambda pe, c=c, mb=mb, bk=bk: pe.matmul(
                        bk.ap[:, 0:256], memT_t[:, c, mb * 128:(mb + 1) * 128], wkv.ap[:, c, 256:512],
                        start=(c == 0), stop=(c == 7)),
                        reads=[wkv] + memT if c == 0 else [], writes=[bk] if c == 0 else [], inc=(c == 7))
                P.op("act", lambda a, bk=bk, mb=mb: a.copy(
                    vm_sb.ap[:, mb, :, 0:64], bk.ap[:, 0:256].rearrange("p (j d) -> p j d", j=4)),
                    reads=[bk], writes=[vm_sb])

        upto = DEBUG_UPTO
        if upto >= 1:
            mem_kv()
        for g in range(4):
            if upto < 2 or (upto in (2, 3) and g > 0):
                break
            proj_group(g)
            if upto == 2:
                break
            if layer == 1 and g == 0:
                forget_prep()
            if g < 3:
                if layer == 0:
                    attn_banded(g)
                else:
                    load_aug(g)
                    attn_qtiles(g, True, 70,
                                lambda j, KB: qk_all[0:70, 4 + j, KB * 128:(KB + 1) * 128],
                                lambda j, KB: v_sb[:, KB, j, :],
                                lambda KB: [qkt[KB // 8], vt[KB // 8], vones, augb])
            else:
                attn_qtiles(g, False, 64,
                            lambda j, KB: kmT.ap[:, j, KB * 128:(KB + 1) * 128],
                            lambda j, KB: vm_sb.ap[:, KB, j, :],
                            lambda KB: [kmT, vm_sb])
        P.end_stage()


def stage_out(P, layer, x_src, x_dst, og, w_out, gain, bias, ident_f32):
    P.begin_stage()
    nch = 4 if layer == 0 else 8
    NB = 4
    NPB = 3
    with ExitStack() as es:
        C = Ctx(P, es)
        pb = [Buf(C.ps(f"pb{i}", [128, 512], F32)) for i in range(8)]
        s_const = P.new_dma_sem()
        idf, idb = make_ident(P, C, ident_f32, s_const)
        epi = LNEpi(P, C, gain, bias, s_const, depth=NB)
        wo = Buf(C.sb("wo", [128, nch, D], BF16))
        s_wo = P.new_dma_sem(sw=True)
        P.dma("pool", wo.ap[:, :, :], w_out.rearrange("(c p) n -> p c n", p=128), s_wo, writes=[wo])
        ogt = [[Buf(C.sb(f"ogt{k}_{i}", [128, 260], F32)) for i in range(4)] for k in range(NB)]
        s_og = [P.new_dma_sem() for _ in range(NB)]
        rec = [[Buf(C.sb(f"rec{k}_{i}", [128, 4], F32)) for i in range(4)] for k in range(NB)]
        cat = [Buf(C.sb(f"cat{k}", [128, nch * 128], BF16)) for k in range(NB)]
        catT = [Buf(C.sb(f"catT{k}", [128, nch, 128], BF16)) for k in range(NB)]
        xsv = x_src.rearrange("(u p) d -> u p d", p=128)
        xdv = x_dst.rearrange("(u p) d -> u p d", p=128)
        ogv = [o.rearrange("(u p) f -> u p f", p=128) for o in og]
        NU = S // 128

        def loads(u):
            k = u % NB
            epi.load_x(k, xsv[u])
            for i in range(4):
                P.dma("sp", ogt[k][i].ap[:, :], ogv[i][u], s_og[k], writes=[ogt[k][i]])
            for i in range(4):
                ogt[k][i].w = (s_og[k].key, s_og[k].count)
                ogt[k][i].r = {}

        st_banks = {}

        def stage_a(u):
            k = u % NB
            if layer == 0:
                a0, a1, a2 = ogt[k][0], ogt[k][1], ogt[k][2]
                P.op("dve", lambda g_: g_.tensor_tensor(a0.ap[:, :], a0.ap[:, :], a1.ap[:, :], ALU.add),
                     reads=[a0, a1], writes=[a0])
                P.op("dve", lambda g_: g_.tensor_tensor(a0.ap[:, :], a0.ap[:, :], a2.ap[:, :], ALU.add),
                     reads=[a0, a2], writes=[a0])
                parts = [(ogt[k][0], 0), (ogt[k][3], 256)]
            else:
                parts = [(ogt[k][i], 256 * i) for i in range(4)]
            ct = cat[k]
            for pi, (src, col) in enumerate(parts):
                rc = rec[k][pi]
                sv = src.ap[:, :].rearrange("p (j e) -> p j e", j=4)
                P.op("dve", lambda v: v.reciprocal(rc.ap[:, :], sv[:, :, 64]), reads=[src], writes=[rc])
                P.op("dve", lambda v: v.tensor_tensor(
                    ct.ap[:, col:col + 256].rearrange("p (j d) -> p j d", j=4), sv[:, :, 0:64],
                    rc.ap[:, :].unsqueeze(2).broadcast_to([128, 4, 64]), ALU.mult),
                    reads=[src, rc], writes=[ct] if pi == 0 else [], touch=[ct])
            tb = pb[6 + u % 2]
            tbv = tb.ap.bitcast(BF16)
            for ch in range(nch):
                P.op("pe", lambda pe, ch=ch: pe.transpose(
                    tbv[:, ch * 128:(ch + 1) * 128], ct.ap[:, ch * 128:(ch + 1) * 128], idb.ap[:, :]),
                    reads=[ct, idb], writes=[tb] if ch == 0 else [], inc=(ch == nch - 1))
            cT = catT[k]
            P.op("act", lambda a: a.copy(cT.ap[:, :, :], tbv[:, 0:nch * 128].rearrange("p (c n) -> p c n", c=nch)),
                 reads=[tb], writes=[cT])
            kb = u % NPB
            banks = (pb[2 * kb], pb[2 * kb + 1])
            st_banks[u] = banks
            for n in range(2):
                bank = banks[n]
                for ch in range(nch):
                    P.op("pe", lambda pe, bank=bank, ch=ch, n=n: pe.matmul(
                        bank.ap[:, :], cT.ap[:, ch, :], wo.ap[:, ch, n * 512:(n + 1) * 512],
                        start=(ch == 0), stop=(ch == nch - 1)),
                        reads=[cT, wo] if ch == 0 else [], writes=[bank] if ch == 0 else [], inc=(ch == nch - 1))

        for u in range(min(NB, NU)):
            loads(u)
        for step in range(NU + 3):
            ua, u1, u2 = step, step - 1, step - 2
            if 0 <= u2 < NU:
                epi.e2(u2 % NB, xdv[u2])
                if u2 + NB < NU:
                    loads(u2 + NB)
            if 0 <= u1 < NU:
                epi.e1a(u1 % NB)
            if ua < NU:
                stage_a(ua)
            if 0 <= u1 < NU:
                epi.e1b(u1 % NB, st_banks[u1], 1.0)
        P.end_stage()


def host_constants():
    ident = np.eye(128, dtype=np.float32)
    NEG = np.float32(-1.0e5)
    k = np.arange(128)[:, None]
    q = np.arange(128)[None, :]
    m_diag = np.where(k <= q, 0.0, NEG).astype(np.float32)
    m_prev = np.where(k >= q, 0.0, NEG).astype(np.float32)
    m_all = np.full((128, 128), NEG, np.float32)
    masks = np.concatenate([m_diag, m_prev, m_all], axis=1)
    pos = np.arange(S, dtype=np.float32)
    inv_freq = (1.0 / (np.float32(500000.0) ** (np.arange(8, dtype=np.float32) / np.float32(8)))).astype(np.float32)
    ang = (pos[:, None] * inv_freq[None, :]).astype(np.float32)
    cos = np.cos(ang).astype(np.float32)
    sin = np.sin(ang).astype(np.float32)
    tabs = np.zeros((3, S, 128), np.float32)
    for g, r in enumerate((1, 4, 16)):
        L = S // r
        m = np.arange(S)
        tok = (m // L) + r * (m % L)
        tabs[g, :, 0:64] = np.tile(cos[tok], (1, 8))
        tabs[g, :, 64:128] = np.tile(sin[tok], (1, 8))
    return ident, masks, tabs


def build_program(stages=None):
    nc = bass.Bass("TRN2", target_bir_lowering=False)

    def din(name, shape):
        return nc.dram_tensor(name, list(shape), F32, kind="ExternalInput").ap()

    x = din("x", [S, D])
    mem = din("mem", [MEML, D])
    f1gu = din("ffn1_w_gate_up", [DEPTH, D, 2 * DFF])
    f1d = din("ffn1_w_down", [DEPTH, DFF, D])
    f2gu = din("ffn2_w_gate_up", [DEPTH, D, 2 * DFF])
    f2d = din("ffn2_w_down", [DEPTH, DFF, D])
    lng = din("ln_gain", [DEPTH, 3, D])
    lnb = din("ln_bias", [DEPTH, 3, D])
    wkv = din("mem_w_kv", [DEPTH, D, 2 * MEMW])
    awin = din("a_w_in", [1, D, A_IN_W])
    awout = din("a_w_out", [1, 4 * HD + MEMW, D])
    bwin = din("b_w_in", [1, D, B_IN_W])
    bfb = din("b_forget_bias", [1, NMIX])
    bwout = din("b_w_out", [1, MIXW + MEMW, D])
    ident = din("c_ident", [128, 128])
    masks = din("c_masks", [128, 384])
    tabs = din("c_rope", [3, S, 128])
    out = nc.dram_tensor("out", [S, D], F32, kind="ExternalOutput").ap()
    xa = nc.dram_tensor("scr_xa", [S, D], F32, kind="Internal").ap()
    xb = nc.dram_tensor("scr_xb", [S, D], F32, kind="Internal").ap()
    og = [nc.dram_tensor(f"scr_og{i}", [S, 260], F32, kind="Internal").ap() for i in range(4)]
    cq = nc.dram_tensor("scr_cq", [NMIX, 6, S], BF16, kind="Internal").ap()
    ck = nc.dram_tensor("scr_ck", [NMIX, 6, S], BF16, kind="Internal").ap()
    consts = (ident, tabs, masks)
    with ExitStack() as es:
        P = Prog(nc, es)
        allst = [
            lambda: stage_ffn(P, x, xa, f1gu[0], f1d[0], lng[0, 0], lnb[0, 0], ident),
            lambda: stage_mix(P, 0, xa, awin[0], mem, wkv[0], og, consts),
            lambda: stage_out(P, 0, xa, xb, og, awout[0], lng[0, 1], lnb[0, 1], ident),
            lambda: stage_ffn(P, xb, xa, f2gu[0], f2d[0], lng[0, 2], lnb[0, 2], ident),
            lambda: stage_ffn(P, xa, xb, f1gu[1], f1d[1], lng[1, 0], lnb[1, 0], ident),
            lambda: stage_mix(P, 1, xb, bwin[0], mem, wkv[1], og, consts, fbias=bfb[0], cq=cq, ck=ck),
            lambda: stage_out(P, 1, xb, xa, og, bwout[0], lng[1, 1], lnb[1, 1], ident),
            lambda: stage_ffn(P, xa, out, f2gu[1], f2d[1], lng[1, 2], lnb[1, 2], ident),
        ]
        for i, st in enumerate(allst):
            if stages is None or i in stages:
                st()
    return nc


def kernel(x, mem, ffn1_w_gate_up, ffn1_w_down, ffn2_w_gate_up, ffn2_w_down, ln_gain, ln_bias, mem_w_kv,
           a_w_in, a_w_out, b_w_in, b_forget_bias, b_w_out):
    ncores = 8
    ident, masks, tabs = host_constants()
    f32 = lambda a: np.ascontiguousarray(np.asarray(a, dtype=np.float32))
    shared = {
        "ffn1_w_gate_up": f32(ffn1_w_gate_up), "ffn1_w_down": f32(ffn1_w_down),
        "ffn2_w_gate_up": f32(ffn2_w_gate_up), "ffn2_w_down": f32(ffn2_w_down),
        "ln_gain": f32(ln_gain), "ln_bias": f32(ln_bias), "mem_w_kv": f32(mem_w_kv),
        "a_w_in": f32(a_w_in), "a_w_out": f32(a_w_out), "b_w_in": f32(b_w_in),
        "b_forget_bias": f32(b_forget_bias), "b_w_out": f32(b_w_out),
        "c_ident": ident, "c_masks": masks, "c_rope": tabs,
    }
    xs = f32(x)
    ms = f32(mem)
    in_maps = []
    for b in range(ncores):
        d = dict(shared)
        d["x"] = xs[b]
        d["mem"] = ms[b]
        in_maps.append(d)
    nc = build_program()
    res = run_bass_kernel_spmd(nc, in_maps, core_ids=list(range(ncores)))
    return np.stack([np.asarray(r["out"], dtype=np.float32) for r in res.results], axis=0)
```

```python
import sys
import numpy as np
from contextlib import ExitStack

import concourse.bass as bass
import concourse.mybir as mybir
from concourse.bass_utils import run_bass_kernel_spmd

F32 = mybir.dt.float32
BF16 = mybir.dt.bfloat16
AF = mybir.ActivationFunctionType
ALU = mybir.AluOpType

D = 1024
S = 4096
DFF = 2816
NFC = DFF // 128
DEPTH = 2
ALPHA = float((2 * DEPTH) ** 0.25)
LN_EPS = 1e-5
HD = 64
NMIX = 12
NMEMH = 4
MEML = 256
MIXW = 768
MEMW = 256
A_IN_W = 3 * MIXW + MEMW
B_IN_W = 3 * MIXW + NMIX + MEMW
SCALE = HD ** -0.5
TT = 1024
NT = S // TT

ENGS = ("pe", "act", "dve", "pool", "sp")
DEBUG_LINES = False
DEBUG_UPTO = 99
DEBUG_PROJ = 0
LINEMAP = {}


class Buf:
    def __init__(self, ap, name=""):
        self.ap = ap
        self.name = name
        self.w = None
        self.r = {}

    def __getitem__(self, k):
        return self.ap[k]


class DmaSem:
    def __init__(self, key, handle):
        self.key = key
        self.h = handle
        self.count = 0


class _Rec:
    def __init__(self):
        self.call = None

    def __getattr__(self, name):
        def f(*a, **kw):
            assert self.call is None
            self.call = (name, a, kw)
            return self
        return f


class Prog:
    def __init__(self, nc, es, n_dma_sems=18, n_stage_sets=8, n_sw_sems=26):
        self.nc = nc
        self.sem = {}
        self.q = {e: [] for e in ENGS}
        self.cnt = {e: 0 for e in ENGS}
        self.waited = {e: {} for e in ENGS}
        self.pend = {e: ([], []) for e in ENGS}
        self.free_eng_sets = []
        for i in range(n_stage_sets):
            st = {}
            for e in ENGS:
                st[e] = es.enter_context(nc.semaphore(f"s_{e}_{i}"))
            self.free_eng_sets.append(st)
        self.dma_sems = []
        for i in range(n_dma_sems):
            k = f"dma{i}"
            h = es.enter_context(nc.semaphore(f"s_{k}"))
            self.sem[k] = h
            self.dma_sems.append(DmaSem(k, h))
        self.sw_sems = []
        for i in range(n_sw_sems):
            k = f"swdma{i}"
            h = es.enter_context(nc.semaphore(f"s_{k}"))
            self.sem[k] = h
            self.sw_sems.append(DmaSem(k, h))
        self.sw_next = 0
        self.stage_sw = []
        self.dma_next = 0
        self.stage_id = -1

    def begin_stage(self):
        self.stage_id += 1
        st = self.free_eng_sets[self.stage_id]
        for e in ENGS:
            self.sem[e] = st[e]
        self.q = {e: [] for e in ENGS}
        self.cnt = {e: 0 for e in ENGS}
        self.waited = {e: {} for e in ENGS}
        self.pend = {e: ([], []) for e in ENGS}
        self.dma_next = 0
        self.stage_sw = []

    def new_dma_sem(self, sw=False):
        if sw:
            s = self.sw_sems[self.sw_next]
            self.sw_next += 1
            self.stage_sw.append(s)
            return s
        s = self.dma_sems[self.dma_next]
        self.dma_next += 1
        return s

    def _eng(self, e):
        nc = self.nc
        return {"pe": nc.tensor, "act": nc.scalar, "dve": nc.vector, "pool": nc.gpsimd, "sp": nc.sync}[e]

    def _wait(self, e, k, v):
        if self.waited[e].get(k, 0) >= v:
            return
        self.waited[e][k] = v
        h = self.sem[k]
        self.q[e].append(lambda eng, h=h, v=v: eng.wait_ge(h, v))

    def _deps(self, e, reads, writes):
        for b in reads:
            if b.w is not None:
                self._wait(e, *b.w)
        for b in writes:
            if b.w is not None:
                self._wait(e, *b.w)
            for t in b.r.values():
                self._wait(e, *t)

    def op(self, e, fn, reads=(), writes=(), inc=True, touch=()):
        self._deps(e, reads, writes)
        pr, pw = self.pend[e]
        pr.extend(reads)
        pw.extend(writes)
        pw.extend(touch)
        rec = _Rec()
        fn(rec)
        name, a, kw = rec.call
        ln = sys._getframe(1).f_lineno if DEBUG_LINES else 0

        def emit(eng, name=name, a=a, kw=kw, ln=ln):
            i = getattr(eng, name)(*a, **kw)
            if DEBUG_LINES:
                LINEMAP[i.ins.name] = ln
            return i
        if not inc:
            self.q[e].append(emit)
            return None
        self.cnt[e] += 1
        tok = (e, self.cnt[e])
        h = self.sem[e]
        self.q[e].append(lambda eng, emit=emit, h=h: emit(eng).then_inc(h, 1))
        for b in pr:
            b.r[e] = tok
        for b in pw:
            b.w = tok
            b.r = {}
        self.pend[e] = ([], [])
        return tok

    def dma(self, e, out_ap, in_ap, sem, reads=(), writes=()):
        assert not self.pend[e][0] and not self.pend[e][1]
        assert (e == "pool") == sem.key.startswith("swdma"), (e, sem.key)
        self._deps(e, reads, writes)
        sem.count += 16
        tok = (sem.key, sem.count)
        h = sem.h
        self.q[e].append(lambda eng, o=out_ap, i=in_ap, h=h: eng.dma_start(out=o, in_=i).then_inc(h, 16))
        for b in reads:
            b.r[sem.key] = tok
        for b in writes:
            b.w = tok
            b.r = {}
        return tok

    def wait_tok(self, e, tok):
        self._wait(e, *tok)

    def end_stage(self, final_toks=()):
        for e in ENGS:
            assert not self.pend[e][0] and not self.pend[e][1], e
        for e in ENGS:
            for k in ENGS:
                if k != e and self.cnt[k] > 0:
                    self._wait(e, k, self.cnt[k])
        for t in final_toks:
            self._wait("sp", *t)
        for ds in self.dma_sems[:self.dma_next] + self.stage_sw:
            if ds.count > 0:
                self._wait("sp", ds.key, ds.count)
        with self.nc.Block() as block:
            for e, reg in (("pe", block.tensor), ("act", block.scalar), ("dve", block.vector),
                           ("pool", block.gpsimd), ("sp", block.sync)):
                lst = self.q[e]

                def body(eng, lst=lst):
                    for f in lst:
                        f(eng)
                reg(body)


class Ctx:
    def __init__(self, P, es):
        self.P = P
        self.nc = P.nc
        self.es = es

    def sb(self, name, shape, dt):
        t = self.es.enter_context(self.nc.sbuf_tensor(f"{name}_{self.P.stage_id}", list(shape), dt))
        return t

    def ps(self, name, shape, dt=F32):
        t = self.es.enter_context(self.nc.psum_tensor(f"{name}_{self.P.stage_id}", list(shape), dt))
        return t


def make_ident(P, C, ident_f32, sem):
    idf = Buf(C.sb("idf", [128, 128], F32))
    idb = Buf(C.sb("idb", [128, 128], BF16))
    P.dma("sp", idf.ap[:, :], ident_f32, P.new_dma_sem(), writes=[idf])
    P.op("dve", lambda v: v.tensor_copy(idb.ap[:, :], idf.ap[:, :]), reads=[idf], writes=[idb])
    return idf, idb


class XPrep:
    def __init__(self, P, C, idb, banks, cast_eng="act"):
        self.P = P
        self.cast_eng = cast_eng
        self.xst = [Buf(C.sb(f"xst{i}", [128, D], F32)) for i in range(2)]
        self.xbf = [Buf(C.sb(f"xbf{i}", [128, D], BF16)) for i in range(2)]
        self.sem = [P.new_dma_sem() for _ in range(2)]
        self.idb = idb
        self.banks = banks
        self.n = 0

    def __call__(self, row_ap, dst):
        self.part2(self.part1(row_ap), dst)

    def part1(self, row_ap):
        P = self.P
        k = self.n % 2
        bank = self.banks[self.n % len(self.banks)]
        self.n += 1
        xst, xbf = self.xst[k], self.xbf[k]
        P.dma("sp", xst.ap[:, :], row_ap, self.sem[k], writes=[xst])
        if self.cast_eng == "act":
            P.op("act", lambda a: a.copy(xbf.ap[:, :], xst.ap[:, :]), reads=[xst], writes=[xbf])
        else:
            P.op("pool", lambda g: g.tensor_copy(xbf.ap[:, :], xst.ap[:, :]), reads=[xst], writes=[xbf])
        return (k, bank)

    def part2(self, h, dst):
        P = self.P
        k, bank = h
        xbf, idb = self.xbf[k], self.idb
        pv = bank.ap.bitcast(BF16)
        for c in range(8):
            P.op("pe", lambda pe, c=c: pe.transpose(
                pv[:, c * 128:(c + 1) * 128], xbf.ap[:, c * 128:(c + 1) * 128], idb.ap[:, :]),
                reads=[xbf, idb], writes=[bank] if c == 0 else [], inc=(c == 7))
        P.op("dve", lambda v: v.tensor_copy(dst.ap, pv[:, :].rearrange("p (c n) -> p c n", c=8)),
             reads=[bank], writes=[dst])


class LNEpi:
    def __init__(self, P, C, gain, bias, sem_const, depth=2):
        self.P = P
        self.depth = depth
        self.xres = [Buf(C.sb(f"xres{i}", [128, D], F32)) for i in range(depth)]
        self.zb = [Buf(C.sb(f"z{i}", [128, D], F32)) for i in range(depth)]
        self.ob = [Buf(C.sb(f"ob{i}", [128, D], F32)) for i in range(depth)]
        self.st = [Buf(C.sb(f"st{i}", [128, 16], F32)) for i in range(depth)]
        self.gbc = Buf(C.sb("gbc", [128, D], F32))
        self.bbc = Buf(C.sb("bbc", [128, D], F32))
        self.s_x = [P.new_dma_sem() for _ in range(depth)]
        self.s_o = [P.new_dma_sem() for _ in range(depth)]
        P.dma("sp", self.gbc.ap[:, :], gain.partition_broadcast(128), P.new_dma_sem(), writes=[self.gbc])
        P.dma("sp", self.bbc.ap[:, :], bias.partition_broadcast(128), P.new_dma_sem(), writes=[self.bbc])

    def load_x(self, k, row_ap):
        self.P.dma("sp", self.xres[k].ap[:, :], row_ap, self.s_x[k], writes=[self.xres[k]])

    def __call__(self, k, banks, dst_ap, yscale):
        self.e1(k, banks, yscale)
        self.e2(k, dst_ap)

    def e1(self, k, banks, yscale):
        self.e1a(k)
        self.e1b(k, banks, yscale)

    def e1a(self, k):
        xres = self.xres[k]
        self.P.op("act", lambda a: a.mul(xres.ap[:, :], xres.ap[:, :], ALPHA), reads=[xres], writes=[xres])

    def e1b(self, k, banks, yscale):
        P = self.P
        xres, z, o, sb_ = self.xres[k], self.zb[k], self.ob[k], self.st[k]
        for n in range(2):
            P.op("dve", lambda v, n=n: v.scalar_tensor_tensor(
                z.ap[:, n * 512:(n + 1) * 512], banks[n].ap[:, :], float(yscale),
                xres.ap[:, n * 512:(n + 1) * 512], ALU.mult, ALU.add),
                reads=[banks[n], xres], writes=[z] if n == 0 else [], inc=(n == 1))
        for n in range(2):
            P.op("dve", lambda v, n=n: v.bn_stats(sb_.ap[:, n * 6:(n + 1) * 6], z.ap[:, n * 512:(n + 1) * 512]),
                 reads=[z], writes=[sb_] if n == 0 else [], inc=(n == 1))
        P.op("dve", lambda v: v.bn_aggr(sb_.ap[:, 12:14], sb_.ap[:, 0:12]), reads=[sb_], writes=[sb_])
        P.op("act", lambda a: a.activation(sb_.ap[:, 14:15], sb_.ap[:, 13:14], AF.Sqrt, bias=LN_EPS, scale=1.0),
             reads=[sb_], writes=[sb_])

    def e2(self, k, dst_ap):
        P = self.P
        xres, z, o, sb_ = self.xres[k], self.zb[k], self.ob[k], self.st[k]
        gbc, bbc = self.gbc, self.bbc
        P.op("dve", lambda v: v.reciprocal(sb_.ap[:, 14:15], sb_.ap[:, 14:15]), reads=[sb_], writes=[sb_])
        P.op("dve", lambda v: v.scalar_tensor_tensor(
            sb_.ap[:, 15:16], sb_.ap[:, 12:13], -1.0, sb_.ap[:, 14:15], ALU.mult, ALU.mult),
            reads=[sb_], writes=[sb_])
        P.op("act", lambda a: a.activation(z.ap[:, :], z.ap[:, :], AF.Identity, bias=sb_.ap[:, 15:16],
                                           scale=sb_.ap[:, 14:15]), reads=[z, sb_], writes=[z])
        P.op("pool", lambda g: g.tensor_tensor(o.ap[:, :], z.ap[:, :], gbc.ap[:, :], ALU.mult),
             reads=[z, gbc], writes=[o])
        P.op("pool", lambda g: g.tensor_tensor(o.ap[:, :], o.ap[:, :], bbc.ap[:, :], ALU.add),
             reads=[o, bbc], writes=[o])
        P.dma("sp", dst_ap, o.ap[:, :], self.s_o[k], reads=[o])


def stage_ffn(P, x_src, x_dst, w_gu, w_down, gain, bias, ident_f32, ntiles=NT):
    P.begin_stage()
    NWG = 3
    with ExitStack() as es:
        C = Ctx(P, es)
        wd = C.sb("wd", [128, NFC, D], BF16)
        wg = [Buf(C.sb(f"wg{i}", [128, 8, 512], BF16)) for i in range(NWG)]
        xT_t = [C.sb(f"xT{i}", [128, 8, TT], BF16) for i in range(2)]
        gT_t = C.sb("gT", [128, NFC, TT], BF16)
        sg = [Buf(C.sb(f"sg{i}", [128, 512], F32)) for i in range(2)]
        pg = [Buf(C.ps(f"pg{i}", [128, 512], F32)) for i in range(4)]
        pd = [Buf(C.ps(f"pd{i}", [128, 512], F32)) for i in range(4)]
        wd_b = [Buf(wd[:, j, :]) for j in range(NFC)]
        xT = [[Buf(xT_t[i][:, :, s * 128:(s + 1) * 128]) for s in range(8)] for i in range(2)]
        gT = [[Buf(gT_t[:, j, h * 512:(h + 1) * 512]) for h in range(2)] for j in range(NFC)]

        s_const = P.new_dma_sem()
        s_wd = P.new_dma_sem(sw=True)
        s_wg = [P.new_dma_sem(sw=True) for _ in range(NWG)]
        idf, idb = make_ident(P, C, ident_f32, s_const)
        xprep = XPrep(P, C, idb, pd)
        epi = LNEpi(P, C, gain, bias, s_const)

        wguv = w_gu.rearrange("(c p) n -> p c n", p=128)

        def load_wg(step):
            j2 = step % (NFC // 2)
            slot = step % NWG
            b = wg[slot]
            P.dma("pool", b.ap[:, :, 0:256], wguv[:, :, 256 * j2:256 * j2 + 256], s_wg[slot], writes=[b])
            P.dma("pool", b.ap[:, :, 256:512], wguv[:, :, DFF + 256 * j2:DFF + 256 * j2 + 256], s_wg[slot])
            b.w = (s_wg[slot].key, s_wg[slot].count)

        nsteps = ntiles * (NFC // 2)
        xsv = x_src.rearrange("(t s p) d -> t s p d", s=8, p=128)
        xdv = x_dst.rearrange("(t s p) d -> t s p d", s=8, p=128)

        for stp in range(min(NWG, nsteps)):
            load_wg(stp)
        wdv = w_down.rearrange("(c p) n -> p c n", p=128)
        for j in range(NFC):
            P.dma("pool", wd[:, j, :], wdv[:, j, :], s_wd, writes=[wd_b[j]])
        for j in range(NFC):
            wd_b[j].w = (s_wd.key, s_wd.count)
        h0 = xprep.part1(xsv[0, 0])
        for s in range(8):
            h1 = xprep.part1(xsv[0, s + 1]) if s + 1 < 8 else None
            xprep.part2(h0, xT[0][s])
            h0 = h1

        gstep = 0
        grp = 0
        xh = {}
        pending_e2 = None
        for t in range(ntiles):
            xTt = xT[t % 2]
            for j2 in range(NFC // 2):
                if j2 == 1 and pending_e2 is not None:
                    epi.e2(*pending_e2)
                    pending_e2 = None
                slot = gstep % NWG
                wb = wg[slot]
                for jj in range(2):
                    j = 2 * j2 + jj
                    for h in range(2):
                        bg = pg[2 * (grp % 2)]
                        bu = pg[2 * (grp % 2) + 1]
                        sgb = sg[grp % 2]
                        grp += 1
                        rd = [wb] + [xTt[4 * h + q] for q in range(4)]
                        for (bank, off) in ((bg, 0), (bu, 256)):
                            for c in range(8):
                                P.op("pe", lambda pe, bank=bank, c=c, off=off, jj=jj, h=h, wb=wb, t=t: pe.matmul(
                                    bank.ap[:, :], wb.ap[:, c, off + jj * 128:off + jj * 128 + 128],
                                    xT_t[t % 2][:, c, h * 512:(h + 1) * 512], start=(c == 0), stop=(c == 7)),
                                    reads=rd if c == 0 else [], writes=[bank] if c == 0 else [], inc=(c == 7))
                        P.op("act", lambda a, sgb=sgb, bg=bg: a.activation(sgb.ap[:, :], bg.ap[:, :], AF.Silu),
                             reads=[bg], writes=[sgb])
                        dst = gT[j][h]
                        P.op("dve", lambda v, dst=dst, sgb=sgb, bu=bu: v.tensor_tensor(
                            dst.ap, sgb.ap[:, :], bu.ap[:, :], ALU.mult), reads=[sgb, bu], writes=[dst])
                gstep += 1
                if gstep + NWG - 1 < nsteps:
                    load_wg(gstep + NWG - 1)
                if t + 1 < ntiles:
                    if 2 <= j2 <= 9:
                        xprep.part2(xh.pop(j2 - 2), xT[(t + 1) % 2][j2 - 2])
                    if 1 <= j2 <= 8:
                        xh[j2 - 1] = xprep.part1(xsv[t + 1, j2 - 1])
            epi.load_x(0, xsv[t, 0])
            for s in range(8):
                k = s % 2
                if s + 1 < 8:
                    epi.load_x((s + 1) % 2, xsv[t, s + 1])
                banks = (pd[2 * k], pd[2 * k + 1])
                for n in range(2):
                    bank = banks[n]
                    for j in range(NFC):
                        P.op("pe", lambda pe, bank=bank, j=j, s=s, n=n: pe.matmul(
                            bank.ap[:, :], gT_t[:, j, s * 128:(s + 1) * 128], wd[:, j, n * 512:(n + 1) * 512],
                            start=(j == 0), stop=(j == NFC - 1)),
                            reads=[gT[j][s // 4], wd_b[j]], writes=[bank] if j == 0 else [], inc=(j == NFC - 1))
                epi.e1(k, banks, 0.5)
                if s >= 1:
                    epi.e2((s - 1) % 2, xdv[t, s - 1])
            pending_e2 = (1, xdv[t, 7])
        epi.e2(*pending_e2)
        P.end_stage()


def stage_mix(P, layer, x_src, w_in, mem, w_kv, og, consts, fbias=None, cq=None, ck=None):
    ident_f32, rope_tabs, masks_f32 = consts
    P.begin_stage()
    dil = (1, 4, 16) if layer == 0 else (1, 1, 1)
    qmem_off = 3 * MIXW if layer == 0 else 3 * MIXW + NMIX
    with ExitStack() as es:
        C = Ctx(P, es)
        pb = [Buf(C.ps(f"pb{i}", [128, 512], F32)) for i in range(8)]
        s_const = P.new_dma_sem()
        idf, idb = make_ident(P, C, ident_f32, s_const)
        xprep = XPrep(P, C, idb, [pb[6]])
        xT_t = [C.sb(f"xT{i}", [128, 8, TT], BF16) for i in range(2)]
        xT = [[Buf(xT_t[i][:, :, s * 128:(s + 1) * 128]) for s in range(8)] for i in range(2)]
        wgt = [Buf(C.sb("wgt0", [128, 8, 768], BF16))] * 2
        s_wgt = [P.new_dma_sem(sw=True)] * 2
        qkbf = [Buf(C.sb(f"qkbf{i}", [128, 512], BF16)) for i in range(2)]
        qk_all = C.sb("qkall", [70, 8, S], BF16)
        qkt = [Buf(qk_all[0:64, :, t * TT:(t + 1) * TT]) for t in range(NT)]
        v_sb = C.sb("vsb", [128, 32, 4, 65], BF16)
        vt = [Buf(v_sb[:, 8 * t:8 * t + 8, :, :]) for t in range(NT)]
        vones = Buf(v_sb[:, :, :, 64:65])
        pT = [Buf(C.sb(f"pT{i}", [128, 512], BF16)) for i in range(3)]
        ost = [Buf(C.sb(f"ost{i}", [128, 1040], F32)) for i in range(2)]
        s_ost = [P.new_dma_sem() for _ in range(2)]
        mkf = Buf(C.sb("mkf", [128, 384], F32))
        mk = Buf(C.sb("mk", [128, 3, 128], BF16))
        memT_t = C.sb("memT", [128, 8, MEML], BF16)
        memT = [Buf(memT_t[:, :, i * 128:(i + 1) * 128]) for i in range(2)]
        wkv = Buf(C.sb("wkv", [128, 8, 512], BF16))
        kmT = Buf(C.sb("kmT", [64, 4, MEML], BF16))
        vm_sb = Buf(C.sb("vmsb", [128, 2, 4, 65], BF16))
        s_wkv = P.new_dma_sem(sw=True)
        if layer == 0:
            tabg = Buf(C.sb("tabg", [128, 32, 128], F32))
            s_tab = P.new_dma_sem()
            rtmp = [[Buf(C.sb(f"rt{i}_{q}", [128, 64], F32)) for q in range(4)] for i in range(2)]
            a32b = [Buf(C.sb(f"a32_{i}", [128, 512], F32)) for i in range(2)]
        else:
            wf = Buf(C.sb("wf", [128, 8, NMIX], BF16))
            s_wf = P.new_dma_sem(sw=True)
            fst = [Buf(C.sb(f"fst{i}", [128, NMIX], F32)) for i in range(2)]
            fT = Buf(C.sb("fT", [NMIX, S], F32))
            augb = Buf(qk_all[64:70, :, :])
            s_aug = P.new_dma_sem()
            cq_b = Buf(cq)
            ck_b = Buf(ck)

        P.dma("sp", mkf.ap[:, :], masks_f32, s_const, writes=[mkf])
        P.op("dve", lambda v: v.tensor_copy(mk.ap[:, :, :], mkf.ap[:, :].rearrange("p (a b) -> p a b", a=3)),
             reads=[mkf], writes=[mk])
        M_DIAG, M_PREV, M_ALL = 0, 1, 2
        if layer == 0:
            m01 = Buf(C.sb("m01", [128, 2, 512], BF16))
            mv_ = mkf.ap[:, :].rearrange("p (a b) -> p a b", a=3)
            for hh in range(2):
                for (sel, src) in ((0, M_ALL), (1, M_PREV)):
                    P.op("dve", lambda v, hh=hh, sel=sel, src=src: v.tensor_scalar(
                        m01.ap[:, sel, hh * 256:hh * 256 + 128], mv_[:, src, :], 0.0, None, ALU.is_equal),
                        reads=[mkf], writes=[m01])
                    P.op("dve", lambda v, hh=hh, sel=sel: v.tensor_scalar(
                        m01.ap[:, sel, hh * 256 + 128:hh * 256 + 256], mv_[:, M_DIAG, :], 0.0, None,
                        ALU.is_equal), reads=[mkf], writes=[m01])
        P.op("pool", lambda g: g.memset(v_sb[:, :, :, :].rearrange("p u j e -> p (u j) e")[:, :, 64:65], 1.0),
             writes=[vones])
        P.op("pool", lambda g: g.memset(vm_sb.ap[:, :, :, :].rearrange("p u j e -> p (u j) e")[:, :, 64:65], 1.0),
             writes=[vm_sb])

        w_inv = w_in.rearrange("(c p) n -> p c n", p=128)

        def load_wgt(g):
            b = wgt[g % 2]
            if g < 3:
                for i in range(3):
                    P.dma("pool", b.ap[:, :, i * 256:(i + 1) * 256],
                          w_inv[:, :, i * MIXW + g * 256:i * MIXW + (g + 1) * 256], s_wgt[g % 2],
                          writes=[b] if i == 0 else [])
            else:
                P.dma("pool", b.ap[:, :, 0:256], w_inv[:, :, qmem_off:qmem_off + 256], s_wgt[g % 2], writes=[b])
            b.w = (s_wgt[g % 2].key, s_wgt[g % 2].count)

        load_wgt(0)
        if layer == 1:
            P.dma("pool", wf.ap[:, :, :], w_inv[:, :, 3 * MIXW:3 * MIXW + NMIX], s_wf, writes=[wf])
        P.dma("pool", wkv.ap[:, :, :], w_kv.rearrange("(c p) n -> p c n", p=128), s_wkv, writes=[wkv])

        def proj_mm(g, u, xTbuf_t, xTb, s, nslots, with_v, with_f, rope_ap):
            wb = wgt[g % 2]
            A = pb[u % 3]
            Bk = pb[3 + u % 3]
            ncol = nslots * 64
            for c in range(8):
                P.op("pe", lambda pe, c=c: pe.matmul(
                    A.ap[:, 0:ncol], xTbuf_t[:, c, s * 128:(s + 1) * 128], wb.ap[:, c, 0:ncol],
                    start=(c == 0), stop=(c == 7)),
                    reads=[xTb, wb] if c == 0 else [], writes=[A] if c == 0 else [], inc=(c == 7))
            if with_v:
                for c in range(8):
                    P.op("pe", lambda pe, c=c: pe.matmul(
                        Bk.ap[:, 0:256], xTbuf_t[:, c, s * 128:(s + 1) * 128], wb.ap[:, c, 512:768],
                        start=(c == 0), stop=(c == 7), skip_group_check=True),
                        reads=[xTb, wb] if c == 0 else [], writes=[Bk] if c == 0 else [],
                        inc=(c == 7 and not with_f))
            if with_f:
                for c in range(8):
                    P.op("pe", lambda pe, c=c: pe.matmul(
                        Bk.ap[:, 256:256 + NMIX], xTbuf_t[:, c, s * 128:(s + 1) * 128], wf.ap[:, c, :],
                        start=False, stop=(c == 7), skip_group_check=True), reads=[wf] if c == 0 else [], inc=(c == 7))

        def proj_fin(g, u, xTbuf_t, xTb, s, nslots, with_v, with_f, rope_ap):
            wb = wgt[g % 2]
            A = pb[u % 3]
            Bk = pb[3 + u % 3]
            ncol = nslots * 64
            qb = qkbf[u % 2]
            if rope_ap is not None:
                tb = tabg
                a32 = a32b[u % 2]
                P.op("act", lambda a: a.copy(a32.ap[:, :], A.ap[:, :]), reads=[A], writes=[a32])
                Av = a32.ap[:, :].rearrange("p (j d) -> p j d", j=8)
                qv = qb.ap[:, :].rearrange("p (j d) -> p j d", j=8)
                cosv = tb.ap[:, u, 0:64].rearrange("p (j d) -> p j d", j=8)
                sinv = tb.ap[:, u, 64:128].rearrange("p (j d) -> p j d", j=8)
                rt = rtmp[u % 2]
                rv = [r_.ap[:, :].rearrange("p (j d) -> p j d", j=8) for r_ in rt]
                P.op("pool", lambda g_: g_.tensor_copy(qv[:, :, 16:64], Av[:, :, 16:64]), reads=[a32], writes=[qb])
                P.op("dve", lambda v: v.tensor_tensor(rv[0], Av[:, :, 0:8], cosv, ALU.mult),
                     reads=[a32, tb], writes=[rt[0]], inc=False)
                P.op("dve", lambda v: v.tensor_tensor(rv[1], Av[:, :, 8:16], sinv, ALU.mult),
                     reads=[a32, tb], writes=[rt[1]], inc=False)
                P.op("dve", lambda v: v.tensor_tensor(rv[2], Av[:, :, 8:16], cosv, ALU.mult),
                     reads=[a32, tb], writes=[rt[2]], inc=False)
                P.op("dve", lambda v: v.tensor_tensor(rv[3], Av[:, :, 0:8], sinv, ALU.mult),
                     reads=[a32, tb], writes=[rt[3]])
                P.op("pool", lambda g_: g_.tensor_tensor(qv[:, :, 0:8], rv[0], rv[1], ALU.subtract),
                     reads=[rt[0], rt[1]], writes=[qb], inc=False)
                P.op("pool", lambda g_: g_.tensor_tensor(qv[:, :, 8:16], rv[2], rv[3], ALU.add),
                     reads=[rt[2], rt[3]], writes=[qb])
            else:
                P.op("act", lambda a: a.copy(qb.ap[:, 0:ncol], A.ap[:, 0:ncol]), reads=[A], writes=[qb])
            if with_v:
                P.op("act", lambda a: a.copy(v_sb[:, u, :, 0:64], Bk.ap[:, 0:256].rearrange("p (j d) -> p j d", j=4)),
                     reads=[Bk], writes=[vt[u // 8]])
            if with_f:
                fs = fst[u % 2]
                P.op("act", lambda a: a.copy(fs.ap[:, :], Bk.ap[:, 256:256 + NMIX]), reads=[Bk], writes=[fs])
                P.op("pe", lambda pe: pe.transpose(Bk.ap[0:NMIX, 384:512], fs.ap[:, :], idf.ap[:, :]),
                     reads=[fs, idf], writes=[Bk])
                P.op("dve", lambda v: v.tensor_copy(fT.ap[:, u * 128:(u + 1) * 128], Bk.ap[0:NMIX, 384:512]),
                     reads=[Bk], writes=[fT])
            tq = pb[7]
            tqv = tq.ap.bitcast(BF16)
            for sl in range(nslots):
                P.op("pe", lambda pe, sl=sl: pe.transpose(
                    tqv[0:64, sl * 128:(sl + 1) * 128], qb.ap[:, sl * 64:(sl + 1) * 64], idb.ap[:, :]),
                    reads=[qb, idb], writes=[tq] if sl == 0 else [], inc=(sl == nslots - 1))
            P.op("dve", lambda v: v.tensor_copy(
                qk_all[0:64, 0:nslots, u * 128:(u + 1) * 128],
                tqv[0:64, 0:nslots * 128].rearrange("p (j n) -> p j n", j=nslots)),
                reads=[tq], writes=[qkt[u // 8]])

        def proj_group(g):
            r = dil[g] if g < 3 else 1
            L = S // r
            nslots = 8 if g < 3 else 4
            xr = x_src.rearrange("(n r) d -> r n d", r=r)
            if layer == 0 and g < 3:
                P.dma("sp", tabg.ap[:, :, :], rope_tabs[g].rearrange("(u p) f -> p u f", p=128), s_tab, writes=[tabg])

            def rows(u):
                m0 = u * 128
                return xr[m0 // L, (m0 % L):(m0 % L) + 128, :]

            if g == 0:
                for s in range(8):
                    xprep(rows(s), xT[0][s])
            rope_ap = True if (layer == 0 and g < 3) else None
            xh = {}

            def args(u):
                t, s = u // 8, u % 8
                return (g, u, xT_t[t % 2], xT[t % 2][s], s, nslots, g < 3, (layer == 1 and g == 0), rope_ap)

            proj_mm(*args(0))
            proj_mm(*args(1))
            for u in range(32):
                if u + 7 in xh:
                    v = u + 7
                    xprep.part2(xh.pop(v), xT[(v // 8) % 2][v % 8])
                if u + 8 < 32:
                    xh[u + 8] = xprep.part1(rows(u + 8))
                if u + 2 < 32:
                    proj_mm(*args(u + 2))
                proj_fin(*args(u))
            if g + 1 < 4:
                load_wgt(g + 1)
                r2 = dil[g + 1] if g + 1 < 3 else 1
                xr2 = x_src.rearrange("(n r) d -> r n d", r=r2)
                L2 = S // r2
                for s in range(8):
                    m0 = s * 128
                    xprep(xr2[m0 // L2, (m0 % L2):(m0 % L2) + 128, :], xT[0][s])

        def forget_prep():
            CH = 512
            nb_ = Buf(C.sb("negb", [NMIX, 1], F32))
            ones = Buf(C.sb("ones12", [NMIX, CH], F32))
            onesb = Buf(C.sb("ones12b", [NMIX, CH], BF16))
            e1 = Buf(C.sb("e1", [NMIX, CH], F32))
            cc = [Buf(C.sb(f"cc{i}", [NMIX, CH], F32)) for i in range(2)]
            c8 = Buf(C.sb("c8", [NMIX, CH], F32))
            pc = [Buf(C.sb(f"pc{i}", [NMIX, CH], BF16)) for i in range(3)]
            npc = [Buf(C.sb(f"npc{i}", [NMIX, CH], BF16)) for i in range(3)]
            s_c = P.new_dma_sem()
            s_cs = P.new_dma_sem()
            P.dma("sp", nb_.ap[:, :], fbias.rearrange("(h o) -> h o", o=1), s_c, writes=[nb_])
            P.op("act", lambda a: a.mul(nb_.ap[:, :], nb_.ap[:, :], -1.0), reads=[nb_], writes=[nb_])
            P.op("pool", lambda g_: g_.memset(ones.ap[:, :], 1.0), writes=[ones])
            P.op("pool", lambda g_: g_.memset(onesb.ap[:, :], 1.0), writes=[onesb])
            for ci in range(S // CH):
                sl = slice(ci * CH, (ci + 1) * CH)
                P.op("act", lambda a, sl=sl: a.activation(e1.ap[:, :], fT.ap[:, sl], AF.Exp, bias=nb_.ap[:, 0:1],
                                                          scale=-1.0), reads=[fT, nb_], writes=[e1])
                P.op("act", lambda a: a.activation(e1.ap[:, :], e1.ap[:, :], AF.Ln, bias=1.0, scale=1.0),
                     reads=[e1], writes=[e1])
                cur, prev = cc[ci % 2], cc[(ci + 1) % 2]
                init = 0.0 if ci == 0 else prev.ap[:, CH - 1:CH]
                P.op("dve", lambda v, cur=cur, init=init: v.tensor_tensor_scan(
                    cur.ap[:, :], ones.ap[:, :], e1.ap[:, :], init, ALU.mult, ALU.subtract),
                    reads=[ones, e1, prev], writes=[cur])
                P.op("act", lambda a, cur=cur: a.mul(c8.ap[:, :], cur.ap[:, :], 1.0 / SCALE), reads=[cur], writes=[c8])
                for i in range(3):
                    P.op("dve", lambda v, i=i: v.tensor_copy(pc[i].ap[:, :], c8.ap[:, :]), reads=[c8], writes=[pc[i]])
                    if i < 2:
                        P.op("dve", lambda v, i=i: v.tensor_tensor(c8.ap[:, :], c8.ap[:, :], pc[i].ap[:, :],
                                                                    ALU.subtract), reads=[c8, pc[i]], writes=[c8])
                for i in range(3):
                    P.op("act", lambda a, i=i: a.mul(npc[i].ap[:, :], pc[i].ap[:, :], -1.0),
                         reads=[pc[i]], writes=[npc[i]])
                for i in range(3):
                    P.dma("sp", cq[:, i, sl], pc[i].ap[:, :], s_cs, reads=[pc[i]], writes=[cq_b] if i == 0 else [])
                    P.dma("sp", ck[:, 3 + i, sl], npc[i].ap[:, :], s_cs, reads=[npc[i]], writes=[ck_b] if i == 0 else [])
                    P.dma("sp", cq[:, 3 + i, sl], onesb.ap[:, :], s_cs, reads=[onesb])
                    P.dma("sp", ck[:, i, sl], onesb.ap[:, :], s_cs, reads=[onesb])
                cq_b.w = (s_cs.key, s_cs.count)
                ck_b.w = (s_cs.key, s_cs.count)
                for b_ in pc + npc + [onesb]:
                    b_.r[s_cs.key] = (s_cs.key, s_cs.count)

        def load_aug(g):
            for j in range(4):
                P.dma("sp", qk_all[64:70, j, :], cq[4 * g + j], s_aug, reads=[cq_b], writes=[augb] if j == 0 else [])
                P.dma("sp", qk_all[64:70, 4 + j, :], ck[4 * g + j], s_aug, reads=[ck_b])
            augb.w = (s_aug.key, s_aug.count)

        cnt = {"s": 0, "o": 0, "st": 0}

        def flush_o(ob, ncol, dst_ap, view=None):
            k = cnt["st"] % 2
            cnt["st"] += 1
            stg = ost[k]
            P.op("dve", lambda v: v.tensor_copy(stg.ap[:, 0:ncol], ob.ap[:, 0:ncol]), reads=[ob], writes=[stg])
            return stg, k

        def attn_banded(g):
            r = dil[g]
            nb = 32 // r
            ogv = og[g].rearrange("(n r) f -> r n f", r=r)
            items = [(B, hp) for B in range(32) for hp in range(2)]
            st_ = {}

            def emit_score(idx):
                B, hp = items[idx]
                b = B % nb
                sb_ = pb[idx % 3]
                pt = pT[idx % 3]
                tiles_q = [qkt[B // 8]] + ([qkt[(B - 1) // 8]] if b > 0 else [])
                for hh in range(2):
                    j = hp * 2 + hh
                    reg = sb_.ap[:, hh * 256:hh * 256 + 128]
                    first = (hh == 0)
                    KP = (B - 1) if b > 0 else B
                    P.op("pe", lambda pe: pe.matmul(
                        reg, qk_all[0:64, 4 + j, KP * 128:(KP + 1) * 128], qk_all[0:64, j, B * 128:(B + 1) * 128],
                        start=first, stop=True, skip_group_check=True),
                        reads=tiles_q, writes=[sb_] if first else [], inc=False)
                    reg2 = sb_.ap[:, hh * 256 + 128:hh * 256 + 256]
                    P.op("pe", lambda pe: pe.matmul(
                        reg2, qk_all[0:64, 4 + j, B * 128:(B + 1) * 128], qk_all[0:64, j, B * 128:(B + 1) * 128],
                        start=False, stop=True, skip_group_check=True), reads=tiles_q, inc=(hh == 1))
                P.op("act", lambda a: a.activation(pt.ap[:, :], sb_.ap[:, :], AF.Exp, scale=SCALE),
                     reads=[sb_], writes=[pt])
                msel = 1 if b > 0 else 0
                P.op("dve", lambda v: v.tensor_tensor(pt.ap[:, :], pt.ap[:, :], m01.ap[:, msel, :], ALU.mult),
                     reads=[pt, m01], writes=[pt])

            def emit_pv(idx):
                B, hp = items[idx]
                b = B % nb
                c = B // nb
                pt = pT[idx % 3]
                if hp == 0:
                    st_["ob"] = pb[3 + cnt["o"] % 2]
                    cnt["o"] += 1
                ob = st_["ob"]
                tiles_v = [vt[B // 8]] + ([vt[(B - 1) // 8]] if b > 0 else [])
                for hh in range(2):
                    j = hp * 2 + hh
                    oreg = ob.ap[:, j * 65:(j + 1) * 65]
                    firstw = (hp == 0 and hh == 0)
                    if b > 0:
                        P.op("pe", lambda pe: pe.matmul(
                            oreg, pt.ap[:, hh * 256:hh * 256 + 128], v_sb[:, B - 1, j, :], start=firstw, stop=False,
                            skip_group_check=True),
                            reads=[pt, vones] + tiles_v, writes=[ob] if firstw else [], inc=False)
                    P.op("pe", lambda pe: pe.matmul(
                        oreg, pt.ap[:, hh * 256 + 128:hh * 256 + 256], v_sb[:, B, j, :],
                        start=(b == 0 and firstw), stop=True, skip_group_check=True),
                        reads=[pt, vones] + tiles_v, writes=[ob] if (firstw and b == 0) else [],
                        touch=[ob], inc=(hh == 1))
                if hp == 1:
                    stg, k = flush_o(ob, 260, None)
                    P.dma("sp", ogv[c, b * 128:(b + 1) * 128, :], stg.ap[:, 0:260], s_ost[k], reads=[stg])

            LA = 2
            for idx in range(len(items) + LA):
                if idx < len(items):
                    emit_score(idx)
                if idx - LA >= 0:
                    emit_pv(idx - LA)

        def attn_qtiles(g, causal, K, kT_fn, v_fn, kdeps_fn):
            ogv = og[g].rearrange("(t i p) (j e) -> t p i j e", i=4, p=128, j=4)
            items = []
            for T in range(8):
                nkb = (4 * T + 4) if causal else 2
                for j in range(4):
                    for KB in range(nkb):
                        items.append((T, j, KB, nkb))
            st_ = {}

            def emit_score(idx):
                T, j, KB, nkb = items[idx]
                sb_ = pb[idx % 3]
                pt = pT[idx % 3]
                jd = KB - 4 * T if causal else -1
                kdeps = kdeps_fn(KB)
                qdeps = [qkt[T // 2]] + ([augb] if (layer == 1 and causal) else [])
                if jd < 0:
                    c0 = 0
                    P.op("pe", lambda pe: pe.matmul(
                        sb_.ap[:, 0:512], kT_fn(j, KB), qk_all[0:K, j, T * 512:(T + 1) * 512],
                        start=True, stop=True), reads=kdeps + qdeps, writes=[sb_])
                else:
                    c0 = jd * 128
                    reg = sb_.ap[:, c0:c0 + 128]
                    P.op("pe", lambda pe: pe.matmul(reg, idb.ap[:, :], mk.ap[:, M_DIAG, :],
                                                     start=True, stop=False, skip_group_check=True),
                         reads=[idb, mk], writes=[sb_], inc=False)
                    P.op("pe", lambda pe: pe.matmul(
                        reg, kT_fn(j, KB), qk_all[0:K, j, T * 512 + c0:T * 512 + c0 + 128],
                        start=False, stop=True, skip_group_check=True), reads=kdeps + qdeps, inc=(jd == 3))
                    if jd < 3:
                        P.op("pe", lambda pe: pe.matmul(
                            sb_.ap[:, c0 + 128:512], kT_fn(j, KB),
                            qk_all[0:K, j, T * 512 + c0 + 128:(T + 1) * 512], start=False, stop=True,
                            skip_group_check=True))
                P.op("act", lambda a: a.activation(
                    pt.ap[:, c0:512], sb_.ap[:, c0:512], AF.Exp, scale=SCALE), reads=[sb_], writes=[pt])

            def emit_pv(idx):
                T, j, KB, nkb = items[idx]
                pt = pT[idx % 3]
                jd = KB - 4 * T if causal else -1
                kdeps = kdeps_fn(KB)
                if KB == 0:
                    st_["ob"] = pb[3 + cnt["o"] % 2]
                    cnt["o"] += 1
                    if j == 0:
                        st_["k"] = cnt["st"] % 2
                        cnt["st"] += 1
                ob = st_["ob"]
                stg = ost[st_["k"]]
                stv = stg.ap[:, :].rearrange("p (i j e) -> p i j e", i=4, j=4)
                i0_ = max(jd, 0)
                for i in range(i0_, 4):
                    last_kb = (4 * T + i) if causal else 1
                    P.op("pe", lambda pe: pe.matmul(
                        ob.ap[:, i * 65:(i + 1) * 65], pt.ap[:, i * 128:(i + 1) * 128], v_fn(j, KB),
                        start=(KB == 0 and i == 0), stop=(KB == last_kb), skip_group_check=True),
                        reads=[pt] + kdeps, writes=[ob] if (KB == 0 and i == 0) else [], touch=[ob], inc=(i == 3))
                if KB == nkb - 1:
                    P.op("dve", lambda v: v.tensor_copy(
                        stv[:, :, j, :], ob.ap[:, 0:260].rearrange("p (i e) -> p i e", i=4)),
                        reads=[ob], writes=[stg])
                    if j == 3:
                        P.dma("sp", ogv[T], stv, s_ost[st_["k"]], reads=[stg])

            LA = 2
            for idx in range(len(items) + LA):
                if idx < len(items):
                    emit_score(idx)
                if idx - LA >= 0:
                    emit_pv(idx - LA)

        def mem_kv():
            mv = mem.rearrange("(s p) d -> s p d", p=128)
            for i in range(2):
                xprep(mv[i], memT[i])
            for hp in range(2):
                bk = pb[hp]
                for hh in range(2):
                    j = hp * 2 + hh
                    for c in range(8):
                        P.op("pe", lambda pe, c=c, j=j, hh=hh, bk=bk: pe.matmul(
                            bk.ap[0:64, hh * 256:(hh + 1) * 256], wkv.ap[:, c, j * 64:(j + 1) * 64],
                            memT_t[:, c, :], start=(c == 0 and hh == 0), stop=(c == 7), skip_group_check=True),
                            reads=[wkv] + memT if c == 0 else [], writes=[bk] if (c == 0 and hh == 0) else [],
                            inc=(c == 7 and hh == 1))
                P.op("act", lambda a, bk=bk, hp=hp: a.copy(
                    kmT.ap[:, 2 * hp:2 * hp + 2, :], bk.ap[0:64, :].rearrange("p (j n) -> p j n", j=2)),
                    reads=[bk], writes=[kmT])
            for mb in range(2):
                bk = pb[2 + mb]
                for c in range(8):
                    P.op("pe", lambda pe, c=c, mb=mb, bk=bk: pe.matmul(
                        bk.ap[:, 0:256], memT_t[:, c, mb * 128:(mb + 1) * 128], wkv.ap[:, c, 256:512],
                        start=(c == 0), stop=(c == 7)),
                        reads=[wkv] + memT if c == 0 else [], writes=[bk] if c == 0 else [], inc=(c == 7))
                P.op("act", lambda a, bk=bk, mb=mb: a.copy(
                    vm_sb.ap[:, mb, :, 0:64], bk.ap[:, 0:256].rearrange("p (j d) -> p j d", j=4)),
                    reads=[bk], writes=[vm_sb])

        upto = DEBUG_UPTO
        if upto >= 1:
            mem_kv()
        for g in range(4):
            if upto < 2 or (upto in (2, 3) and g > 0):
                break
            proj_group(g)
            if upto == 2:
                break
            if layer == 1 and g == 0:
                forget_prep()
            if g < 3:
                if layer == 0:
                    attn_banded(g)
                else:
                    load_aug(g)
                    attn_qtiles(g, True, 70,
                                lambda j, KB: qk_all[0:70, 4 + j, KB * 128:(KB + 1) * 128],
                                lambda j, KB: v_sb[:, KB, j, :],
                                lambda KB: [qkt[KB // 8], vt[KB // 8], vones, augb])
            else:
                attn_qtiles(g, False, 64,
                            lambda j, KB: kmT.ap[:, j, KB * 128:(KB + 1) * 128],
                            lambda j, KB: vm_sb.ap[:, KB, j, :],
                            lambda KB: [kmT, vm_sb])
        P.end_stage()


def stage_out(P, layer, x_src, x_dst, og, w_out, gain, bias, ident_f32):
    P.begin_stage()
    nch = 4 if layer == 0 else 8
    NB = 4
    NPB = 3
    with ExitStack() as es:
        C = Ctx(P, es)
        pb = [Buf(C.ps(f"pb{i}", [128, 512], F32)) for i in range(8)]
        s_const = P.new_dma_sem()
        idf, idb = make_ident(P, C, ident_f32, s_const)
        epi = LNEpi(P, C, gain, bias, s_const, depth=NB)
        wo = Buf(C.sb("wo", [128, nch, D], BF16))
        s_wo = P.new_dma_sem(sw=True)
        P.dma("pool", wo.ap[:, :, :], w_out.rearrange("(c p) n -> p c n", p=128), s_wo, writes=[wo])
        ogt = [[Buf(C.sb(f"ogt{k}_{i}", [128, 260], F32)) for i in range(4)] for k in range(NB)]
        s_og = [P.new_dma_sem() for _ in range(NB)]
        rec = [[Buf(C.sb(f"rec{k}_{i}", [128, 4], F32)) for i in range(4)] for k in range(NB)]
        cat = [Buf(C.sb(f"cat{k}", [128, nch * 128], BF16)) for k in range(NB)]
        catT = [Buf(C.sb(f"catT{k}", [128, nch, 128], BF16)) for k in range(NB)]
        xsv = x_src.rearrange("(u p) d -> u p d", p=128)
        xdv = x_dst.rearrange("(u p) d -> u p d", p=128)
        ogv = [o.rearrange("(u p) f -> u p f", p=128) for o in og]
        NU = S // 128

        def loads(u):
            k = u % NB
            epi.load_x(k, xsv[u])
            for i in range(4):
                P.dma("sp", ogt[k][i].ap[:, :], ogv[i][u], s_og[k], writes=[ogt[k][i]])
            for i in range(4):
                ogt[k][i].w = (s_og[k].key, s_og[k].count)
                ogt[k][i].r = {}

        st_banks = {}

        def stage_a(u):
            k = u % NB
            if layer == 0:
                a0, a1, a2 = ogt[k][0], ogt[k][1], ogt[k][2]
                P.op("dve", lambda g_: g_.tensor_tensor(a0.ap[:, :], a0.ap[:, :], a1.ap[:, :], ALU.add),
                     reads=[a0, a1], writes=[a0])
                P.op("dve", lambda g_: g_.tensor_tensor(a0.ap[:, :], a0.ap[:, :], a2.ap[:, :], ALU.add),
                     reads=[a0, a2], writes=[a0])
                parts = [(ogt[k][0], 0), (ogt[k][3], 256)]
            else:
                parts = [(ogt[k][i], 256 * i) for i in range(4)]
            ct = cat[k]
            for pi, (src, col) in enumerate(parts):
                rc = rec[k][pi]
                sv = src.ap[:, :].rearrange("p (j e) -> p j e", j=4)
                P.op("dve", lambda v: v.reciprocal(rc.ap[:, :], sv[:, :, 64]), reads=[src], writes=[rc])
                P.op("dve", lambda v: v.tensor_tensor(
                    ct.ap[:, col:col + 256].rearrange("p (j d) -> p j d", j=4), sv[:, :, 0:64],
                    rc.ap[:, :].unsqueeze(2).broadcast_to([128, 4, 64]), ALU.mult),
                    reads=[src, rc], writes=[ct] if pi == 0 else [], touch=[ct])
            tb = pb[6 + u % 2]
            tbv = tb.ap.bitcast(BF16)
            for ch in range(nch):
                P.op("pe", lambda pe, ch=ch: pe.transpose(
                    tbv[:, ch * 128:(ch + 1) * 128], ct.ap[:, ch * 128:(ch + 1) * 128], idb.ap[:, :]),
                    reads=[ct, idb], writes=[tb] if ch == 0 else [], inc=(ch == nch - 1))
            cT = catT[k]
            P.op("act", lambda a: a.copy(cT.ap[:, :, :], tbv[:, 0:nch * 128].rearrange("p (c n) -> p c n", c=nch)),
                 reads=[tb], writes=[cT])
            kb = u % NPB
            banks = (pb[2 * kb], pb[2 * kb + 1])
            st_banks[u] = banks
            for n in range(2):
                bank = banks[n]
                for ch in range(nch):
                    P.op("pe", lambda pe, bank=bank, ch=ch, n=n: pe.matmul(
                        bank.ap[:, :], cT.ap[:, ch, :], wo.ap[:, ch, n * 512:(n + 1) * 512],
                        start=(ch == 0), stop=(ch == nch - 1)),
                        reads=[cT, wo] if ch == 0 else [], writes=[bank] if ch == 0 else [], inc=(ch == nch - 1))

        for u in range(min(NB, NU)):
            loads(u)
        for step in range(NU + 3):
            ua, u1, u2 = step, step - 1, step - 2
            if 0 <= u2 < NU:
                epi.e2(u2 % NB, xdv[u2])
                if u2 + NB < NU:
                    loads(u2 + NB)
            if 0 <= u1 < NU:
                epi.e1a(u1 % NB)
            if ua < NU:
                stage_a(ua)
            if 0 <= u1 < NU:
                epi.e1b(u1 % NB, st_banks[u1], 1.0)
        P.end_stage()


def host_constants():
    ident = np.eye(128, dtype=np.float32)
    NEG = np.float32(-1.0e5)
    k = np.arange(128)[:, None]
    q = np.arange(128)[None, :]
    m_diag = np.where(k <= q, 0.0, NEG).astype(np.float32)
    m_prev = np.where(k >= q, 0.0, NEG).astype(np.float32)
    m_all = np.full((128, 128), NEG, np.float32)
    masks = np.concatenate([m_diag, m_prev, m_all], axis=1)
    pos = np.arange(S, dtype=np.float32)
    inv_freq = (1.0 / (np.float32(500000.0) ** (np.arange(8, dtype=np.float32) / np.float32(8)))).astype(np.float32)
    ang = (pos[:, None] * inv_freq[None, :]).astype(np.float32)
    cos = np.cos(ang).astype(np.float32)
    sin = np.sin(ang).astype(np.float32)
    tabs = np.zeros((3, S, 128), np.float32)
    for g, r in enumerate((1, 4, 16)):
        L = S // r
        m = np.arange(S)
        tok = (m // L) + r * (m % L)
        tabs[g, :, 0:64] = np.tile(cos[tok], (1, 8))
        tabs[g, :, 64:128] = np.tile(sin[tok], (1, 8))
    return ident, masks, tabs


def build_program(stages=None):
    nc = bass.Bass("TRN2", target_bir_lowering=False)

    def din(name, shape):
        return nc.dram_tensor(name, list(shape), F32, kind="ExternalInput").ap()

    x = din("x", [S, D])
    mem = din("mem", [MEML, D])
    f1gu = din("ffn1_w_gate_up", [DEPTH, D, 2 * DFF])
    f1d = din("ffn1_w_down", [DEPTH, DFF, D])
    f2gu = din("ffn2_w_gate_up", [DEPTH, D, 2 * DFF])
    f2d = din("ffn2_w_down", [DEPTH, DFF, D])
    lng = din("ln_gain", [DEPTH, 3, D])
    lnb = din("ln_bias", [DEPTH, 3, D])
    wkv = din("mem_w_kv", [DEPTH, D, 2 * MEMW])
    awin = din("a_w_in", [1, D, A_IN_W])
    awout = din("a_w_out", [1, 4 * HD + MEMW, D])
    bwin = din("b_w_in", [1, D, B_IN_W])
    bfb = din("b_forget_bias", [1, NMIX])
    bwout = din("b_w_out", [1, MIXW + MEMW, D])
    ident = din("c_ident", [128, 128])
    masks = din("c_masks", [128, 384])
    tabs = din("c_rope", [3, S, 128])
    out = nc.dram_tensor("out", [S, D], F32, kind="ExternalOutput").ap()
    xa = nc.dram_tensor("scr_xa", [S, D], F32, kind="Internal").ap()
    xb = nc.dram_tensor("scr_xb", [S, D], F32, kind="Internal").ap()
    og = [nc.dram_tensor(f"scr_og{i}", [S, 260], F32, kind="Internal").ap() for i in range(4)]
    cq = nc.dram_tensor("scr_cq", [NMIX, 6, S], BF16, kind="Internal").ap()
    ck = nc.dram_tensor("scr_ck", [NMIX, 6, S], BF16, kind="Internal").ap()
    consts = (ident, tabs, masks)
    with ExitStack() as es:
        P = Prog(nc, es)
        allst = [
            lambda: stage_ffn(P, x, xa, f1gu[0], f1d[0], lng[0, 0], lnb[0, 0], ident),
            lambda: stage_mix(P, 0, xa, awin[0], mem, wkv[0], og, consts),
            lambda: stage_out(P, 0, xa, xb, og, awout[0], lng[0, 1], lnb[0, 1], ident),
            lambda: stage_ffn(P, xb, xa, f2gu[0], f2d[0], lng[0, 2], lnb[0, 2], ident),
            lambda: stage_ffn(P, xa, xb, f1gu[1], f1d[1], lng[1, 0], lnb[1, 0], ident),
            lambda: stage_mix(P, 1, xb, bwin[0], mem, wkv[1], og, consts, fbias=bfb[0], cq=cq, ck=ck),
            lambda: stage_out(P, 1, xb, xa, og, bwout[0], lng[1, 1], lnb[1, 1], ident),
            lambda: stage_ffn(P, xa, out, f2gu[1], f2d[1], lng[1, 2], lnb[1, 2], ident),
        ]
        for i, st in enumerate(allst):
            if stages is None or i in stages:
                st()
    return nc


def kernel(x, mem, ffn1_w_gate_up, ffn1_w_down, ffn2_w_gate_up, ffn2_w_down, ln_gain, ln_bias, mem_w_kv,
           a_w_in, a_w_out, b_w_in, b_forget_bias, b_w_out):
    ncores = 8
    ident, masks, tabs = host_constants()
    f32 = lambda a: np.ascontiguousarray(np.asarray(a, dtype=np.float32))
    shared = {
        "ffn1_w_gate_up": f32(ffn1_w_gate_up), "ffn1_w_down": f32(ffn1_w_down),
        "ffn2_w_gate_up": f32(ffn2_w_gate_up), "ffn2_w_down": f32(ffn2_w_down),
        "ln_gain": f32(ln_gain), "ln_bias": f32(ln_bias), "mem_w_kv": f32(mem_w_kv),
        "a_w_in": f32(a_w_in), "a_w_out": f32(a_w_out), "b_w_in": f32(b_w_in),
        "b_forget_bias": f32(b_forget_bias), "b_w_out": f32(b_w_out),
        "c_ident": ident, "c_masks": masks, "c_rope": tabs,
    }
    xs = f32(x)
    ms = f32(mem)
    in_maps = []
    for b in range(ncores):
        d = dict(shared)
        d["x"] = xs[b]
        d["mem"] = ms[b]
        in_maps.append(d)
    nc = build_program()
    res = run_bass_kernel_spmd(nc, in_maps, core_ids=list(range(ncores)))
    return np.stack([np.asarray(r["out"], dtype=np.float32) for r in res.results], axis=0)
```

```python
import sys
import numpy as np
from contextlib import ExitStack

import concourse.bass as bass
import concourse.mybir as mybir
from concourse.bass_utils import run_bass_kernel_spmd

F32 = mybir.dt.float32
BF16 = mybir.dt.bfloat16
AF = mybir.ActivationFunctionType
ALU = mybir.AluOpType

D = 1024
S = 4096
DFF = 2816
NFC = DFF // 128
DEPTH = 2
ALPHA = float((2 * DEPTH) ** 0.25)
LN_EPS = 1e-5
HD = 64
NMIX = 12
NMEMH = 4
MEML = 256
MIXW = 768
MEMW = 256
A_IN_W = 3 * MIXW + MEMW
B_IN_W = 3 * MIXW + NMIX + MEMW
SCALE = HD ** -0.5
TT = 1024
NT = S // TT

ENGS = ("pe", "act", "dve", "pool", "sp")
DEBUG_LINES = False
DEBUG_UPTO = 99
DEBUG_PROJ = 0
LINEMAP = {}


class Buf:
    def __init__(self, ap, name=""):
        self.ap = ap
        self.name = name
        self.w = None
        self.r = {}

    def __getitem__(self, k):
        return self.ap[k]


class DmaSem:
    def __init__(self, key, handle):
        self.key = key
        self.h = handle
        self.count = 0


class _Rec:
    def __init__(self):
        self.call = None

    def __getattr__(self, name):
        def f(*a, **kw):
            assert self.call is None
            self.call = (name, a, kw)
            return self
        return f


class Prog:
    def __init__(self, nc, es, n_dma_sems=18, n_stage_sets=8, n_sw_sems=26):
        self.nc = nc
        self.sem = {}
        self.q = {e: [] for e in ENGS}
        self.cnt = {e: 0 for e in ENGS}
        self.waited = {e: {} for e in ENGS}
        self.pend = {e: ([], []) for e in ENGS}
        self.free_eng_sets = []
        for i in range(n_stage_sets):
            st = {}
            for e in ENGS:
                st[e] = es.enter_context(nc.semaphore(f"s_{e}_{i}"))
            self.free_eng_sets.append(st)
        self.dma_sems = []
        for i in range(n_dma_sems):
            k = f"dma{i}"
            h = es.enter_context(nc.semaphore(f"s_{k}"))
            self.sem[k] = h
            self.dma_sems.append(DmaSem(k, h))
        self.sw_sems = []
        for i in range(n_sw_sems):
            k = f"swdma{i}"
            h = es.enter_context(nc.semaphore(f"s_{k}"))
            self.sem[k] = h
            self.sw_sems.append(DmaSem(k, h))
        self.sw_next = 0
        self.stage_sw = []
        self.dma_next = 0
        self.stage_id = -1

    def begin_stage(self):
        self.stage_id += 1
        st = self.free_eng_sets[self.stage_id]
        for e in ENGS:
            self.sem[e] = st[e]
        self.q = {e: [] for e in ENGS}
        self.cnt = {e: 0 for e in ENGS}
        self.waited = {e: {} for e in ENGS}
        self.pend = {e: ([], []) for e in ENGS}
        self.dma_next = 0
        self.stage_sw = []

    def new_dma_sem(self, sw=False):
        if sw:
            s = self.sw_sems[self.sw_next]
            self.sw_next += 1
            self.stage_sw.append(s)
            return s
        s = self.dma_sems[self.dma_next]
        self.dma_next += 1
        return s

    def _eng(self, e):
        nc = self.nc
        return {"pe": nc.tensor, "act": nc.scalar, "dve": nc.vector, "pool": nc.gpsimd, "sp": nc.sync}[e]

    def _wait(self, e, k, v):
        if self.waited[e].get(k, 0) >= v:
            return
        self.waited[e][k] = v
        h = self.sem[k]
        self.q[e].append(lambda eng, h=h, v=v: eng.wait_ge(h, v))

    def _deps(self, e, reads, writes):
        for b in reads:
            if b.w is not None:
                self._wait(e, *b.w)
        for b in writes:
            if b.w is not None:
                self._wait(e, *b.w)
            for t in b.r.values():
                self._wait(e, *t)

    def op(self, e, fn, reads=(), writes=(), inc=True, touch=()):
        self._deps(e, reads, writes)
        pr, pw = self.pend[e]
        pr.extend(reads)
        pw.extend(writes)
        pw.extend(touch)
        rec = _Rec()
        fn(rec)
        name, a, kw = rec.call
        ln = sys._getframe(1).f_lineno if DEBUG_LINES else 0

        def emit(eng, name=name, a=a, kw=kw, ln=ln):
            i = getattr(eng, name)(*a, **kw)
            if DEBUG_LINES:
                LINEMAP[i.ins.name] = ln
            return i
        if not inc:
            self.q[e].append(emit)
            return None
        self.cnt[e] += 1
        tok = (e, self.cnt[e])
        h = self.sem[e]
        self.q[e].append(lambda eng, emit=emit, h=h: emit(eng).then_inc(h, 1))
        for b in pr:
            b.r[e] = tok
        for b in pw:
            b.w = tok
            b.r = {}
        self.pend[e] = ([], [])
        return tok

    def dma(self, e, out_ap, in_ap, sem, reads=(), writes=()):
        assert not self.pend[e][0] and not self.pend[e][1]
        assert (e == "pool") == sem.key.startswith("swdma"), (e, sem.key)
        self._deps(e, reads, writes)
        sem.count += 16
        tok = (sem.key, sem.count)
        h = sem.h
        self.q[e].append(lambda eng, o=out_ap, i=in_ap, h=h: eng.dma_start(out=o, in_=i).then_inc(h, 16))
        for b in reads:
            b.r[sem.key] = tok
        for b in writes:
            b.w = tok
            b.r = {}
        return tok

    def wait_tok(self, e, tok):
        self._wait(e, *tok)

    def end_stage(self, final_toks=()):
        for e in ENGS:
            assert not self.pend[e][0] and not self.pend[e][1], e
        for e in ENGS:
            for k in ENGS:
                if k != e and self.cnt[k] > 0:
                    self._wait(e, k, self.cnt[k])
        for t in final_toks:
            self._wait("sp", *t)
        for ds in self.dma_sems[:self.dma_next] + self.stage_sw:
            if ds.count > 0:
                self._wait("sp", ds.key, ds.count)
        with self.nc.Block() as block:
            for e, reg in (("pe", block.tensor), ("act", block.scalar), ("dve", block.vector),
                           ("pool", block.gpsimd), ("sp", block.sync)):
                lst = self.q[e]

                def body(eng, lst=lst):
                    for f in lst:
                        f(eng)
                reg(body)


class Ctx:
    def __init__(self, P, es):
        self.P = P
        self.nc = P.nc
        self.es = es

    def sb(self, name, shape, dt):
        t = self.es.enter_context(self.nc.sbuf_tensor(f"{name}_{self.P.stage_id}", list(shape), dt))
        return t

    def ps(self, name, shape, dt=F32):
        t = self.es.enter_context(self.nc.psum_tensor(f"{name}_{self.P.stage_id}", list(shape), dt))
        return t


def make_ident(P, C, ident_f32, sem):
    idf = Buf(C.sb("idf", [128, 128], F32))
    idb = Buf(C.sb("idb", [128, 128], BF16))
    P.dma("sp", idf.ap[:, :], ident_f32, P.new_dma_sem(), writes=[idf])
    P.op("dve", lambda v: v.tensor_copy(idb.ap[:, :], idf.ap[:, :]), reads=[idf], writes=[idb])
    return idf, idb


class XPrep:
    def __init__(self, P, C, idb, banks, cast_eng="act"):
        self.P = P
        self.cast_eng = cast_eng
        self.xst = [Buf(C.sb(f"xst{i}", [128, D], F32)) for i in range(2)]
        self.xbf = [Buf(C.sb(f"xbf{i}", [128, D], BF16)) for i in range(2)]
        self.sem = [P.new_dma_sem() for _ in range(2)]
        self.idb = idb
        self.banks = banks
        self.n = 0

    def __call__(self, row_ap, dst):
        self.part2(self.part1(row_ap), dst)

    def part1(self, row_ap):
        P = self.P
        k = self.n % 2
        bank = self.banks[self.n % len(self.banks)]
        self.n += 1
        xst, xbf = self.xst[k], self.xbf[k]
        P.dma("sp", xst.ap[:, :], row_ap, self.sem[k], writes=[xst])
        if self.cast_eng == "act":
            P.op("act", lambda a: a.copy(xbf.ap[:, :], xst.ap[:, :]), reads=[xst], writes=[xbf])
        else:
            P.op("pool", lambda g: g.tensor_copy(xbf.ap[:, :], xst.ap[:, :]), reads=[xst], writes=[xbf])
        return (k, bank)

    def part2(self, h, dst):
        P = self.P
        k, bank = h
        xbf, idb = self.xbf[k], self.idb
        pv = bank.ap.bitcast(BF16)
        for c in range(8):
            P.op("pe", lambda pe, c=c: pe.transpose(
                pv[:, c * 128:(c + 1) * 128], xbf.ap[:, c * 128:(c + 1) * 128], idb.ap[:, :]),
                reads=[xbf, idb], writes=[bank] if c == 0 else [], inc=(c == 7))
        P.op("dve", lambda v: v.tensor_copy(dst.ap, pv[:, :].rearrange("p (c n) -> p c n", c=8)),
             reads=[bank], writes=[dst])


class LNEpi:
    def __init__(self, P, C, gain, bias, sem_const, depth=2):
        self.P = P
        self.depth = depth
        self.xres = [Buf(C.sb(f"xres{i}", [128, D], F32)) for i in range(depth)]
        self.zb = [Buf(C.sb(f"z{i}", [128, D], F32)) for i in range(depth)]
        self.ob = [Buf(C.sb(f"ob{i}", [128, D], F32)) for i in range(depth)]
        self.st = [Buf(C.sb(f"st{i}", [128, 16], F32)) for i in range(depth)]
        self.gbc = Buf(C.sb("gbc", [128, D], F32))
        self.bbc = Buf(C.sb("bbc", [128, D], F32))
        self.s_x = [P.new_dma_sem() for _ in range(depth)]
        self.s_o = [P.new_dma_sem() for _ in range(depth)]
        P.dma("sp", self.gbc.ap[:, :], gain.partition_broadcast(128), P.new_dma_sem(), writes=[self.gbc])
        P.dma("sp", self.bbc.ap[:, :], bias.partition_broadcast(128), P.new_dma_sem(), writes=[self.bbc])

    def load_x(self, k, row_ap):
        self.P.dma("sp", self.xres[k].ap[:, :], row_ap, self.s_x[k], writes=[self.xres[k]])

    def __call__(self, k, banks, dst_ap, yscale):
        self.e1(k, banks, yscale)
        self.e2(k, dst_ap)

    def e1(self, k, banks, yscale):
        self.e1a(k)
        self.e1b(k, banks, yscale)

    def e1a(self, k):
        xres = self.xres[k]
        self.P.op("act", lambda a: a.mul(xres.ap[:, :], xres.ap[:, :], ALPHA), reads=[xres], writes=[xres])

    def e1b(self, k, banks, yscale):
        P = self.P
        xres, z, o, sb_ = self.xres[k], self.zb[k], self.ob[k], self.st[k]
        for n in range(2):
            P.op("dve", lambda v, n=n: v.scalar_tensor_tensor(
                z.ap[:, n * 512:(n + 1) * 512], banks[n].ap[:, :], float(yscale),
                xres.ap[:, n * 512:(n + 1) * 512], ALU.mult, ALU.add),
                reads=[banks[n], xres], writes=[z] if n == 0 else [], inc=(n == 1))
        for n in range(2):
            P.op("dve", lambda v, n=n: v.bn_stats(sb_.ap[:, n * 6:(n + 1) * 6], z.ap[:, n * 512:(n + 1) * 512]),
                 reads=[z], writes=[sb_] if n == 0 else [], inc=(n == 1))
        P.op("dve", lambda v: v.bn_aggr(sb_.ap[:, 12:14], sb_.ap[:, 0:12]), reads=[sb_], writes=[sb_])
        P.op("act", lambda a: a.activation(sb_.ap[:, 14:15], sb_.ap[:, 13:14], AF.Sqrt, bias=LN_EPS, scale=1.0),
             reads=[sb_], writes=[sb_])

    def e2(self, k, dst_ap):
        P = self.P
        xres, z, o, sb_ = self.xres[k], self.zb[k], self.ob[k], self.st[k]
        gbc, bbc = self.gbc, self.bbc
        P.op("dve", lambda v: v.reciprocal(sb_.ap[:, 14:15], sb_.ap[:, 14:15]), reads=[sb_], writes=[sb_])
        P.op("dve", lambda v: v.scalar_tensor_tensor(
            sb_.ap[:, 15:16], sb_.ap[:, 12:13], -1.0, sb_.ap[:, 14:15], ALU.mult, ALU.mult),
            reads=[sb_], writes=[sb_])
        P.op("act", lambda a: a.activation(z.ap[:, :], z.ap[:, :], AF.Identity, bias=sb_.ap[:, 15:16],
                                           scale=sb_.ap[:, 14:15]), reads=[z, sb_], writes=[z])
        P.op("pool", lambda g: g.tensor_tensor(o.ap[:, :], z.ap[:, :], gbc.ap[:, :], ALU.mult),
             reads=[z, gbc], writes=[o])
        P.op("pool", lambda g: g.tensor_tensor(o.ap[:, :], o.ap[:, :], bbc.ap[:, :], ALU.add),
             reads=[o, bbc], writes=[o])
        P.dma("sp", dst_ap, o.ap[:, :], self.s_o[k], reads=[o])


def stage_ffn(P, x_src, x_dst, w_gu, w_down, gain, bias, ident_f32, ntiles=NT):
    P.begin_stage()
    NWG = 3
    with ExitStack() as es:
        C = Ctx(P, es)
        wd = C.sb("wd", [128, NFC, D], BF16)
        wg = [Buf(C.sb(f"wg{i}", [128, 8, 512], BF16)) for i in range(NWG)]
        xT_t = [C.sb(f"xT{i}", [128, 8, TT], BF16) for i in range(2)]
        gT_t = C.sb("gT", [128, NFC, TT], BF16)
        sg = [Buf(C.sb(f"sg{i}", [128, 512], F32)) for i in range(2)]
        pg = [Buf(C.ps(f"pg{i}", [128, 512], F32)) for i in range(4)]
        pd = [Buf(C.ps(f"pd{i}", [128, 512], F32)) for i in range(4)]
        wd_b = [Buf(wd[:, j, :]) for j in range(NFC)]
        xT = [[Buf(xT_t[i][:, :, s * 128:(s + 1) * 128]) for s in range(8)] for i in range(2)]
        gT = [[Buf(gT_t[:, j, h * 512:(h + 1) * 512]) for h in range(2)] for j in range(NFC)]

        s_const = P.new_dma_sem()
        s_wd = P.new_dma_sem(sw=True)
        s_wg = [P.new_dma_sem(sw=True) for _ in range(NWG)]
        idf, idb = make_ident(P, C, ident_f32, s_const)
        xprep = XPrep(P, C, idb, pd)
        epi = LNEpi(P, C, gain, bias, s_const)

        wguv = w_gu.rearrange("(c p) n -> p c n", p=128)

        def load_wg(step):
            j2 = step % (NFC // 2)
            slot = step % NWG
            b = wg[slot]
            P.dma("pool", b.ap[:, :, 0:256], wguv[:, :, 256 * j2:256 * j2 + 256], s_wg[slot], writes=[b])
            P.dma("pool", b.ap[:, :, 256:512], wguv[:, :, DFF + 256 * j2:DFF + 256 * j2 + 256], s_wg[slot])
            b.w = (s_wg[slot].key, s_wg[slot].count)

        nsteps = ntiles * (NFC // 2)
        xsv = x_src.rearrange("(t s p) d -> t s p d", s=8, p=128)
        xdv = x_dst.rearrange("(t s p) d -> t s p d", s=8, p=128)

        for stp in range(min(NWG, nsteps)):
            load_wg(stp)
        wdv = w_down.rearrange("(c p) n -> p c n", p=128)
        for j in range(NFC):
            P.dma("pool", wd[:, j, :], wdv[:, j, :], s_wd, writes=[wd_b[j]])
        for j in range(NFC):
            wd_b[j].w = (s_wd.key, s_wd.count)
        h0 = xprep.part1(xsv[0, 0])
        for s in range(8):
            h1 = xprep.part1(xsv[0, s + 1]) if s + 1 < 8 else None
            xprep.part2(h0, xT[0][s])
            h0 = h1

        gstep = 0
        grp = 0
        xh = {}
        pending_e2 = None
        for t in range(ntiles):
            xTt = xT[t % 2]
            for j2 in range(NFC // 2):
                if j2 == 1 and pending_e2 is not None:
                    epi.e2(*pending_e2)
                    pending_e2 = None
                slot = gstep % NWG
                wb = wg[slot]
                for jj in range(2):
                    j = 2 * j2 + jj
                    for h in range(2):
                        bg = pg[2 * (grp % 2)]
                        bu = pg[2 * (grp % 2) + 1]
                        sgb = sg[grp % 2]
                        grp += 1
                        rd = [wb] + [xTt[4 * h + q] for q in range(4)]
                        for (bank, off) in ((bg, 0), (bu, 256)):
                            for c in range(8):
                                P.op("pe", lambda pe, bank=bank, c=c, off=off, jj=jj, h=h, wb=wb, t=t: pe.matmul(
                                    bank.ap[:, :], wb.ap[:, c, off + jj * 128:off + jj * 128 + 128],
                                    xT_t[t % 2][:, c, h * 512:(h + 1) * 512], start=(c == 0), stop=(c == 7)),
                                    reads=rd if c == 0 else [], writes=[bank] if c == 0 else [], inc=(c == 7))
                        P.op("act", lambda a, sgb=sgb, bg=bg: a.activation(sgb.ap[:, :], bg.ap[:, :], AF.Silu),
                             reads=[bg], writes=[sgb])
                        dst = gT[j][h]
                        P.op("dve", lambda v, dst=dst, sgb=sgb, bu=bu: v.tensor_tensor(
                            dst.ap, sgb.ap[:, :], bu.ap[:, :], ALU.mult), reads=[sgb, bu], writes=[dst])
                gstep += 1
                if gstep + NWG - 1 < nsteps:
                    load_wg(gstep + NWG - 1)
                if t + 1 < ntiles:
                    if 2 <= j2 <= 9:
                        xprep.part2(xh.pop(j2 - 2), xT[(t + 1) % 2][j2 - 2])
                    if 1 <= j2 <= 8:
                        xh[j2 - 1] = xprep.part1(xsv[t + 1, j2 - 1])
            epi.load_x(0, xsv[t, 0])
            for s in range(8):
                k = s % 2
                if s + 1 < 8:
                    epi.load_x((s + 1) % 2, xsv[t, s + 1])
                banks = (pd[2 * k], pd[2 * k + 1])
                for n in range(2):
                    bank = banks[n]
                    for j in range(NFC):
                        P.op("pe", lambda pe, bank=bank, j=j, s=s, n=n: pe.matmul(
                            bank.ap[:, :], gT_t[:, j, s * 128:(s + 1) * 128], wd[:, j, n * 512:(n + 1) * 512],
                            start=(j == 0), stop=(j == NFC - 1)),
                            reads=[gT[j][s // 4], wd_b[j]], writes=[bank] if j == 0 else [], inc=(j == NFC - 1))
                epi.e1(k, banks, 0.5)
                if s >= 1:
                    epi.e2((s - 1) % 2, xdv[t, s - 1])
            pending_e2 = (1, xdv[t, 7])
        epi.e2(*pending_e2)
        P.end_stage()


def stage_mix(P, layer, x_src, w_in, mem, w_kv, og, consts, fbias=None, cq=None, ck=None):
    ident_f32, rope_tabs, masks_f32 = consts
    P.begin_stage()
    dil = (1, 4, 16) if layer == 0 else (1, 1, 1)
    qmem_off = 3 * MIXW if layer == 0 else 3 * MIXW + NMIX
    with ExitStack() as es:
        C = Ctx(P, es)
        pb = [Buf(C.ps(f"pb{i}", [128, 512], F32)) for i in range(8)]
        s_const = P.new_dma_sem()
        idf, idb = make_ident(P, C, ident_f32, s_const)
        xprep = XPrep(P, C, idb, [pb[6]])
        xT_t = [C.sb(f"xT{i}", [128, 8, TT], BF16) for i in range(2)]
        xT = [[Buf(xT_t[i][:, :, s * 128:(s + 1) * 128]) for s in range(8)] for i in range(2)]
        wgt = [Buf(C.sb("wgt0", [128, 8, 768], BF16))] * 2
        s_wgt = [P.new_dma_sem(sw=True)] * 2
        qkbf = [Buf(C.sb(f"qkbf{i}", [128, 512], BF16)) for i in range(2)]
        qk_all = C.sb("qkall", [70, 8, S], BF16)
        qkt = [Buf(qk_all[0:64, :, t * TT:(t + 1) * TT]) for t in range(NT)]
        v_sb = C.sb("vsb", [128, 32, 4, 65], BF16)
        vt = [Buf(v_sb[:, 8 * t:8 * t + 8, :, :]) for t in range(NT)]
        vones = Buf(v_sb[:, :, :, 64:65])
        pT = [Buf(C.sb(f"pT{i}", [128, 512], BF16)) for i in range(3)]
        ost = [Buf(C.sb(f"ost{i}", [128, 1040], F32)) for i in range(2)]
        s_ost = [P.new_dma_sem() for _ in range(2)]
        mkf = Buf(C.sb("mkf", [128, 384], F32))
        mk = Buf(C.sb("mk", [128, 3, 128], BF16))
        memT_t = C.sb("memT", [128, 8, MEML], BF16)
        memT = [Buf(memT_t[:, :, i * 128:(i + 1) * 128]) for i in range(2)]
        wkv = Buf(C.sb("wkv", [128, 8, 512], BF16))
        kmT = Buf(C.sb("kmT", [64, 4, MEML], BF16))
        vm_sb = Buf(C.sb("vmsb", [128, 2, 4, 65], BF16))
        s_wkv = P.new_dma_sem(sw=True)
        if layer == 0:
            tabg = Buf(C.sb("tabg", [128, 32, 128], F32))
            s_tab = P.new_dma_sem()
            rtmp = [[Buf(C.sb(f"rt{i}_{q}", [128, 64], F32)) for q in range(4)] for i in range(2)]
            a32b = [Buf(C.sb(f"a32_{i}", [128, 512], F32)) for i in range(2)]
        else:
            wf = Buf(C.sb("wf", [128, 8, NMIX], BF16))
            s_wf = P.new_dma_sem(sw=True)
            fst = [Buf(C.sb(f"fst{i}", [128, NMIX], F32)) for i in range(2)]
            fT = Buf(C.sb("fT", [NMIX, S], F32))
            augb = Buf(qk_all[64:70, :, :])
            s_aug = P.new_dma_sem()
            cq_b = Buf(cq)
            ck_b = Buf(ck)

        P.dma("sp", mkf.ap[:, :], masks_f32, s_const, writes=[mkf])
        P.op("dve", lambda v: v.tensor_copy(mk.ap[:, :, :], mkf.ap[:, :].rearrange("p (a b) -> p a b", a=3)),
             reads=[mkf], writes=[mk])
        M_DIAG, M_PREV, M_ALL = 0, 1, 2
        if layer == 0:
            m01 = Buf(C.sb("m01", [128, 2, 512], BF16))
            mv_ = mkf.ap[:, :].rearrange("p (a b) -> p a b", a=3)
            for hh in range(2):
                for (sel, src) in ((0, M_ALL), (1, M_PREV)):
                    P.op("dve", lambda v, hh=hh, sel=sel, src=src: v.tensor_scalar(
                        m01.ap[:, sel, hh * 256:hh * 256 + 128], mv_[:, src, :], 0.0, None, ALU.is_equal),
                        reads=[mkf], writes=[m01])
                    P.op("dve", lambda v, hh=hh, sel=sel: v.tensor_scalar(
                        m01.ap[:, sel, hh * 256 + 128:hh * 256 + 256], mv_[:, M_DIAG, :], 0.0, None,
                        ALU.is_equal), reads=[mkf], writes=[m01])
        P.op("pool", lambda g: g.memset(v_sb[:, :, :, :].rearrange("p u j e -> p (u j) e")[:, :, 64:65], 1.0),
             writes=[vones])
        P.op("pool", lambda g: g.memset(vm_sb.ap[:, :, :, :].rearrange("p u j e -> p (u j) e")[:, :, 64:65], 1.0),
             writes=[vm_sb])

        w_inv = w_in.rearrange("(c p) n -> p c n", p=128)

        def load_wgt(g):
            b = wgt[g % 2]
            if g < 3:
                for i in range(3):
                    P.dma("pool", b.ap[:, :, i * 256:(i + 1) * 256],
                          w_inv[:, :, i * MIXW + g * 256:i * MIXW + (g + 1) * 256], s_wgt[g % 2],
                          writes=[b] if i == 0 else [])
            else:
                P.dma("pool", b.ap[:, :, 0:256], w_inv[:, :, qmem_off:qmem_off + 256], s_wgt[g % 2], writes=[b])
            b.w = (s_wgt[g % 2].key, s_wgt[g % 2].count)

        load_wgt(0)
        if layer == 1:
            P.dma("pool", wf.ap[:, :, :], w_inv[:, :, 3 * MIXW:3 * MIXW + NMIX], s_wf, writes=[wf])
        P.dma("pool", wkv.ap[:, :, :], w_kv.rearrange("(c p) n -> p c n", p=128), s_wkv, writes=[wkv])

        def proj_mm(g, u, xTbuf_t, xTb, s, nslots, with_v, with_f, rope_ap):
            wb = wgt[g % 2]
            A = pb[u % 3]
            Bk = pb[3 + u % 3]
            ncol = nslots * 64
            for c in range(8):
                P.op("pe", lambda pe, c=c: pe.matmul(
                    A.ap[:, 0:ncol], xTbuf_t[:, c, s * 128:(s + 1) * 128], wb.ap[:, c, 0:ncol],
                    start=(c == 0), stop=(c == 7)),
                    reads=[xTb, wb] if c == 0 else [], writes=[A] if c == 0 else [], inc=(c == 7))
            if with_v:
                for c in range(8):
                    P.op("pe", lambda pe, c=c: pe.matmul(
                        Bk.ap[:, 0:256], xTbuf_t[:, c, s * 128:(s + 1) * 128], wb.ap[:, c, 512:768],
                        start=(c == 0), stop=(c == 7), skip_group_check=True),
                        reads=[xTb, wb] if c == 0 else [], writes=[Bk] if c == 0 else [],
                        inc=(c == 7 and not with_f))
            if with_f:
                for c in range(8):
                    P.op("pe", lambda pe, c=c: pe.matmul(
                        Bk.ap[:, 256:256 + NMIX], xTbuf_t[:, c, s * 128:(s + 1) * 128], wf.ap[:, c, :],
                        start=False, stop=(c == 7), skip_group_check=True), reads=[wf] if c == 0 else [], inc=(c == 7))

        def proj_fin(g, u, xTbuf_t, xTb, s, nslots, with_v, with_f, rope_ap):
            wb = wgt[g % 2]
            A = pb[u % 3]
            Bk = pb[3 + u % 3]
            ncol = nslots * 64
            qb = qkbf[u % 2]
            if rope_ap is not None:
                tb = tabg
                a32 = a32b[u % 2]
                P.op("act", lambda a: a.copy(a32.ap[:, :], A.ap[:, :]), reads=[A], writes=[a32])
                Av = a32.ap[:, :].rearrange("p (j d) -> p j d", j=8)
                qv = qb.ap[:, :].rearrange("p (j d) -> p j d", j=8)
                cosv = tb.ap[:, u, 0:64].rearrange("p (j d) -> p j d", j=8)
                sinv = tb.ap[:, u, 64:128].rearrange("p (j d) -> p j d", j=8)
                rt = rtmp[u % 2]
                rv = [r_.ap[:, :].rearrange("p (j d) -> p j d", j=8) for r_ in rt]
                P.op("pool", lambda g_: g_.tensor_copy(qv[:, :, 16:64], Av[:, :, 16:64]), reads=[a32], writes=[qb])
                P.op("dve", lambda v: v.tensor_tensor(rv[0], Av[:, :, 0:8], cosv, ALU.mult),
                     reads=[a32, tb], writes=[rt[0]], inc=False)
                P.op("dve", lambda v: v.tensor_tensor(rv[1], Av[:, :, 8:16], sinv, ALU.mult),
                     reads=[a32, tb], writes=[rt[1]], inc=False)
                P.op("dve", lambda v: v.tensor_tensor(rv[2], Av[:, :, 8:16], cosv, ALU.mult),
                     reads=[a32, tb], writes=[rt[2]], inc=False)
                P.op("dve", lambda v: v.tensor_tensor(rv[3], Av[:, :, 0:8], sinv, ALU.mult),
                     reads=[a32, tb], writes=[rt[3]])
                P.op("pool", lambda g_: g_.tensor_tensor(qv[:, :, 0:8], rv[0], rv[1], ALU.subtract),
                     reads=[rt[0], rt[1]], writes=[qb], inc=False)
                P.op("pool", lambda g_: g_.tensor_tensor(qv[:, :, 8:16], rv[2], rv[3], ALU.add),
                     reads=[rt[2], rt[3]], writes=[qb])
            else:
                P.op("act", lambda a: a.copy(qb.ap[:, 0:ncol], A.ap[:, 0:ncol]), reads=[A], writes=[qb])
            if with_v:
                P.op("act", lambda a: a.copy(v_sb[:, u, :, 0:64], Bk.ap[:, 0:256].rearrange("p (j d) -> p j d", j=4)),
                     reads=[Bk], writes=[vt[u // 8]])
            if with_f:
                fs = fst[u % 2]
                P.op("act", lambda a: a.copy(fs.ap[:, :], Bk.ap[:, 256:256 + NMIX]), reads=[Bk], writes=[fs])
                P.op("pe", lambda pe: pe.transpose(Bk.ap[0:NMIX, 384:512], fs.ap[:, :], idf.ap[:, :]),
                     reads=[fs, idf], writes=[Bk])
                P.op("dve", lambda v: v.tensor_copy(fT.ap[:, u * 128:(u + 1) * 128], Bk.ap[0:NMIX, 384:512]),
                     reads=[Bk], writes=[fT])
            tq = pb[7]
            tqv = tq.ap.bitcast(BF16)
            for sl in range(nslots):
                P.op("pe", lambda pe, sl=sl: pe.transpose(
                    tqv[0:64, sl * 128:(sl + 1) * 128], qb.ap[:, sl * 64:(sl + 1) * 64], idb.ap[:, :]),
                    reads=[qb, idb], writes=[tq] if sl == 0 else [], inc=(sl == nslots - 1))
            P.op("dve", lambda v: v.tensor_copy(
                qk_all[0:64, 0:nslots, u * 128:(u + 1) * 128],
                tqv[0:64, 0:nslots * 128].rearrange("p (j n) -> p j n", j=nslots)),
                reads=[tq], writes=[qkt[u // 8]])

        def proj_group(g):
            r = dil[g] if g < 3 else 1
            L = S // r
            nslots = 8 if g < 3 else 4
            xr = x_src.rearrange("(n r) d -> r n d", r=r)
            if layer == 0 and g < 3:
                P.dma("sp", tabg.ap[:, :, :], rope_tabs[g].rearrange("(u p) f -> p u f", p=128), s_tab, writes=[tabg])

            def rows(u):
                m0 = u * 128
                return xr[m0 // L, (m0 % L):(m0 % L) + 128, :]

            if g == 0:
                for s in range(8):
                    xprep(rows(s), xT[0][s])
            rope_ap = True if (layer == 0 and g < 3) else None
            xh = {}

            def args(u):
                t, s = u // 8, u % 8
                return (g, u, xT_t[t % 2], xT[t % 2][s], s, nslots, g < 3, (layer == 1 and g == 0), rope_ap)

            proj_mm(*args(0))
            proj_mm(*args(1))
            for u in range(32):
                if u + 7 in xh:
                    v = u + 7
                    xprep.part2(xh.pop(v), xT[(v // 8) % 2][v % 8])
                if u + 8 < 32:
                    xh[u + 8] = xprep.part1(rows(u + 8))
                if u + 2 < 32:
                    proj_mm(*args(u + 2))
                proj_fin(*args(u))
            if g + 1 < 4:
                load_wgt(g + 1)
                r2 = dil[g + 1] if g + 1 < 3 else 1
                xr2 = x_src.rearrange("(n r) d -> r n d", r=r2)
                L2 = S // r2
                for s in range(8):
                    m0 = s * 128
                    xprep(xr2[m0 // L2, (m0 % L2):(m0 % L2) + 128, :], xT[0][s])

        def forget_prep():
            CH = 512
            nb_ = Buf(C.sb("negb", [NMIX, 1], F32))
            ones = Buf(C.sb("ones12", [NMIX, CH], F32))
            onesb = Buf(C.sb("ones12b", [NMIX, CH], BF16))
            e1s = [Buf(C.sb(f"e1_{i}", [NMIX, CH], F32)) for i in range(2)]
            cc = [Buf(C.sb(f"cc{i}", [NMIX, CH], F32)) for i in range(2)]
            c8s = [Buf(C.sb(f"c8_{i}", [NMIX, CH], F32)) for i in range(2)]
            pc = [Buf(C.sb(f"pc{i}", [NMIX, CH], BF16)) for i in range(3)]
            npc = [Buf(C.sb(f"npc{i}", [NMIX, CH], BF16)) for i in range(3)]
            s_c = P.new_dma_sem()
            s_cs = P.new_dma_sem()
            P.dma("sp", nb_.ap[:, :], fbias.rearrange("(h o) -> h o", o=1), s_c, writes=[nb_])
            P.op("act", lambda a: a.mul(nb_.ap[:, :], nb_.ap[:, :], -1.0), reads=[nb_], writes=[nb_])
            P.op("pool", lambda g_: g_.memset(ones.ap[:, :], 1.0), writes=[ones])
            P.op("pool", lambda g_: g_.memset(onesb.ap[:, :], 1.0), writes=[onesb])
            for ci in range(S // CH):
                sl = slice(ci * CH, (ci + 1) * CH)
                e1 = e1s[ci % 2]
                c8 = c8s[ci % 2]
                P.op("act", lambda a, sl=sl: a.activation(e1.ap[:, :], fT.ap[:, sl], AF.Exp, bias=nb_.ap[:, 0:1],
                                                          scale=-1.0), reads=[fT, nb_], writes=[e1])
                P.op("act", lambda a: a.activation(e1.ap[:, :], e1.ap[:, :], AF.Ln, bias=1.0, scale=1.0),
                     reads=[e1], writes=[e1])
                cur, prev = cc[ci % 2], cc[(ci + 1) % 2]
                init = 0.0 if ci == 0 else prev.ap[:, CH - 1:CH]
                P.op("dve", lambda v, cur=cur, init=init: v.tensor_tensor_scan(
                    cur.ap[:, :], ones.ap[:, :], e1.ap[:, :], init, ALU.mult, ALU.subtract),
                    reads=[ones, e1, prev], writes=[cur])
                P.op("act", lambda a, cur=cur: a.mul(c8.ap[:, :], cur.ap[:, :], 1.0 / SCALE), reads=[cur], writes=[c8])
                for i in range(3):
                    P.op("dve", lambda v, i=i: v.tensor_copy(pc[i].ap[:, :], c8.ap[:, :]), reads=[c8], writes=[pc[i]])
                    if i < 2:
                        P.op("dve", lambda v, i=i: v.tensor_tensor(c8.ap[:, :], c8.ap[:, :], pc[i].ap[:, :],
                                                                    ALU.subtract), reads=[c8, pc[i]], writes=[c8])
                for i in range(3):
                    P.op("act", lambda a, i=i: a.mul(npc[i].ap[:, :], pc[i].ap[:, :], -1.0),
                         reads=[pc[i]], writes=[npc[i]])
                for i in range(3):
                    P.dma("sp", cq[:, i, sl], pc[i].ap[:, :], s_cs, reads=[pc[i]], writes=[cq_b] if i == 0 else [])
                    P.dma("sp", ck[:, 3 + i, sl], npc[i].ap[:, :], s_cs, reads=[npc[i]], writes=[ck_b] if i == 0 else [])
                    P.dma("sp", cq[:, 3 + i, sl], onesb.ap[:, :], s_cs, reads=[onesb])
                    P.dma("sp", ck[:, i, sl], onesb.ap[:, :], s_cs, reads=[onesb])
                cq_b.w = (s_cs.key, s_cs.count)
                ck_b.w = (s_cs.key, s_cs.count)
                for b_ in pc + npc + [onesb]:
                    b_.r[s_cs.key] = (s_cs.key, s_cs.count)

        def load_aug(g):
            for j in range(4):
                P.dma("sp", qk_all[64:70, j, :], cq[4 * g + j], s_aug, reads=[cq_b], writes=[augb] if j == 0 else [])
                P.dma("sp", qk_all[64:70, 4 + j, :], ck[4 * g + j], s_aug, reads=[ck_b])
            augb.w = (s_aug.key, s_aug.count)

        cnt = {"s": 0, "o": 0, "st": 0}

        def flush_o(ob, ncol, dst_ap, view=None):
            k = cnt["st"] % 2
            cnt["st"] += 1
            stg = ost[k]
            P.op("dve", lambda v: v.tensor_copy(stg.ap[:, 0:ncol], ob.ap[:, 0:ncol]), reads=[ob], writes=[stg])
            return stg, k

        def attn_banded(g):
            r = dil[g]
            nb = 32 // r
            ogv = og[g].rearrange("(n r) f -> r n f", r=r)
            items = [(B, hp) for B in range(32) for hp in range(2)]
            st_ = {}

            def emit_score(idx):
                B, hp = items[idx]
                b = B % nb
                sb_ = pb[idx % 3]
                pt = pT[idx % 3]
                tiles_q = [qkt[B // 8]] + ([qkt[(B - 1) // 8]] if b > 0 else [])
                for hh in range(2):
                    j = hp * 2 + hh
                    reg = sb_.ap[:, hh * 256:hh * 256 + 128]
                    first = (hh == 0)
                    KP = (B - 1) if b > 0 else B
                    P.op("pe", lambda pe: pe.matmul(
                        reg, qk_all[0:64, 4 + j, KP * 128:(KP + 1) * 128], qk_all[0:64, j, B * 128:(B + 1) * 128],
                        start=first, stop=True, skip_group_check=True),
                        reads=tiles_q, writes=[sb_] if first else [], inc=False)
                    reg2 = sb_.ap[:, hh * 256 + 128:hh * 256 + 256]
                    P.op("pe", lambda pe: pe.matmul(
                        reg2, qk_all[0:64, 4 + j, B * 128:(B + 1) * 128], qk_all[0:64, j, B * 128:(B + 1) * 128],
                        start=False, stop=True, skip_group_check=True), reads=tiles_q, inc=(hh == 1))
                P.op("act", lambda a: a.activation(pt.ap[:, :], sb_.ap[:, :], AF.Exp, scale=SCALE),
                     reads=[sb_], writes=[pt])
                msel = 1 if b > 0 else 0
                P.op("dve", lambda v: v.tensor_tensor(pt.ap[:, :], pt.ap[:, :], m01.ap[:, msel, :], ALU.mult),
                     reads=[pt, m01], writes=[pt])

            def emit_pv(idx):
                B, hp = items[idx]
                b = B % nb
                c = B // nb
                pt = pT[idx % 3]
                if hp == 0:
                    st_["ob"] = pb[3 + cnt["o"] % 2]
                    cnt["o"] += 1
                ob = st_["ob"]
                tiles_v = [vt[B // 8]] + ([vt[(B - 1) // 8]] if b > 0 else [])
                for hh in range(2):
                    j = hp * 2 + hh
                    oreg = ob.ap[:, j * 65:(j + 1) * 65]
                    firstw = (hp == 0 and hh == 0)
                    if b > 0:
                        P.op("pe", lambda pe: pe.matmul(
                            oreg, pt.ap[:, hh * 256:hh * 256 + 128], v_sb[:, B - 1, j, :], start=firstw, stop=False,
                            skip_group_check=True),
                            reads=[pt, vones] + tiles_v, writes=[ob] if firstw else [], inc=False)
                    P.op("pe", lambda pe: pe.matmul(
                        oreg, pt.ap[:, hh * 256 + 128:hh * 256 + 256], v_sb[:, B, j, :],
                        start=(b == 0 and firstw), stop=True, skip_group_check=True),
                        reads=[pt, vones] + tiles_v, writes=[ob] if (firstw and b == 0) else [],
                        touch=[ob], inc=(hh == 1))
                if hp == 1:
                    stg, k = flush_o(ob, 260, None)
                    P.dma("sp", ogv[c, b * 128:(b + 1) * 128, :], stg.ap[:, 0:260], s_ost[k], reads=[stg])

            LA = 2
            for idx in range(len(items) + LA):
                if idx < len(items):
                    emit_score(idx)
                if idx - LA >= 0:
                    emit_pv(idx - LA)

        def attn_qtiles(g, causal, K, kT_fn, v_fn, kdeps_fn):
            ogv = og[g].rearrange("(t i p) (j e) -> t p i j e", i=4, p=128, j=4)
            items = []
            for T in range(8):
                nkb = (4 * T + 4) if causal else 2
                for j in range(4):
                    for KB in range(nkb):
                        items.append((T, j, KB, nkb))
            st_ = {}

            def emit_score(idx):
                T, j, KB, nkb = items[idx]
                sb_ = pb[idx % 3]
                pt = pT[idx % 3]
                jd = KB - 4 * T if causal else -1
                kdeps = kdeps_fn(KB)
                qdeps = [qkt[T // 2]] + ([augb] if (layer == 1 and causal) else [])
                if jd < 0:
                    c0 = 0
                    P.op("pe", lambda pe: pe.matmul(
                        sb_.ap[:, 0:512], kT_fn(j, KB), qk_all[0:K, j, T * 512:(T + 1) * 512],
                        start=True, stop=True), reads=kdeps + qdeps, writes=[sb_])
                else:
                    c0 = jd * 128
                    reg = sb_.ap[:, c0:c0 + 128]
                    P.op("pe", lambda pe: pe.matmul(reg, idb.ap[:, :], mk.ap[:, M_DIAG, :],
                                                     start=True, stop=False, skip_group_check=True),
                         reads=[idb, mk], writes=[sb_], inc=False)
                    P.op("pe", lambda pe: pe.matmul(
                        reg, kT_fn(j, KB), qk_all[0:K, j, T * 512 + c0:T * 512 + c0 + 128],
                        start=False, stop=True, skip_group_check=True), reads=kdeps + qdeps, inc=(jd == 3))
                    if jd < 3:
                        P.op("pe", lambda pe: pe.matmul(
                            sb_.ap[:, c0 + 128:512], kT_fn(j, KB),
                            qk_all[0:K, j, T * 512 + c0 + 128:(T + 1) * 512], start=False, stop=True,
                            skip_group_check=True))
                P.op("act", lambda a: a.activation(
                    pt.ap[:, c0:512], sb_.ap[:, c0:512], AF.Exp, scale=SCALE), reads=[sb_], writes=[pt])

            def emit_pv(idx):
                T, j, KB, nkb = items[idx]
                pt = pT[idx % 3]
                jd = KB - 4 * T if causal else -1
                kdeps = kdeps_fn(KB)
                if KB == 0:
                    st_["ob"] = pb[3 + cnt["o"] % 2]
                    cnt["o"] += 1
                    if j == 0:
                        st_["k"] = cnt["st"] % 2
                        cnt["st"] += 1
                ob = st_["ob"]
                stg = ost[st_["k"]]
                stv = stg.ap[:, :].rearrange("p (i j e) -> p i j e", i=4, j=4)
                i0_ = max(jd, 0)
                for i in range(i0_, 4):
                    last_kb = (4 * T + i) if causal else 1
                    P.op("pe", lambda pe: pe.matmul(
                        ob.ap[:, i * 65:(i + 1) * 65], pt.ap[:, i * 128:(i + 1) * 128], v_fn(j, KB),
                        start=(KB == 0 and i == 0), stop=(KB == last_kb), skip_group_check=True),
                        reads=[pt] + kdeps, writes=[ob] if (KB == 0 and i == 0) else [], touch=[ob], inc=(i == 3))
                if KB == nkb - 1:
                    P.op("dve", lambda v: v.tensor_copy(
                        stv[:, :, j, :], ob.ap[:, 0:260].rearrange("p (i e) -> p i e", i=4)),
                        reads=[ob], writes=[stg])
                    if j == 3:
                        P.dma("sp", ogv[T], stv, s_ost[st_["k"]], reads=[stg])

            LA = 2
            for idx in range(len(items) + LA):
                if idx < len(items):
                    emit_score(idx)
                if idx - LA >= 0:
                    emit_pv(idx - LA)

        def mem_kv():
            mv = mem.rearrange("(s p) d -> s p d", p=128)
            for i in range(2):
                xprep(mv[i], memT[i])
            for hp in range(2):
                bk = pb[hp]
                for hh in range(2):
                    j = hp * 2 + hh
                    for c in range(8):
                        P.op("pe", lambda pe, c=c, j=j, hh=hh, bk=bk: pe.matmul(
                            bk.ap[0:64, hh * 256:(hh + 1) * 256], wkv.ap[:, c, j * 64:(j + 1) * 64],
                            memT_t[:, c, :], start=(c == 0 and hh == 0), stop=(c == 7), skip_group_check=True),
                            reads=[wkv] + memT if c == 0 else [], writes=[bk] if (c == 0 and hh == 0) else [],
                            inc=(c == 7 and hh == 1))
                P.op("act", lambda a, bk=bk, hp=hp: a.copy(
                    kmT.ap[:, 2 * hp:2 * hp + 2, :], bk.ap[0:64, :].rearrange("p (j n) -> p j n", j=2)),
                    reads=[bk], writes=[kmT])
            for mb in range(2):
                bk = pb[2 + mb]
                for c in range(8):
                    P.op("pe", lambda pe, c=c, mb=mb, bk=bk: pe.matmul(
                        bk.ap[:, 0:256], memT_t[:, c, mb * 128:(mb + 1) * 128], wkv.ap[:, c, 256:512],
                        start=(c == 0), stop=(c == 7)),
                        reads=[wkv] + memT if c == 0 else [], writes=[bk] if c == 0 else [], inc=(c == 7))
                P.op("act", lambda a, bk=bk, mb=mb: a.copy(
                    vm_sb.ap[:, mb, :, 0:64], bk.ap[:, 0:256].rearrange("p (j d) -> p j d", j=4)),
                    reads=[bk], writes=[vm_sb])

        upto = DEBUG_UPTO
        for g in range(4):
            if upto < 2 or (upto in (2, 3) and g > 0):
                break
            proj_group(g)
            if upto == 2:
                break
            if layer == 1 and g == 0:
                forget_prep()
            if g < 3:
                if layer == 0:
                    attn_banded(g)
                else:
                    load_aug(g)
                    attn_qtiles(g, True, 70,
                                lambda j, KB: qk_all[0:70, 4 + j, KB * 128:(KB + 1) * 128],
                                lambda j, KB: v_sb[:, KB, j, :],
                                lambda KB: [qkt[KB // 8], vt[KB // 8], vones, augb])
            else:
                mem_kv()
                attn_qtiles(g, False, 64,
                            lambda j, KB: kmT.ap[:, j, KB * 128:(KB + 1) * 128],
                            lambda j, KB: vm_sb.ap[:, KB, j, :],
                            lambda KB: [kmT, vm_sb])
        P.end_stage()


def stage_out(P, layer, x_src, x_dst, og, w_out, gain, bias, ident_f32):
    P.begin_stage()
    nch = 4 if layer == 0 else 8
    NB = 4
    NPB = 3
    with ExitStack() as es:
        C = Ctx(P, es)
        pb = [Buf(C.ps(f"pb{i}", [128, 512], F32)) for i in range(8)]
        s_const = P.new_dma_sem()
        idf, idb = make_ident(P, C, ident_f32, s_const)
        epi = LNEpi(P, C, gain, bias, s_const, depth=NB)
        wo = Buf(C.sb("wo", [128, nch, D], BF16))
        s_wo = P.new_dma_sem(sw=True)
        P.dma("pool", wo.ap[:, :, :], w_out.rearrange("(c p) n -> p c n", p=128), s_wo, writes=[wo])
        ogt = [[Buf(C.sb(f"ogt{k}_{i}", [128, 260], F32)) for i in range(4)] for k in range(NB)]
        s_og = [P.new_dma_sem() for _ in range(NB)]
        rec = [[Buf(C.sb(f"rec{k}_{i}", [128, 4], F32)) for i in range(4)] for k in range(NB)]
        cat = [Buf(C.sb(f"cat{k}", [128, nch * 128], BF16)) for k in range(NB)]
        catT = [Buf(C.sb(f"catT{k}", [128, nch, 128], BF16)) for k in range(NB)]
        xsv = x_src.rearrange("(u p) d -> u p d", p=128)
        xdv = x_dst.rearrange("(u p) d -> u p d", p=128)
        ogv = [o.rearrange("(u p) f -> u p f", p=128) for o in og]
        NU = S // 128

        def loads(u):
            k = u % NB
            epi.load_x(k, xsv[u])
            for i in range(4):
                P.dma("sp", ogt[k][i].ap[:, :], ogv[i][u], s_og[k], writes=[ogt[k][i]])
            for i in range(4):
                ogt[k][i].w = (s_og[k].key, s_og[k].count)
                ogt[k][i].r = {}

        st_banks = {}

        def stage_a(u):
            k = u % NB
            if layer == 0:
                a0, a1, a2 = ogt[k][0], ogt[k][1], ogt[k][2]
                P.op("dve", lambda g_: g_.tensor_tensor(a0.ap[:, :], a0.ap[:, :], a1.ap[:, :], ALU.add),
                     reads=[a0, a1], writes=[a0])
                P.op("dve", lambda g_: g_.tensor_tensor(a0.ap[:, :], a0.ap[:, :], a2.ap[:, :], ALU.add),
                     reads=[a0, a2], writes=[a0])
                parts = [(ogt[k][0], 0), (ogt[k][3], 256)]
            else:
                parts = [(ogt[k][i], 256 * i) for i in range(4)]
            ct = cat[k]
            for pi, (src, col) in enumerate(parts):
                rc = rec[k][pi]
                sv = src.ap[:, :].rearrange("p (j e) -> p j e", j=4)
                P.op("dve", lambda v: v.reciprocal(rc.ap[:, :], sv[:, :, 64]), reads=[src], writes=[rc])
                if layer == 0:
                    for hh in range(4):
                        P.op("act", lambda a, hh=hh: a.activation(
                            ct.ap[:, col + 64 * hh:col + 64 * hh + 64], sv[:, hh, 0:64], AF.Copy,
                            scale=rc.ap[:, hh:hh + 1]),
                            reads=[src, rc], writes=[ct] if (pi == 0 and hh == 0) else [], touch=[ct], inc=(hh == 3))
                else:
                    P.op("dve", lambda v: v.tensor_tensor(
                        ct.ap[:, col:col + 256].rearrange("p (j d) -> p j d", j=4), sv[:, :, 0:64],
                        rc.ap[:, :].unsqueeze(2).broadcast_to([128, 4, 64]), ALU.mult),
                        reads=[src, rc], writes=[ct] if pi == 0 else [], touch=[ct])
            tb = pb[6 + u % 2]
            tbv = tb.ap.bitcast(BF16)
            for ch in range(nch):
                P.op("pe", lambda pe, ch=ch: pe.transpose(
                    tbv[:, ch * 128:(ch + 1) * 128], ct.ap[:, ch * 128:(ch + 1) * 128], idb.ap[:, :]),
                    reads=[ct, idb], writes=[tb] if ch == 0 else [], inc=(ch == nch - 1))
            cT = catT[k]
            P.op("act", lambda a: a.copy(cT.ap[:, :, :], tbv[:, 0:nch * 128].rearrange("p (c n) -> p c n", c=nch)),
                 reads=[tb], writes=[cT])
            kb = u % NPB
            banks = (pb[2 * kb], pb[2 * kb + 1])
            st_banks[u] = banks
            for n in range(2):
                bank = banks[n]
                for ch in range(nch):
                    P.op("pe", lambda pe, bank=bank, ch=ch, n=n: pe.matmul(
                        bank.ap[:, :], cT.ap[:, ch, :], wo.ap[:, ch, n * 512:(n + 1) * 512],
                        start=(ch == 0), stop=(ch == nch - 1)),
                        reads=[cT, wo] if ch == 0 else [], writes=[bank] if ch == 0 else [], inc=(ch == nch - 1))

        for u in range(min(NB, NU)):
            loads(u)
        for step in range(NU + 3):
            ua, u1, u2 = step, step - 1, step - 2
            if 0 <= u2 < NU:
                epi.e2(u2 % NB, xdv[u2])
                if u2 + NB < NU:
                    loads(u2 + NB)
            if 0 <= u1 < NU:
                epi.e1a(u1 % NB)
            if ua < NU:
                stage_a(ua)
            if 0 <= u1 < NU:
                epi.e1b(u1 % NB, st_banks[u1], 1.0)
        P.end_stage()


def host_constants():
    ident = np.eye(128, dtype=np.float32)
    NEG = np.float32(-1.0e5)
    k = np.arange(128)[:, None]
    q = np.arange(128)[None, :]
    m_diag = np.where(k <= q, 0.0, NEG).astype(np.float32)
    m_prev = np.where(k >= q, 0.0, NEG).astype(np.float32)
    m_all = np.full((128, 128), NEG, np.float32)
    masks = np.concatenate([m_diag, m_prev, m_all], axis=1)
    pos = np.arange(S, dtype=np.float32)
    inv_freq = (1.0 / (np.float32(500000.0) ** (np.arange(8, dtype=np.float32) / np.float32(8)))).astype(np.float32)
    ang = (pos[:, None] * inv_freq[None, :]).astype(np.float32)
    cos = np.cos(ang).astype(np.float32)
    sin = np.sin(ang).astype(np.float32)
    tabs = np.zeros((3, S, 128), np.float32)
    for g, r in enumerate((1, 4, 16)):
        L = S // r
        m = np.arange(S)
        tok = (m // L) + r * (m % L)
        tabs[g, :, 0:64] = np.tile(cos[tok], (1, 8))
        tabs[g, :, 64:128] = np.tile(sin[tok], (1, 8))
    return ident, masks, tabs


def build_program(stages=None):
    nc = bass.Bass("TRN2", target_bir_lowering=False)

    def din(name, shape):
        return nc.dram_tensor(name, list(shape), F32, kind="ExternalInput").ap()

    x = din("x", [S, D])
    mem = din("mem", [MEML, D])
    f1gu = din("ffn1_w_gate_up", [DEPTH, D, 2 * DFF])
    f1d = din("ffn1_w_down", [DEPTH, DFF, D])
    f2gu = din("ffn2_w_gate_up", [DEPTH, D, 2 * DFF])
    f2d = din("ffn2_w_down", [DEPTH, DFF, D])
    lng = din("ln_gain", [DEPTH, 3, D])
    lnb = din("ln_bias", [DEPTH, 3, D])
    wkv = din("mem_w_kv", [DEPTH, D, 2 * MEMW])
    awin = din("a_w_in", [1, D, A_IN_W])
    awout = din("a_w_out", [1, 4 * HD + MEMW, D])
    bwin = din("b_w_in", [1, D, B_IN_W])
    bfb = din("b_forget_bias", [1, NMIX])
    bwout = din("b_w_out", [1, MIXW + MEMW, D])
    ident = din("c_ident", [128, 128])
    masks = din("c_masks", [128, 384])
    tabs = din("c_rope", [3, S, 128])
    out = nc.dram_tensor("out", [S, D], F32, kind="ExternalOutput").ap()
    xa = nc.dram_tensor("scr_xa", [S, D], F32, kind="Internal").ap()
    xb = nc.dram_tensor("scr_xb", [S, D], F32, kind="Internal").ap()
    og = [nc.dram_tensor(f"scr_og{i}", [S, 260], F32, kind="Internal").ap() for i in range(4)]
    cq = nc.dram_tensor("scr_cq", [NMIX, 6, S], BF16, kind="Internal").ap()
    ck = nc.dram_tensor("scr_ck", [NMIX, 6, S], BF16, kind="Internal").ap()
    consts = (ident, tabs, masks)
    with ExitStack() as es:
        P = Prog(nc, es)
        allst = [
            lambda: stage_ffn(P, x, xa, f1gu[0], f1d[0], lng[0, 0], lnb[0, 0], ident),
            lambda: stage_mix(P, 0, xa, awin[0], mem, wkv[0], og, consts),
            lambda: stage_out(P, 0, xa, xb, og, awout[0], lng[0, 1], lnb[0, 1], ident),
            lambda: stage_ffn(P, xb, xa, f2gu[0], f2d[0], lng[0, 2], lnb[0, 2], ident),
            lambda: stage_ffn(P, xa, xb, f1gu[1], f1d[1], lng[1, 0], lnb[1, 0], ident),
            lambda: stage_mix(P, 1, xb, bwin[0], mem, wkv[1], og, consts, fbias=bfb[0], cq=cq, ck=ck),
            lambda: stage_out(P, 1, xb, xa, og, bwout[0], lng[1, 1], lnb[1, 1], ident),
            lambda: stage_ffn(P, xa, out, f2gu[1], f2d[1], lng[1, 2], lnb[1, 2], ident),
        ]
        for i, st in enumerate(allst):
            if stages is None or i in stages:
                st()
    return nc


def kernel(x, mem, ffn1_w_gate_up, ffn1_w_down, ffn2_w_gate_up, ffn2_w_down, ln_gain, ln_bias, mem_w_kv,
           a_w_in, a_w_out, b_w_in, b_forget_bias, b_w_out):
    ncores = 8
    ident, masks, tabs = host_constants()
    f32 = lambda a: np.ascontiguousarray(np.asarray(a, dtype=np.float32))
    shared = {
        "ffn1_w_gate_up": f32(ffn1_w_gate_up), "ffn1_w_down": f32(ffn1_w_down),
        "ffn2_w_gate_up": f32(ffn2_w_gate_up), "ffn2_w_down": f32(ffn2_w_down),
        "ln_gain": f32(ln_gain), "ln_bias": f32(ln_bias), "mem_w_kv": f32(mem_w_kv),
        "a_w_in": f32(a_w_in), "a_w_out": f32(a_w_out), "b_w_in": f32(b_w_in),
        "b_forget_bias": f32(b_forget_bias), "b_w_out": f32(b_w_out),
        "c_ident": ident, "c_masks": masks, "c_rope": tabs,
    }
    xs = f32(x)
    ms = f32(mem)
    in_maps = []
    for b in range(ncores):
        d = dict(shared)
        d["x"] = xs[b]
        d["mem"] = ms[b]
        in_maps.append(d)
    nc = build_program()
    res = run_bass_kernel_spmd(nc, in_maps, core_ids=list(range(ncores)))
    return np.stack([np.asarray(r["out"], dtype=np.float32) for r in res.results], axis=0)
```

```python
import sys
import numpy as np
from contextlib import ExitStack

import concourse.bass as bass
import concourse.mybir as mybir
from concourse.bass_utils import run_bass_kernel_spmd

F32 = mybir.dt.float32
BF16 = mybir.dt.bfloat16
AF = mybir.ActivationFunctionType
ALU = mybir.AluOpType

D = 1024
S = 4096
DFF = 2816
NFC = DFF // 128
DEPTH = 2
ALPHA = float((2 * DEPTH) ** 0.25)
LN_EPS = 1e-5
HD = 64
NMIX = 12
NMEMH = 4
MEML = 256
MIXW = 768
MEMW = 256
A_IN_W = 3 * MIXW + MEMW
B_IN_W = 3 * MIXW + NMIX + MEMW
SCALE = HD ** -0.5
TT = 1024
NT = S // TT

ENGS = ("pe", "act", "dve", "pool", "sp")
DEBUG_LINES = False
DEBUG_UPTO = 99
DEBUG_PROJ = 0
LINEMAP = {}


class Buf:
    def __init__(self, ap, name=""):
        self.ap = ap
        self.name = name
        self.w = None
        self.r = {}

    def __getitem__(self, k):
        return self.ap[k]


class DmaSem:
    def __init__(self, key, handle):
        self.key = key
        self.h = handle
        self.count = 0


class _Rec:
    def __init__(self):
        self.call = None

    def __getattr__(self, name):
        def f(*a, **kw):
            assert self.call is None
            self.call = (name, a, kw)
            return self
        return f


class Prog:
    def __init__(self, nc, es, n_dma_sems=18, n_stage_sets=8, n_sw_sems=26):
        self.nc = nc
        self.sem = {}
        self.q = {e: [] for e in ENGS}
        self.cnt = {e: 0 for e in ENGS}
        self.waited = {e: {} for e in ENGS}
        self.pend = {e: ([], []) for e in ENGS}
        self.free_eng_sets = []
        for i in range(n_stage_sets):
            st = {}
            for e in ENGS:
                st[e] = es.enter_context(nc.semaphore(f"s_{e}_{i}"))
            self.free_eng_sets.append(st)
        self.dma_sems = []
        for i in range(n_dma_sems):
            k = f"dma{i}"
            h = es.enter_context(nc.semaphore(f"s_{k}"))
            self.sem[k] = h
            self.dma_sems.append(DmaSem(k, h))
        self.sw_sems = []
        for i in range(n_sw_sems):
            k = f"swdma{i}"
            h = es.enter_context(nc.semaphore(f"s_{k}"))
            self.sem[k] = h
            self.sw_sems.append(DmaSem(k, h))
        self.sw_next = 0
        self.stage_sw = []
        self.dma_next = 0
        self.stage_id = -1

    def begin_stage(self):
        self.stage_id += 1
        st = self.free_eng_sets[self.stage_id]
        for e in ENGS:
            self.sem[e] = st[e]
        self.q = {e: [] for e in ENGS}
        self.cnt = {e: 0 for e in ENGS}
        self.waited = {e: {} for e in ENGS}
        self.pend = {e: ([], []) for e in ENGS}
        self.dma_next = 0
        self.stage_sw = []

    def new_dma_sem(self, sw=False):
        if sw:
            s = self.sw_sems[self.sw_next]
            self.sw_next += 1
            self.stage_sw.append(s)
            return s
        s = self.dma_sems[self.dma_next]
        self.dma_next += 1
        return s

    def _eng(self, e):
        nc = self.nc
        return {"pe": nc.tensor, "act": nc.scalar, "dve": nc.vector, "pool": nc.gpsimd, "sp": nc.sync}[e]

    def _wait(self, e, k, v):
        if self.waited[e].get(k, 0) >= v:
            return
        self.waited[e][k] = v
        h = self.sem[k]
        self.q[e].append(lambda eng, h=h, v=v: eng.wait_ge(h, v))

    def _deps(self, e, reads, writes):
        for b in reads:
            if b.w is not None:
                self._wait(e, *b.w)
        for b in writes:
            if b.w is not None:
                self._wait(e, *b.w)
            for t in b.r.values():
                self._wait(e, *t)

    def op(self, e, fn, reads=(), writes=(), inc=True, touch=()):
        self._deps(e, reads, writes)
        pr, pw = self.pend[e]
        pr.extend(reads)
        pw.extend(writes)
        pw.extend(touch)
        rec = _Rec()
        fn(rec)
        name, a, kw = rec.call
        ln = sys._getframe(1).f_lineno if DEBUG_LINES else 0

        def emit(eng, name=name, a=a, kw=kw, ln=ln):
            i = getattr(eng, name)(*a, **kw)
            if DEBUG_LINES:
                LINEMAP[i.ins.name] = ln
            return i
        if not inc:
            self.q[e].append(emit)
            return None
        self.cnt[e] += 1
        tok = (e, self.cnt[e])
        h = self.sem[e]
        self.q[e].append(lambda eng, emit=emit, h=h: emit(eng).then_inc(h, 1))
        for b in pr:
            b.r[e] = tok
        for b in pw:
            b.w = tok
            b.r = {}
        self.pend[e] = ([], [])
        return tok

    def dma(self, e, out_ap, in_ap, sem, reads=(), writes=()):
        assert not self.pend[e][0] and not self.pend[e][1]
        assert (e == "pool") == sem.key.startswith("swdma"), (e, sem.key)
        self._deps(e, reads, writes)
        sem.count += 16
        tok = (sem.key, sem.count)
        h = sem.h
        self.q[e].append(lambda eng, o=out_ap, i=in_ap, h=h: eng.dma_start(out=o, in_=i).then_inc(h, 16))
        for b in reads:
            b.r[sem.key] = tok
        for b in writes:
            b.w = tok
            b.r = {}
        return tok

    def wait_tok(self, e, tok):
        self._wait(e, *tok)

    def end_stage(self, final_toks=()):
        for e in ENGS:
            assert not self.pend[e][0] and not self.pend[e][1], e
        for e in ENGS:
            for k in ENGS:
                if k != e and self.cnt[k] > 0:
                    self._wait(e, k, self.cnt[k])
        for t in final_toks:
            self._wait("sp", *t)
        for ds in self.dma_sems[:self.dma_next] + self.stage_sw:
            if ds.count > 0:
                self._wait("sp", ds.key, ds.count)
        with self.nc.Block() as block:
            for e, reg in (("pe", block.tensor), ("act", block.scalar), ("dve", block.vector),
                           ("pool", block.gpsimd), ("sp", block.sync)):
                lst = self.q[e]

                def body(eng, lst=lst):
                    for f in lst:
                        f(eng)
                reg(body)


class Ctx:
    def __init__(self, P, es):
        self.P = P
        self.nc = P.nc
        self.es = es

    def sb(self, name, shape, dt):
        t = self.es.enter_context(self.nc.sbuf_tensor(f"{name}_{self.P.stage_id}", list(shape), dt))
        return t

    def ps(self, name, shape, dt=F32):
        t = self.es.enter_context(self.nc.psum_tensor(f"{name}_{self.P.stage_id}", list(shape), dt))
        return t


def make_ident(P, C, ident_f32, sem):
    idf = Buf(C.sb("idf", [128, 128], F32))
    idb = Buf(C.sb("idb", [128, 128], BF16))
    P.dma("sp", idf.ap[:, :], ident_f32, P.new_dma_sem(), writes=[idf])
    P.op("dve", lambda v: v.tensor_copy(idb.ap[:, :], idf.ap[:, :]), reads=[idf], writes=[idb])
    return idf, idb


class XPrep:
    def __init__(self, P, C, idb, banks, cast_eng="act"):
        self.P = P
        self.cast_eng = cast_eng
        self.xst = [Buf(C.sb(f"xst{i}", [128, D], F32)) for i in range(2)]
        self.xbf = [Buf(C.sb(f"xbf{i}", [128, D], BF16)) for i in range(2)]
        self.sem = [P.new_dma_sem() for _ in range(2)]
        self.idb = idb
        self.banks = banks
        self.n = 0

    def __call__(self, row_ap, dst):
        self.part2(self.part1(row_ap), dst)

    def part1(self, row_ap):
        P = self.P
        k = self.n % 2
        bank = self.banks[self.n % len(self.banks)]
        self.n += 1
        xst, xbf = self.xst[k], self.xbf[k]
        P.dma("sp", xst.ap[:, :], row_ap, self.sem[k], writes=[xst])
        if self.cast_eng == "act":
            P.op("act", lambda a: a.copy(xbf.ap[:, :], xst.ap[:, :]), reads=[xst], writes=[xbf])
        else:
            P.op("pool", lambda g: g.tensor_copy(xbf.ap[:, :], xst.ap[:, :]), reads=[xst], writes=[xbf])
        return (k, bank)

    def part2(self, h, dst):
        P = self.P
        k, bank = h
        xbf, idb = self.xbf[k], self.idb
        pv = bank.ap.bitcast(BF16)
        for c in range(8):
            P.op("pe", lambda pe, c=c: pe.transpose(
                pv[:, c * 128:(c + 1) * 128], xbf.ap[:, c * 128:(c + 1) * 128], idb.ap[:, :]),
                reads=[xbf, idb], writes=[bank] if c == 0 else [], inc=(c == 7))
        P.op("dve", lambda v: v.tensor_copy(dst.ap, pv[:, :].rearrange("p (c n) -> p c n", c=8)),
             reads=[bank], writes=[dst])


class LNEpi:
    def __init__(self, P, C, gain, bias, sem_const, depth=2):
        self.P = P
        self.depth = depth
        self.xres = [Buf(C.sb(f"xres{i}", [128, D], F32)) for i in range(depth)]
        self.zb = [Buf(C.sb(f"z{i}", [128, D], F32)) for i in range(depth)]
        self.ob = [Buf(C.sb(f"ob{i}", [128, D], F32)) for i in range(depth)]
        self.st = [Buf(C.sb(f"st{i}", [128, 16], F32)) for i in range(depth)]
        self.gbc = Buf(C.sb("gbc", [128, D], F32))
        self.bbc = Buf(C.sb("bbc", [128, D], F32))
        self.s_x = [P.new_dma_sem() for _ in range(depth)]
        self.s_o = [P.new_dma_sem() for _ in range(depth)]
        P.dma("sp", self.gbc.ap[:, :], gain.partition_broadcast(128), P.new_dma_sem(), writes=[self.gbc])
        P.dma("sp", self.bbc.ap[:, :], bias.partition_broadcast(128), P.new_dma_sem(), writes=[self.bbc])

    def load_x(self, k, row_ap):
        self.P.dma("sp", self.xres[k].ap[:, :], row_ap, self.s_x[k], writes=[self.xres[k]])

    def __call__(self, k, banks, dst_ap, yscale):
        self.e1(k, banks, yscale)
        self.e2(k, dst_ap)

    def e1(self, k, banks, yscale):
        self.e1a(k)
        self.e1b(k, banks, yscale)

    def e1a(self, k):
        xres = self.xres[k]
        self.P.op("act", lambda a: a.mul(xres.ap[:, :], xres.ap[:, :], ALPHA), reads=[xres], writes=[xres])

    def e1b(self, k, banks, yscale):
        P = self.P
        xres, z, o, sb_ = self.xres[k], self.zb[k], self.ob[k], self.st[k]
        for n in range(2):
            P.op("dve", lambda v, n=n: v.scalar_tensor_tensor(
                z.ap[:, n * 512:(n + 1) * 512], banks[n].ap[:, :], float(yscale),
                xres.ap[:, n * 512:(n + 1) * 512], ALU.mult, ALU.add),
                reads=[banks[n], xres], writes=[z] if n == 0 else [], inc=(n == 1))
        for n in range(2):
            P.op("dve", lambda v, n=n: v.bn_stats(sb_.ap[:, n * 6:(n + 1) * 6], z.ap[:, n * 512:(n + 1) * 512]),
                 reads=[z], writes=[sb_] if n == 0 else [], inc=(n == 1))
        P.op("dve", lambda v: v.bn_aggr(sb_.ap[:, 12:14], sb_.ap[:, 0:12]), reads=[sb_], writes=[sb_])
        P.op("act", lambda a: a.activation(sb_.ap[:, 14:15], sb_.ap[:, 13:14], AF.Sqrt, bias=LN_EPS, scale=1.0),
             reads=[sb_], writes=[sb_])

    def e2(self, k, dst_ap):
        P = self.P
        xres, z, o, sb_ = self.xres[k], self.zb[k], self.ob[k], self.st[k]
        gbc, bbc = self.gbc, self.bbc
        P.op("dve", lambda v: v.reciprocal(sb_.ap[:, 14:15], sb_.ap[:, 14:15]), reads=[sb_], writes=[sb_])
        P.op("dve", lambda v: v.scalar_tensor_tensor(
            sb_.ap[:, 15:16], sb_.ap[:, 12:13], -1.0, sb_.ap[:, 14:15], ALU.mult, ALU.mult),
            reads=[sb_], writes=[sb_])
        P.op("act", lambda a: a.activation(z.ap[:, :], z.ap[:, :], AF.Identity, bias=sb_.ap[:, 15:16],
                                           scale=sb_.ap[:, 14:15]), reads=[z, sb_], writes=[z])
        P.op("pool", lambda g: g.tensor_tensor(o.ap[:, :], z.ap[:, :], gbc.ap[:, :], ALU.mult),
             reads=[z, gbc], writes=[o])
        P.op("pool", lambda g: g.tensor_tensor(o.ap[:, :], o.ap[:, :], bbc.ap[:, :], ALU.add),
             reads=[o, bbc], writes=[o])
        P.dma("sp", dst_ap, o.ap[:, :], self.s_o[k], reads=[o])


def stage_ffn(P, x_src, x_dst, w_gu, w_down, gain, bias, ident_f32, ntiles=NT):
    P.begin_stage()
    NWG = 3
    with ExitStack() as es:
        C = Ctx(P, es)
        wd = C.sb("wd", [128, NFC, D], BF16)
        wg = [Buf(C.sb(f"wg{i}", [128, 8, 512], BF16)) for i in range(NWG)]
        xT_t = [C.sb(f"xT{i}", [128, 8, TT], BF16) for i in range(2)]
        gT_t = C.sb("gT", [128, NFC, TT], BF16)
        sg = [Buf(C.sb(f"sg{i}", [128, 512], F32)) for i in range(2)]
        pg = [Buf(C.ps(f"pg{i}", [128, 512], F32)) for i in range(4)]
        pd = [Buf(C.ps(f"pd{i}", [128, 512], F32)) for i in range(4)]
        wd_b = [Buf(wd[:, j, :]) for j in range(NFC)]
        xT = [[Buf(xT_t[i][:, :, s * 128:(s + 1) * 128]) for s in range(8)] for i in range(2)]
        gT = [[Buf(gT_t[:, j, h * 512:(h + 1) * 512]) for h in range(2)] for j in range(NFC)]

        s_const = P.new_dma_sem()
        s_wd = P.new_dma_sem(sw=True)
        s_wg = [P.new_dma_sem(sw=True) for _ in range(NWG)]
        idf, idb = make_ident(P, C, ident_f32, s_const)
        xprep = XPrep(P, C, idb, pd)
        epi = LNEpi(P, C, gain, bias, s_const)

        wguv = w_gu.rearrange("(c p) n -> p c n", p=128)

        def load_wg(step):
            j2 = step % (NFC // 2)
            slot = step % NWG
            b = wg[slot]
            P.dma("pool", b.ap[:, :, 0:256], wguv[:, :, 256 * j2:256 * j2 + 256], s_wg[slot], writes=[b])
            P.dma("pool", b.ap[:, :, 256:512], wguv[:, :, DFF + 256 * j2:DFF + 256 * j2 + 256], s_wg[slot])
            b.w = (s_wg[slot].key, s_wg[slot].count)

        nsteps = ntiles * (NFC // 2)
        xsv = x_src.rearrange("(t s p) d -> t s p d", s=8, p=128)
        xdv = x_dst.rearrange("(t s p) d -> t s p d", s=8, p=128)

        for stp in range(min(NWG, nsteps)):
            load_wg(stp)
        wdv = w_down.rearrange("(c p) n -> p c n", p=128)
        for j in range(NFC):
            P.dma("pool", wd[:, j, :], wdv[:, j, :], s_wd, writes=[wd_b[j]])
        for j in range(NFC):
            wd_b[j].w = (s_wd.key, s_wd.count)
        h0 = xprep.part1(xsv[0, 0])
        for s in range(8):
            h1 = xprep.part1(xsv[0, s + 1]) if s + 1 < 8 else None
            xprep.part2(h0, xT[0][s])
            h0 = h1

        gstep = 0
        grp = 0
        xh = {}
        pending_e2 = None
        for t in range(ntiles):
            xTt = xT[t % 2]
            for j2 in range(NFC // 2):
                if j2 == 1 and pending_e2 is not None:
                    epi.e2(*pending_e2)
                    pending_e2 = None
                slot = gstep % NWG
                wb = wg[slot]
                for jj in range(2):
                    j = 2 * j2 + jj
                    for h in range(2):
                        bg = pg[2 * (grp % 2)]
                        bu = pg[2 * (grp % 2) + 1]
                        sgb = sg[grp % 2]
                        grp += 1
                        rd = [wb] + [xTt[4 * h + q] for q in range(4)]
                        for (bank, off) in ((bg, 0), (bu, 256)):
                            for c in range(8):
                                P.op("pe", lambda pe, bank=bank, c=c, off=off, jj=jj, h=h, wb=wb, t=t: pe.matmul(
                                    bank.ap[:, :], wb.ap[:, c, off + jj * 128:off + jj * 128 + 128],
                                    xT_t[t % 2][:, c, h * 512:(h + 1) * 512], start=(c == 0), stop=(c == 7)),
                                    reads=rd if c == 0 else [], writes=[bank] if c == 0 else [], inc=(c == 7))
                        P.op("act", lambda a, sgb=sgb, bg=bg: a.activation(sgb.ap[:, :], bg.ap[:, :], AF.Silu),
                             reads=[bg], writes=[sgb])
                        dst = gT[j][h]
                        P.op("dve", lambda v, dst=dst, sgb=sgb, bu=bu: v.tensor_tensor(
                            dst.ap, sgb.ap[:, :], bu.ap[:, :], ALU.mult), reads=[sgb, bu], writes=[dst])
                gstep += 1
                if gstep + NWG - 1 < nsteps:
                    load_wg(gstep + NWG - 1)
                if t + 1 < ntiles:
                    if 2 <= j2 <= 9:
                        xprep.part2(xh.pop(j2 - 2), xT[(t + 1) % 2][j2 - 2])
                    if 1 <= j2 <= 8:
                        xh[j2 - 1] = xprep.part1(xsv[t + 1, j2 - 1])
            epi.load_x(0, xsv[t, 0])
            for s in range(8):
                k = s % 2
                if s + 1 < 8:
                    epi.load_x((s + 1) % 2, xsv[t, s + 1])
                banks = (pd[2 * k], pd[2 * k + 1])
                for n in range(2):
                    bank = banks[n]
                    for j in range(NFC):
                        P.op("pe", lambda pe, bank=bank, j=j, s=s, n=n: pe.matmul(
                            bank.ap[:, :], gT_t[:, j, s * 128:(s + 1) * 128], wd[:, j, n * 512:(n + 1) * 512],
                            start=(j == 0), stop=(j == NFC - 1)),
                            reads=[gT[j][s // 4], wd_b[j]], writes=[bank] if j == 0 else [], inc=(j == NFC - 1))
                epi.e1(k, banks, 0.5)
                if s >= 1:
                    epi.e2((s - 1) % 2, xdv[t, s - 1])
            pending_e2 = (1, xdv[t, 7])
        epi.e2(*pending_e2)
        P.end_stage()


def stage_mix(P, layer, x_src, w_in, mem, w_kv, og, consts, fbias=None, cq=None, ck=None):
    ident_f32, rope_tabs, masks_f32 = consts
    P.begin_stage()
    dil = (1, 4, 16) if layer == 0 else (1, 1, 1)
    qmem_off = 3 * MIXW if layer == 0 else 3 * MIXW + NMIX
    with ExitStack() as es:
        C = Ctx(P, es)
        pb = [Buf(C.ps(f"pb{i}", [128, 512], F32)) for i in range(8)]
        s_const = P.new_dma_sem()
        idf, idb = make_ident(P, C, ident_f32, s_const)
        xprep = XPrep(P, C, idb, [pb[6]])
        xT_t = [C.sb(f"xT{i}", [128, 8, TT], BF16) for i in range(2)]
        xT = [[Buf(xT_t[i][:, :, s * 128:(s + 1) * 128]) for s in range(8)] for i in range(2)]
        wgt = [Buf(C.sb("wgt0", [128, 8, 768], BF16))] * 2
        s_wgt = [P.new_dma_sem(sw=True)] * 2
        qkbf = [Buf(C.sb(f"qkbf{i}", [128, 512], BF16)) for i in range(2)]
        qk_all = C.sb("qkall", [70, 8, S], BF16)
        qkt = [Buf(qk_all[0:64, :, t * TT:(t + 1) * TT]) for t in range(NT)]
        v_sb = C.sb("vsb", [128, 32, 4, 65], BF16)
        vt = [Buf(v_sb[:, 8 * t:8 * t + 8, :, :]) for t in range(NT)]
        vones = Buf(v_sb[:, :, :, 64:65])
        pT = [Buf(C.sb(f"pT{i}", [128, 512], BF16)) for i in range(3)]
        ost = [Buf(C.sb(f"ost{i}", [128, 1040], F32)) for i in range(2)]
        s_ost = [P.new_dma_sem() for _ in range(2)]
        mkf = Buf(C.sb("mkf", [128, 384], F32))
        mk = Buf(C.sb("mk", [128, 3, 128], BF16))
        memT_t = C.sb("memT", [128, 8, MEML], BF16)
        memT = [Buf(memT_t[:, :, i * 128:(i + 1) * 128]) for i in range(2)]
        wkv = Buf(C.sb("wkv", [128, 8, 512], BF16))
        kmT = Buf(C.sb("kmT", [64, 4, MEML], BF16))
        vm_sb = Buf(C.sb("vmsb", [128, 2, 4, 65], BF16))
        s_wkv = P.new_dma_sem(sw=True)
        if layer == 0:
            tabg = Buf(C.sb("tabg", [128, 32, 128], F32))
            s_tab = P.new_dma_sem()
            rtmp = [[Buf(C.sb(f"rt{i}_{q}", [128, 64], F32)) for q in range(4)] for i in range(2)]
            a32b = [Buf(C.sb(f"a32_{i}", [128, 512], F32)) for i in range(2)]
        else:
            wf = Buf(C.sb("wf", [128, 8, NMIX], BF16))
            s_wf = P.new_dma_sem(sw=True)
            fst = [Buf(C.sb(f"fst{i}", [128, NMIX], F32)) for i in range(2)]
            fT = Buf(C.sb("fT", [NMIX, S], F32))
            augb = Buf(qk_all[64:70, :, :])
            s_aug = P.new_dma_sem()
            cq_b = Buf(cq)
            ck_b = Buf(ck)

        P.dma("sp", mkf.ap[:, :], masks_f32, s_const, writes=[mkf])
        P.op("dve", lambda v: v.tensor_copy(mk.ap[:, :, :], mkf.ap[:, :].rearrange("p (a b) -> p a b", a=3)),
             reads=[mkf], writes=[mk])
        M_DIAG, M_PREV, M_ALL = 0, 1, 2
        if layer == 0:
            m01 = Buf(C.sb("m01", [128, 2, 512], BF16))
            mv_ = mkf.ap[:, :].rearrange("p (a b) -> p a b", a=3)
            for hh in range(2):
                for (sel, src) in ((0, M_ALL), (1, M_PREV)):
                    P.op("dve", lambda v, hh=hh, sel=sel, src=src: v.tensor_scalar(
                        m01.ap[:, sel, hh * 256:hh * 256 + 128], mv_[:, src, :], 0.0, None, ALU.is_equal),
                        reads=[mkf], writes=[m01])
                    P.op("dve", lambda v, hh=hh, sel=sel: v.tensor_scalar(
                        m01.ap[:, sel, hh * 256 + 128:hh * 256 + 256], mv_[:, M_DIAG, :], 0.0, None,
                        ALU.is_equal), reads=[mkf], writes=[m01])
        P.op("pool", lambda g: g.memset(v_sb[:, :, :, :].rearrange("p u j e -> p (u j) e")[:, :, 64:65], 1.0),
             writes=[vones])
        P.op("pool", lambda g: g.memset(vm_sb.ap[:, :, :, :].rearrange("p u j e -> p (u j) e")[:, :, 64:65], 1.0),
             writes=[vm_sb])

        w_inv = w_in.rearrange("(c p) n -> p c n", p=128)

        def load_wgt(g):
            b = wgt[g % 2]
            if g < 3:
                for i in range(3):
                    P.dma("pool", b.ap[:, :, i * 256:(i + 1) * 256],
                          w_inv[:, :, i * MIXW + g * 256:i * MIXW + (g + 1) * 256], s_wgt[g % 2],
                          writes=[b] if i == 0 else [])
            else:
                P.dma("pool", b.ap[:, :, 0:256], w_inv[:, :, qmem_off:qmem_off + 256], s_wgt[g % 2], writes=[b])
            b.w = (s_wgt[g % 2].key, s_wgt[g % 2].count)

        load_wgt(0)
        if layer == 1:
            P.dma("pool", wf.ap[:, :, :], w_inv[:, :, 3 * MIXW:3 * MIXW + NMIX], s_wf, writes=[wf])
        P.dma("pool", wkv.ap[:, :, :], w_kv.rearrange("(c p) n -> p c n", p=128), s_wkv, writes=[wkv])

        def proj_mm(g, u, xTbuf_t, xTb, s, nslots, with_v, with_f, rope_ap):
            wb = wgt[g % 2]
            A = pb[u % 3]
            Bk = pb[3 + u % 3]
            ncol = nslots * 64
            for c in range(8):
                P.op("pe", lambda pe, c=c: pe.matmul(
                    A.ap[:, 0:ncol], xTbuf_t[:, c, s * 128:(s + 1) * 128], wb.ap[:, c, 0:ncol],
                    start=(c == 0), stop=(c == 7)),
                    reads=[xTb, wb] if c == 0 else [], writes=[A] if c == 0 else [], inc=(c == 7))
            if with_v:
                for c in range(8):
                    P.op("pe", lambda pe, c=c: pe.matmul(
                        Bk.ap[:, 0:256], xTbuf_t[:, c, s * 128:(s + 1) * 128], wb.ap[:, c, 512:768],
                        start=(c == 0), stop=(c == 7), skip_group_check=True),
                        reads=[xTb, wb] if c == 0 else [], writes=[Bk] if c == 0 else [],
                        inc=(c == 7 and not with_f))
            if with_f:
                for c in range(8):
                    P.op("pe", lambda pe, c=c: pe.matmul(
                        Bk.ap[:, 256:256 + NMIX], xTbuf_t[:, c, s * 128:(s + 1) * 128], wf.ap[:, c, :],
                        start=False, stop=(c == 7), skip_group_check=True), reads=[wf] if c == 0 else [], inc=(c == 7))

        def proj_fin(g, u, xTbuf_t, xTb, s, nslots, with_v, with_f, rope_ap):
            wb = wgt[g % 2]
            A = pb[u % 3]
            Bk = pb[3 + u % 3]
            ncol = nslots * 64
            qb = qkbf[u % 2]
            if rope_ap is not None:
                tb = tabg
                a32 = a32b[u % 2]
                P.op("act", lambda a: a.copy(a32.ap[:, :], A.ap[:, :]), reads=[A], writes=[a32])
                Av = a32.ap[:, :].rearrange("p (j d) -> p j d", j=8)
                qv = qb.ap[:, :].rearrange("p (j d) -> p j d", j=8)
                cosv = tb.ap[:, u, 0:64].rearrange("p (j d) -> p j d", j=8)
                sinv = tb.ap[:, u, 64:128].rearrange("p (j d) -> p j d", j=8)
                rt = rtmp[u % 2]
                rv = [r_.ap[:, :].rearrange("p (j d) -> p j d", j=8) for r_ in rt]
                P.op("pool", lambda g_: g_.tensor_copy(qv[:, :, 16:64], Av[:, :, 16:64]), reads=[a32], writes=[qb])
                P.op("dve", lambda v: v.tensor_tensor(rv[0], Av[:, :, 0:8], cosv, ALU.mult),
                     reads=[a32, tb], writes=[rt[0]], inc=False)
                P.op("dve", lambda v: v.tensor_tensor(rv[1], Av[:, :, 8:16], sinv, ALU.mult),
                     reads=[a32, tb], writes=[rt[1]], inc=False)
                P.op("dve", lambda v: v.tensor_tensor(rv[2], Av[:, :, 8:16], cosv, ALU.mult),
                     reads=[a32, tb], writes=[rt[2]], inc=False)
                P.op("dve", lambda v: v.tensor_tensor(rv[3], Av[:, :, 0:8], sinv, ALU.mult),
                     reads=[a32, tb], writes=[rt[3]])
                P.op("pool", lambda g_: g_.tensor_tensor(qv[:, :, 0:8], rv[0], rv[1], ALU.subtract),
                     reads=[rt[0], rt[1]], writes=[qb], inc=False)
                P.op("pool", lambda g_: g_.tensor_tensor(qv[:, :, 8:16], rv[2], rv[3], ALU.add),
                     reads=[rt[2], rt[3]], writes=[qb])
            else:
                P.op("act", lambda a: a.copy(qb.ap[:, 0:ncol], A.ap[:, 0:ncol]), reads=[A], writes=[qb])
            if with_v:
                P.op("act", lambda a: a.copy(v_sb[:, u, :, 0:64], Bk.ap[:, 0:256].rearrange("p (j d) -> p j d", j=4)),
                     reads=[Bk], writes=[vt[u // 8]])
            if with_f:
                fs = fst[u % 2]
                P.op("act", lambda a: a.copy(fs.ap[:, :], Bk.ap[:, 256:256 + NMIX]), reads=[Bk], writes=[fs])
                P.op("pe", lambda pe: pe.transpose(Bk.ap[0:NMIX, 384:512], fs.ap[:, :], idf.ap[:, :]),
                     reads=[fs, idf], writes=[Bk])
                P.op("dve", lambda v: v.tensor_copy(fT.ap[:, u * 128:(u + 1) * 128], Bk.ap[0:NMIX, 384:512]),
                     reads=[Bk], writes=[fT])
            tq = pb[7]
            tqv = tq.ap.bitcast(BF16)
            for sl in range(nslots):
                P.op("pe", lambda pe, sl=sl: pe.transpose(
                    tqv[0:64, sl * 128:(sl + 1) * 128], qb.ap[:, sl * 64:(sl + 1) * 64], idb.ap[:, :]),
                    reads=[qb, idb], writes=[tq] if sl == 0 else [], inc=(sl == nslots - 1))
            P.op("dve", lambda v: v.tensor_copy(
                qk_all[0:64, 0:nslots, u * 128:(u + 1) * 128],
                tqv[0:64, 0:nslots * 128].rearrange("p (j n) -> p j n", j=nslots)),
                reads=[tq], writes=[qkt[u // 8]])

        def proj_group(g):
            r = dil[g] if g < 3 else 1
            L = S // r
            nslots = 8 if g < 3 else 4
            xr = x_src.rearrange("(n r) d -> r n d", r=r)
            if layer == 0 and g < 3:
                P.dma("sp", tabg.ap[:, :, :], rope_tabs[g].rearrange("(u p) f -> p u f", p=128), s_tab, writes=[tabg])

            def rows(u):
                m0 = u * 128
                return xr[m0 // L, (m0 % L):(m0 % L) + 128, :]

            if g == 0:
                for s in range(8):
                    xprep(rows(s), xT[0][s])
            rope_ap = True if (layer == 0 and g < 3) else None
            xh = {}

            def args(u):
                t, s = u // 8, u % 8
                return (g, u, xT_t[t % 2], xT[t % 2][s], s, nslots, g < 3, (layer == 1 and g == 0), rope_ap)

            proj_mm(*args(0))
            proj_mm(*args(1))
            for u in range(32):
                if u + 7 in xh:
                    v = u + 7
                    xprep.part2(xh.pop(v), xT[(v // 8) % 2][v % 8])
                if u + 8 < 32:
                    xh[u + 8] = xprep.part1(rows(u + 8))
                if u + 2 < 32:
                    proj_mm(*args(u + 2))
                proj_fin(*args(u))
            if g + 1 < 4:
                load_wgt(g + 1)
                r2 = dil[g + 1] if g + 1 < 3 else 1
                xr2 = x_src.rearrange("(n r) d -> r n d", r=r2)
                L2 = S // r2
                for s in range(8):
                    m0 = s * 128
                    xprep(xr2[m0 // L2, (m0 % L2):(m0 % L2) + 128, :], xT[0][s])

        def forget_prep():
            CH = 512
            nb_ = Buf(C.sb("negb", [NMIX, 1], F32))
            ones = Buf(C.sb("ones12", [NMIX, CH], F32))
            onesb = Buf(C.sb("ones12b", [NMIX, CH], BF16))
            e1s = [Buf(C.sb(f"e1_{i}", [NMIX, CH], F32)) for i in range(2)]
            cc = [Buf(C.sb(f"cc{i}", [NMIX, CH], F32)) for i in range(2)]
            c8s = [Buf(C.sb(f"c8_{i}", [NMIX, CH], F32)) for i in range(2)]
            pc = [Buf(C.sb(f"pc{i}", [NMIX, CH], BF16)) for i in range(3)]
            npc = [Buf(C.sb(f"npc{i}", [NMIX, CH], BF16)) for i in range(3)]
            s_c = P.new_dma_sem()
            s_cs = P.new_dma_sem()
            P.dma("sp", nb_.ap[:, :], fbias.rearrange("(h o) -> h o", o=1), s_c, writes=[nb_])
            P.op("act", lambda a: a.mul(nb_.ap[:, :], nb_.ap[:, :], -1.0), reads=[nb_], writes=[nb_])
            P.op("pool", lambda g_: g_.memset(ones.ap[:, :], 1.0), writes=[ones])
            P.op("pool", lambda g_: g_.memset(onesb.ap[:, :], 1.0), writes=[onesb])
            for ci in range(S // CH):
                sl = slice(ci * CH, (ci + 1) * CH)
                e1 = e1s[ci % 2]
                c8 = c8s[ci % 2]
                P.op("act", lambda a, sl=sl: a.activation(e1.ap[:, :], fT.ap[:, sl], AF.Exp, bias=nb_.ap[:, 0:1],
                                                          scale=-1.0), reads=[fT, nb_], writes=[e1])
                P.op("act", lambda a: a.activation(e1.ap[:, :], e1.ap[:, :], AF.Ln, bias=1.0, scale=1.0),
                     reads=[e1], writes=[e1])
                cur, prev = cc[ci % 2], cc[(ci + 1) % 2]
                init = 0.0 if ci == 0 else prev.ap[:, CH - 1:CH]
                P.op("dve", lambda v, cur=cur, init=init: v.tensor_tensor_scan(
                    cur.ap[:, :], ones.ap[:, :], e1.ap[:, :], init, ALU.mult, ALU.subtract),
                    reads=[ones, e1, prev], writes=[cur])
                P.op("act", lambda a, cur=cur: a.mul(c8.ap[:, :], cur.ap[:, :], 1.0 / SCALE), reads=[cur], writes=[c8])
                for i in range(3):
                    P.op("dve", lambda v, i=i: v.tensor_copy(pc[i].ap[:, :], c8.ap[:, :]), reads=[c8], writes=[pc[i]])
                    if i < 2:
                        P.op("dve", lambda v, i=i: v.tensor_tensor(c8.ap[:, :], c8.ap[:, :], pc[i].ap[:, :],
                                                                    ALU.subtract), reads=[c8, pc[i]], writes=[c8])
                for i in range(3):
                    P.op("act", lambda a, i=i: a.mul(npc[i].ap[:, :], pc[i].ap[:, :], -1.0),
                         reads=[pc[i]], writes=[npc[i]])
                for i in range(3):
                    P.dma("sp", cq[:, i, sl], pc[i].ap[:, :], s_cs, reads=[pc[i]], writes=[cq_b] if i == 0 else [])
                    P.dma("sp", ck[:, 3 + i, sl], npc[i].ap[:, :], s_cs, reads=[npc[i]], writes=[ck_b] if i == 0 else [])
                    P.dma("sp", cq[:, 3 + i, sl], onesb.ap[:, :], s_cs, reads=[onesb])
                    P.dma("sp", ck[:, i, sl], onesb.ap[:, :], s_cs, reads=[onesb])
                cq_b.w = (s_cs.key, s_cs.count)
                ck_b.w = (s_cs.key, s_cs.count)
                for b_ in pc + npc + [onesb]:
                    b_.r[s_cs.key] = (s_cs.key, s_cs.count)

        def load_aug(g):
            for j in range(4):
                P.dma("sp", qk_all[64:70, j, :], cq[4 * g + j], s_aug, reads=[cq_b], writes=[augb] if j == 0 else [])
                P.dma("sp", qk_all[64:70, 4 + j, :], ck[4 * g + j], s_aug, reads=[ck_b])
            augb.w = (s_aug.key, s_aug.count)

        cnt = {"s": 0, "o": 0, "st": 0}

        def flush_o(ob, ncol, dst_ap, view=None):
            k = cnt["st"] % 2
            cnt["st"] += 1
            stg = ost[k]
            P.op("dve", lambda v: v.tensor_copy(stg.ap[:, 0:ncol], ob.ap[:, 0:ncol]), reads=[ob], writes=[stg])
            return stg, k

        def attn_banded(g):
            r = dil[g]
            nb = 32 // r
            ogv = og[g].rearrange("(n r) f -> r n f", r=r)
            items = [(B, hp) for B in range(32) for hp in range(2)]
            st_ = {}

            def emit_score(idx):
                B, hp = items[idx]
                b = B % nb
                sb_ = pb[idx % 3]
                pt = pT[idx % 3]
                tiles_q = [qkt[B // 8]] + ([qkt[(B - 1) // 8]] if b > 0 else [])
                for hh in range(2):
                    j = hp * 2 + hh
                    reg = sb_.ap[:, hh * 256:hh * 256 + 128]
                    first = (hh == 0)
                    KP = (B - 1) if b > 0 else B
                    P.op("pe", lambda pe: pe.matmul(
                        reg, qk_all[0:64, 4 + j, KP * 128:(KP + 1) * 128], qk_all[0:64, j, B * 128:(B + 1) * 128],
                        start=first, stop=True, skip_group_check=True),
                        reads=tiles_q, writes=[sb_] if first else [], inc=False)
                    reg2 = sb_.ap[:, hh * 256 + 128:hh * 256 + 256]
                    P.op("pe", lambda pe: pe.matmul(
                        reg2, qk_all[0:64, 4 + j, B * 128:(B + 1) * 128], qk_all[0:64, j, B * 128:(B + 1) * 128],
                        start=False, stop=True, skip_group_check=True), reads=tiles_q, inc=(hh == 1))
                P.op("act", lambda a: a.activation(pt.ap[:, :], sb_.ap[:, :], AF.Exp, scale=SCALE),
                     reads=[sb_], writes=[pt])
                msel = 1 if b > 0 else 0
                P.op("dve", lambda v: v.tensor_tensor(pt.ap[:, :], pt.ap[:, :], m01.ap[:, msel, :], ALU.mult),
                     reads=[pt, m01], writes=[pt])

            def emit_pv(idx):
                B, hp = items[idx]
                b = B % nb
                c = B // nb
                pt = pT[idx % 3]
                if hp == 0:
                    st_["ob"] = pb[3 + cnt["o"] % 2]
                    cnt["o"] += 1
                ob = st_["ob"]
                tiles_v = [vt[B // 8]] + ([vt[(B - 1) // 8]] if b > 0 else [])
                for hh in range(2):
                    j = hp * 2 + hh
                    oreg = ob.ap[:, j * 65:(j + 1) * 65]
                    firstw = (hp == 0 and hh == 0)
                    if b > 0:
                        P.op("pe", lambda pe: pe.matmul(
                            oreg, pt.ap[:, hh * 256:hh * 256 + 128], v_sb[:, B - 1, j, :], start=firstw, stop=False,
                            skip_group_check=True),
                            reads=[pt, vones] + tiles_v, writes=[ob] if firstw else [], inc=False)
                    P.op("pe", lambda pe: pe.matmul(
                        oreg, pt.ap[:, hh * 256 + 128:hh * 256 + 256], v_sb[:, B, j, :],
                        start=(b == 0 and firstw), stop=True, skip_group_check=True),
                        reads=[pt, vones] + tiles_v, writes=[ob] if (firstw and b == 0) else [],
                        touch=[ob], inc=(hh == 1))
                if hp == 1:
                    stg, k = flush_o(ob, 260, None)
                    P.dma("sp", ogv[c, b * 128:(b + 1) * 128, :], stg.ap[:, 0:260], s_ost[k], reads=[stg])

            LA = 2
            for idx in range(len(items) + LA):
                if idx < len(items):
                    emit_score(idx)
                if idx - LA >= 0:
                    emit_pv(idx - LA)

        def attn_qtiles(g, causal, K, kT_fn, v_fn, kdeps_fn):
            ogv = og[g].rearrange("(t i p) (j e) -> t p i j e", i=4, p=128, j=4)
            items = []
            for T in range(8):
                nkb = (4 * T + 4) if causal else 2
                for j in range(4):
                    for KB in range(nkb):
                        items.append((T, j, KB, nkb))
            st_ = {}

            def emit_score(idx):
                T, j, KB, nkb = items[idx]
                sb_ = pb[idx % 3]
                pt = pT[idx % 3]
                jd = KB - 4 * T if causal else -1
                kdeps = kdeps_fn(KB)
                qdeps = [qkt[T // 2]] + ([augb] if (layer == 1 and causal) else [])
                if jd < 0:
                    c0 = 0
                    P.op("pe", lambda pe: pe.matmul(
                        sb_.ap[:, 0:512], kT_fn(j, KB), qk_all[0:K, j, T * 512:(T + 1) * 512],
                        start=True, stop=True), reads=kdeps + qdeps, writes=[sb_])
                else:
                    c0 = jd * 128
                    reg = sb_.ap[:, c0:c0 + 128]
                    P.op("pe", lambda pe: pe.matmul(reg, idb.ap[:, :], mk.ap[:, M_DIAG, :],
                                                     start=True, stop=False, skip_group_check=True),
                         reads=[idb, mk], writes=[sb_], inc=False)
                    P.op("pe", lambda pe: pe.matmul(
                        reg, kT_fn(j, KB), qk_all[0:K, j, T * 512 + c0:T * 512 + c0 + 128],
                        start=False, stop=True, skip_group_check=True), reads=kdeps + qdeps, inc=(jd == 3))
                    if jd < 3:
                        P.op("pe", lambda pe: pe.matmul(
                            sb_.ap[:, c0 + 128:512], kT_fn(j, KB),
                            qk_all[0:K, j, T * 512 + c0 + 128:(T + 1) * 512], start=False, stop=True,
                            skip_group_check=True))
                P.op("act", lambda a: a.activation(
                    pt.ap[:, c0:512], sb_.ap[:, c0:512], AF.Exp, scale=SCALE), reads=[sb_], writes=[pt])

            def emit_pv(idx):
                T, j, KB, nkb = items[idx]
                pt = pT[idx % 3]
                jd = KB - 4 * T if causal else -1
                kdeps = kdeps_fn(KB)
                if KB == 0:
                    st_["ob"] = pb[3 + cnt["o"] % 2]
                    cnt["o"] += 1
                    if j == 0:
                        st_["k"] = cnt["st"] % 2
                        cnt["st"] += 1
                ob = st_["ob"]
                stg = ost[st_["k"]]
                stv = stg.ap[:, :].rearrange("p (i j e) -> p i j e", i=4, j=4)
                i0_ = max(jd, 0)
                for i in range(i0_, 4):
                    last_kb = (4 * T + i) if causal else 1
                    P.op("pe", lambda pe: pe.matmul(
                        ob.ap[:, i * 65:(i + 1) * 65], pt.ap[:, i * 128:(i + 1) * 128], v_fn(j, KB),
                        start=(KB == 0 and i == 0), stop=(KB == last_kb), skip_group_check=True),
                        reads=[pt] + kdeps, writes=[ob] if (KB == 0 and i == 0) else [], touch=[ob], inc=(i == 3))
                if KB == nkb - 1:
                    P.op("dve", lambda v: v.tensor_copy(
                        stv[:, :, j, :], ob.ap[:, 0:260].rearrange("p (i e) -> p i e", i=4)),
                        reads=[ob], writes=[stg])
                    if j == 3:
                        P.dma("sp", ogv[T], stv, s_ost[st_["k"]], reads=[stg])

            LA = 2
            for idx in range(len(items) + LA):
                if idx < len(items):
                    emit_score(idx)
                if idx - LA >= 0:
                    emit_pv(idx - LA)

        def mem_kv():
            mv = mem.rearrange("(s p) d -> s p d", p=128)
            for i in range(2):
                xprep(mv[i], memT[i])
            for hp in range(2):
                bk = pb[hp]
                for hh in range(2):
                    j = hp * 2 + hh
                    for c in range(8):
                        P.op("pe", lambda pe, c=c, j=j, hh=hh, bk=bk: pe.matmul(
                            bk.ap[0:64, hh * 256:(hh + 1) * 256], wkv.ap[:, c, j * 64:(j + 1) * 64],
                            memT_t[:, c, :], start=(c == 0 and hh == 0), stop=(c == 7), skip_group_check=True),
                            reads=[wkv] + memT if c == 0 else [], writes=[bk] if (c == 0 and hh == 0) else [],
                            inc=(c == 7 and hh == 1))
                P.op("act", lambda a, bk=bk, hp=hp: a.copy(
                    kmT.ap[:, 2 * hp:2 * hp + 2, :], bk.ap[0:64, :].rearrange("p (j n) -> p j n", j=2)),
                    reads=[bk], writes=[kmT])
            for mb in range(2):
                bk = pb[2 + mb]
                for c in range(8):
                    P.op("pe", lambda pe, c=c, mb=mb, bk=bk: pe.matmul(
                        bk.ap[:, 0:256], memT_t[:, c, mb * 128:(mb + 1) * 128], wkv.ap[:, c, 256:512],
                        start=(c == 0), stop=(c == 7)),
                        reads=[wkv] + memT if c == 0 else [], writes=[bk] if c == 0 else [], inc=(c == 7))
                P.op("act", lambda a, bk=bk, mb=mb: a.copy(
                    vm_sb.ap[:, mb, :, 0:64], bk.ap[:, 0:256].rearrange("p (j d) -> p j d", j=4)),
                    reads=[bk], writes=[vm_sb])

        upto = DEBUG_UPTO
        for g in range(4):
            if upto < 2 or (upto in (2, 3) and g > 0):
                break
            proj_group(g)
            if upto == 2:
                break
            if layer == 1 and g == 0:
                forget_prep()
            if g < 3:
                if layer == 0:
                    attn_banded(g)
                else:
                    load_aug(g)
                    attn_qtiles(g, True, 70,
                                lambda j, KB: qk_all[0:70, 4 + j, KB * 128:(KB + 1) * 128],
                                lambda j, KB: v_sb[:, KB, j, :],
                                lambda KB: [qkt[KB // 8], vt[KB // 8], vones, augb])
            else:
                mem_kv()
                attn_qtiles(g, False, 64,
                            lambda j, KB: kmT.ap[:, j, KB * 128:(KB + 1) * 128],
                            lambda j, KB: vm_sb.ap[:, KB, j, :],
                            lambda KB: [kmT, vm_sb])
        P.end_stage()


def stage_out(P, layer, x_src, x_dst, og, w_out, gain, bias, ident_f32):
    P.begin_stage()
    nch = 4 if layer == 0 else 8
    NB = 4
    NPB = 3
    with ExitStack() as es:
        C = Ctx(P, es)
        pb = [Buf(C.ps(f"pb{i}", [128, 512], F32)) for i in range(8)]
        s_const = P.new_dma_sem()
        idf, idb = make_ident(P, C, ident_f32, s_const)
        epi = LNEpi(P, C, gain, bias, s_const, depth=NB)
        wo = Buf(C.sb("wo", [128, nch, D], BF16))
        s_wo = P.new_dma_sem(sw=True)
        P.dma("pool", wo.ap[:, :, :], w_out.rearrange("(c p) n -> p c n", p=128), s_wo, writes=[wo])
        ogt = [[Buf(C.sb(f"ogt{k}_{i}", [128, 260], F32)) for i in range(4)] for k in range(NB)]
        s_og = [P.new_dma_sem() for _ in range(NB)]
        rec = [[Buf(C.sb(f"rec{k}_{i}", [128, 4], F32)) for i in range(4)] for k in range(NB)]
        cat = [Buf(C.sb(f"cat{k}", [128, nch * 128], BF16)) for k in range(NB)]
        catT = [Buf(C.sb(f"catT{k}", [128, nch, 128], BF16)) for k in range(NB)]
        xsv = x_src.rearrange("(u p) d -> u p d", p=128)
        xdv = x_dst.rearrange("(u p) d -> u p d", p=128)
        ogv = [o.rearrange("(u p) f -> u p f", p=128) for o in og]
        NU = S // 128

        def loads(u):
            k = u % NB
            epi.load_x(k, xsv[u])
            for i in range(4):
                P.dma("sp", ogt[k][i].ap[:, :], ogv[i][u], s_og[k], writes=[ogt[k][i]])
            for i in range(4):
                ogt[k][i].w = (s_og[k].key, s_og[k].count)
                ogt[k][i].r = {}

        st_banks = {}

        def stage_a(u):
            k = u % NB
            if layer == 0:
                a0, a1, a2 = ogt[k][0], ogt[k][1], ogt[k][2]
                P.op("dve", lambda g_: g_.tensor_tensor(a0.ap[:, :], a0.ap[:, :], a1.ap[:, :], ALU.add),
                     reads=[a0, a1], writes=[a0])
                P.op("dve", lambda g_: g_.tensor_tensor(a0.ap[:, :], a0.ap[:, :], a2.ap[:, :], ALU.add),
                     reads=[a0, a2], writes=[a0])
                parts = [(ogt[k][0], 0), (ogt[k][3], 256)]
            else:
                parts = [(ogt[k][i], 256 * i) for i in range(4)]
            ct = cat[k]
            for pi, (src, col) in enumerate(parts):
                rc = rec[k][pi]
                sv = src.ap[:, :].rearrange("p (j e) -> p j e", j=4)
                P.op("dve", lambda v: v.reciprocal(rc.ap[:, :], sv[:, :, 64]), reads=[src], writes=[rc])
                if layer == 0 or pi >= 2:
                    for hh in range(4):
                        P.op("act", lambda a, hh=hh: a.activation(
                            ct.ap[:, col + 64 * hh:col + 64 * hh + 64], sv[:, hh, 0:64], AF.Copy,
                            scale=rc.ap[:, hh:hh + 1]),
                            reads=[src, rc], writes=[ct] if (pi == 0 and hh == 0) else [], touch=[ct], inc=(hh == 3))
                else:
                    P.op("dve", lambda v: v.tensor_tensor(
                        ct.ap[:, col:col + 256].rearrange("p (j d) -> p j d", j=4), sv[:, :, 0:64],
                        rc.ap[:, :].unsqueeze(2).broadcast_to([128, 4, 64]), ALU.mult),
                        reads=[src, rc], writes=[ct] if pi == 0 else [], touch=[ct])
            tb = pb[6 + u % 2]
            tbv = tb.ap.bitcast(BF16)
            for ch in range(nch):
                P.op("pe", lambda pe, ch=ch: pe.transpose(
                    tbv[:, ch * 128:(ch + 1) * 128], ct.ap[:, ch * 128:(ch + 1) * 128], idb.ap[:, :]),
                    reads=[ct, idb], writes=[tb] if ch == 0 else [], inc=(ch == nch - 1))
            cT = catT[k]
            P.op("act", lambda a: a.copy(cT.ap[:, :, :], tbv[:, 0:nch * 128].rearrange("p (c n) -> p c n", c=nch)),
                 reads=[tb], writes=[cT])
            kb = u % NPB
            banks = (pb[2 * kb], pb[2 * kb + 1])
            st_banks[u] = banks
            for n in range(2):
                bank = banks[n]
                for ch in range(nch):
                    P.op("pe", lambda pe, bank=bank, ch=ch, n=n: pe.matmul(
                        bank.ap[:, :], cT.ap[:, ch, :], wo.ap[:, ch, n * 512:(n + 1) * 512],
                        start=(ch == 0), stop=(ch == nch - 1)),
                        reads=[cT, wo] if ch == 0 else [], writes=[bank] if ch == 0 else [], inc=(ch == nch - 1))

        for u in range(min(NB, NU)):
            loads(u)
        for step in range(NU + 3):
            ua, u1, u2 = step, step - 1, step - 2
            if 0 <= u2 < NU:
                epi.e2(u2 % NB, xdv[u2])
                if u2 + NB < NU:
                    loads(u2 + NB)
            if 0 <= u1 < NU:
                epi.e1a(u1 % NB)
            if ua < NU:
                stage_a(ua)
            if 0 <= u1 < NU:
                epi.e1b(u1 % NB, st_banks[u1], 1.0)
        P.end_stage()


def host_constants():
    ident = np.eye(128, dtype=np.float32)
    NEG = np.float32(-1.0e5)
    k = np.arange(128)[:, None]
    q = np.arange(128)[None, :]
    m_diag = np.where(k <= q, 0.0, NEG).astype(np.float32)
    m_prev = np.where(k >= q, 0.0, NEG).astype(np.float32)
    m_all = np.full((128, 128), NEG, np.float32)
    masks = np.concatenate([m_diag, m_prev, m_all], axis=1)
    pos = np.arange(S, dtype=np.float32)
    inv_freq = (1.0 / (np.float32(500000.0) ** (np.arange(8, dtype=np.float32) / np.float32(8)))).astype(np.float32)
    ang = (pos[:, None] * inv_freq[None, :]).astype(np.float32)
    cos = np.cos(ang).astype(np.float32)
    sin = np.sin(ang).astype(np.float32)
    tabs = np.zeros((3, S, 128), np.float32)
    for g, r in enumerate((1, 4, 16)):
        L = S // r
        m = np.arange(S)
        tok = (m // L) + r * (m % L)
        tabs[g, :, 0:64] = np.tile(cos[tok], (1, 8))
        tabs[g, :, 64:128] = np.tile(sin[tok], (1, 8))
    return ident, masks, tabs


def build_program(stages=None):
    nc = bass.Bass("TRN2", target_bir_lowering=False)

    def din(name, shape):
        return nc.dram_tensor(name, list(shape), F32, kind="ExternalInput").ap()

    x = din("x", [S, D])
    mem = din("mem", [MEML, D])
    f1gu = din("ffn1_w_gate_up", [DEPTH, D, 2 * DFF])
    f1d = din("ffn1_w_down", [DEPTH, DFF, D])
    f2gu = din("ffn2_w_gate_up", [DEPTH, D, 2 * DFF])
    f2d = din("ffn2_w_down", [DEPTH, DFF, D])
    lng = din("ln_gain", [DEPTH, 3, D])
    lnb = din("ln_bias", [DEPTH, 3, D])
    wkv = din("mem_w_kv", [DEPTH, D, 2 * MEMW])
    awin = din("a_w_in", [1, D, A_IN_W])
    awout = din("a_w_out", [1, 4 * HD + MEMW, D])
    bwin = din("b_w_in", [1, D, B_IN_W])
    bfb = din("b_forget_bias", [1, NMIX])
    bwout = din("b_w_out", [1, MIXW + MEMW, D])
    ident = din("c_ident", [128, 128])
    masks = din("c_masks", [128, 384])
    tabs = din("c_rope", [3, S, 128])
    out = nc.dram_tensor("out", [S, D], F32, kind="ExternalOutput").ap()
    xa = nc.dram_tensor("scr_xa", [S, D], F32, kind="Internal").ap()
    xb = nc.dram_tensor("scr_xb", [S, D], F32, kind="Internal").ap()
    og = [nc.dram_tensor(f"scr_og{i}", [S, 260], F32, kind="Internal").ap() for i in range(4)]
    cq = nc.dram_tensor("scr_cq", [NMIX, 6, S], BF16, kind="Internal").ap()
    ck = nc.dram_tensor("scr_ck", [NMIX, 6, S], BF16, kind="Internal").ap()
    consts = (ident, tabs, masks)
    with ExitStack() as es:
        P = Prog(nc, es)
        allst = [
            lambda: stage_ffn(P, x, xa, f1gu[0], f1d[0], lng[0, 0], lnb[0, 0], ident),
            lambda: stage_mix(P, 0, xa, awin[0], mem, wkv[0], og, consts),
            lambda: stage_out(P, 0, xa, xb, og, awout[0], lng[0, 1], lnb[0, 1], ident),
            lambda: stage_ffn(P, xb, xa, f2gu[0], f2d[0], lng[0, 2], lnb[0, 2], ident),
            lambda: stage_ffn(P, xa, xb, f1gu[1], f1d[1], lng[1, 0], lnb[1, 0], ident),
            lambda: stage_mix(P, 1, xb, bwin[0], mem, wkv[1], og, consts, fbias=bfb[0], cq=cq, ck=ck),
            lambda: stage_out(P, 1, xb, xa, og, bwout[0], lng[1, 1], lnb[1, 1], ident),
            lambda: stage_ffn(P, xa, out, f2gu[1], f2d[1], lng[1, 2], lnb[1, 2], ident),
        ]
        for i, st in enumerate(allst):
            if stages is None or i in stages:
                st()
    return nc


def kernel(x, mem, ffn1_w_gate_up, ffn1_w_down, ffn2_w_gate_up, ffn2_w_down, ln_gain, ln_bias, mem_w_kv,
           a_w_in, a_w_out, b_w_in, b_forget_bias, b_w_out):
    ncores = 8
    ident, masks, tabs = host_constants()
    f32 = lambda a: np.ascontiguousarray(np.asarray(a, dtype=np.float32))
    shared = {
        "ffn1_w_gate_up": f32(ffn1_w_gate_up), "ffn1_w_down": f32(ffn1_w_down),
        "ffn2_w_gate_up": f32(ffn2_w_gate_up), "ffn2_w_down": f32(ffn2_w_down),
        "ln_gain": f32(ln_gain), "ln_bias": f32(ln_bias), "mem_w_kv": f32(mem_w_kv),
        "a_w_in": f32(a_w_in), "a_w_out": f32(a_w_out), "b_w_in": f32(b_w_in),
        "b_forget_bias": f32(b_forget_bias), "b_w_out": f32(b_w_out),
        "c_ident": ident, "c_masks": masks, "c_rope": tabs,
    }
    xs = f32(x)
    ms = f32(mem)
    in_maps = []
    for b in range(ncores):
        d = dict(shared)
        d["x"] = xs[b]
        d["mem"] = ms[b]
        in_maps.append(d)
    nc = build_program()
    res = run_bass_kernel_spmd(nc, in_maps, core_ids=list(range(ncores)))
    return np.stack([np.asarray(r["out"], dtype=np.float32) for r in res.results], axis=0)
```
